# Optimizing a Trainium2 kernel written in Bass

```python
import math
import jax, jax.numpy as jnp
from jax import lax
import numpy as np

D_MODEL = 1024
BATCH = 32
SEQ = 256
DEPTH = 2
DEC_BATCH = 8
DEC_SEQ = 1024
PAST_LEN = 512

GRID_W = 64
D_FF = 2816
S5_WIDTH = 512
S5_GROUP = 16
S5_GROUPS = S5_WIDTH // S5_GROUP
S5_STATE = 64
CONV_WIDTH = 256
CONV_K = 31
SG_WIDTH = 256
SG_CHUNK = 128
SG_HEADS = 4
SG_HEAD_DIM = SG_WIDTH // SG_HEADS
N_BRANCH = 3
IN_COLS = S5_WIDTH + 2 * CONV_WIDTH + 2 * SG_WIDTH
N_MOD = 9
EPS = 1e-6

kernel_name = 'hybrid_s5_conv_sgmlp_diffusion_step'


def _rmsnorm(x, g):
    xf = x.astype(jnp.float32)
    y = xf * lax.rsqrt(jnp.mean(xf * xf, axis=-1, keepdims=True) + EPS)
    return (y * g.astype(jnp.float32)).astype(x.dtype)


def _layernorm(x, g, b):
    xf = x.astype(jnp.float32)
    mu = jnp.mean(xf, axis=-1, keepdims=True)
    var = jnp.mean(jnp.square(xf - mu), axis=-1, keepdims=True)
    y = (xf - mu) * lax.rsqrt(var + EPS)
    return (y * g.astype(jnp.float32) + b.astype(jnp.float32)).astype(x.dtype)


def _swiglu(h, w1, w2):
    g, u = jnp.split(h @ w1, 2, axis=-1)
    return (jax.nn.silu(g) * u) @ w2


def _cmul(ar, ai, br, bi):
    return ar * br - ai * bi, ar * bi + ai * br


def _ssm_combine(e1, e2):
    a1r, a1i, b1r, b1i = e1
    a2r, a2i, b2r, b2i = e2
    ar, ai = _cmul(a2r, a2i, a1r, a1i)
    br, bi = _cmul(a2r, a2i, b1r, b1i)
    return ar, ai, br + b2r, bi + b2i


def _s5_direction(u, a_re, a_im, log_dt, b_re, b_im, c_re, c_im, h0, reverse):
    f32 = jnp.float32
    a_re = a_re.astype(f32); a_im = a_im.astype(f32)
    b_re = b_re.astype(f32); b_im = b_im.astype(f32)
    c_re = c_re.astype(f32); c_im = c_im.astype(f32)
    dt = jnp.exp(log_dt.astype(f32))[:, None]
    mag = jnp.exp(a_re * dt)
    abar_r, abar_i = mag * jnp.cos(a_im * dt), mag * jnp.sin(a_im * dt)
    den = a_re * a_re + a_im * a_im
    nr, ni = abar_r - 1.0, abar_i
    qr = (nr * a_re + ni * a_im) / den
    qi = (ni * a_re - nr * a_im) / den
    bbar_r = qr[..., None] * b_re - qi[..., None] * b_im
    bbar_i = qr[..., None] * b_im + qi[..., None] * b_re
    bu_r = jnp.einsum('gph,blgh->blgp', bbar_r, u)
    bu_i = jnp.einsum('gph,blgh->blgp', bbar_i, u)
    first = -1 if reverse else 0
    ir, ii = _cmul(abar_r, abar_i, h0[..., 0].astype(f32), h0[..., 1].astype(f32))
    bu_r = bu_r.at[:, first].add(ir)
    bu_i = bu_i.at[:, first].add(ii)
    ar = jnp.broadcast_to(abar_r, bu_r.shape)
    ai = jnp.broadcast_to(abar_i, bu_i.shape)
    _, _, sr, si = lax.associative_scan(_ssm_combine, (ar, ai, bu_r, bu_i), reverse=reverse, axis=1)
    y = jnp.einsum('ghp,blgp->blgh', c_re, sr) - jnp.einsum('ghp,blgp->blgh', c_im, si)
    last = 0 if reverse else -1
    h_final = jnp.stack([sr[:, last], si[:, last]], axis=-1)
    return y, h_final


def _s5_branch(z, P, l, h0):
    bn, L, _ = z.shape
    zf = z.astype(jnp.float32)
    u = zf.reshape(bn, L, S5_GROUPS, S5_GROUP)
    ys, hs = [], []
    for d, rev in ((0, False), (1, True)):
        y, hT = _s5_direction(u, P['s5_a_re'][l, d], P['s5_a_im'][l, d], P['s5_log_dt'][l, d],
                              P['s5_b_re'][l, d], P['s5_b_im'][l, d],
                              P['s5_c_re'][l, d], P['s5_c_im'][l, d], h0[:, d], rev)
        ys.append(y)
        hs.append(hT)
    y = (ys[0] + ys[1]).reshape(bn, L, S5_WIDTH) + P['s5_d'][l].astype(jnp.float32) * zf
    y = jax.nn.gelu(y)
    y = y * jax.nn.sigmoid(y @ P['s5_w_glu'][l].astype(jnp.float32))
    return y.astype(z.dtype), jnp.stack(hs, axis=1)


def _conv_branch(z, P, l):
    a, b = jnp.split(z, 2, axis=-1)
    g = a * jax.nn.sigmoid(b)
    w = P['conv_w'][l].astype(g.dtype)[:, None, :]
    y = lax.conv_general_dilated(g, w, window_strides=(1,), padding=[(CONV_K // 2, CONV_K // 2)],
                                 dimension_numbers=('NWC', 'WIO', 'NWC'),
                                 feature_group_count=CONV_WIDTH)
    y = y + P['conv_b'][l]
    y = _layernorm(y, P['conv_ln_g'][l], P['conv_ln_b'][l])
    return jax.nn.silu(y)


def _sgmlp_branch(z, P, l):
    bn, L, _ = z.shape
    z = jax.nn.gelu(z)
    u, v = jnp.split(z, 2, axis=-1)
    v = _layernorm(v, P['sg_ln_g'][l], P['sg_ln_b'][l])
    v = v.reshape(bn, L // SG_CHUNK, SG_CHUNK, SG_HEADS, SG_HEAD_DIM)
    s = jnp.einsum('hqk,bnkhc->bnqhc', P['sg_w'][l], v) + P['sg_b'][l].T[None, None, :, :, None]
    return u * s.reshape(bn, L, SG_WIDTH)


def _mixer(h, P, l, h0):
    z = h @ P['w_in'][l]
    z_a, z_b, z_c = jnp.split(z, [S5_WIDTH, S5_WIDTH + 2 * CONV_WIDTH], axis=-1)
    y_a, h_final = _s5_branch(z_a, P, l, h0)
    y_b = _conv_branch(z_b, P, l)
    y_c = _sgmlp_branch(z_c, P, l)
    gates = jax.nn.sigmoid(h @ P['w_gate'][l] + P['b_gate'][l])
    g_a, g_b, g_c = jnp.split(gates, N_BRANCH, axis=-1)
    merged = (g_a * (y_a @ P['w_br_a'][l]) + g_b * (y_b @ P['w_br_b'][l])
              + g_c * (y_c @ P['w_br_c'][l]))
    return merged @ P['w_out'][l], h_final


def _layer(x, cond, P, l, h0):
    mod = (jax.nn.silu(cond) @ P['w_mod'][l] + P['b_mod'][l]).reshape(cond.shape[0], N_MOD, 1, D_MODEL)
    sh1, sc1, gt1, sh2, sc2, gt2, sh3, sc3, gt3 = [mod[:, i] for i in range(N_MOD)]
    h = _rmsnorm(x, P['norm_g'][l, 0]) * (1.0 + sc1) + sh1
    x = x + 0.5 * gt1 * _swiglu(h, P['ffn_w1'][l, 0], P['ffn_w2'][l, 0])
    h = _rmsnorm(x, P['norm_g'][l, 1]) * (1.0 + sc2) + sh2
    y, h_final = _mixer(h, P, l, h0)
    x = x + gt2 * y
    h = _rmsnorm(x, P['norm_g'][l, 2]) * (1.0 + sc3) + sh3
    x = x + 0.5 * gt3 * _swiglu(h, P['ffn_w1'][l, 1], P['ffn_w2'][l, 1])
    return x, h_final


def _grid_pos_embed(n_tokens, dim):
    rows = n_tokens // GRID_W
    rr, cc = jnp.meshgrid(jnp.arange(rows, dtype=jnp.float32), jnp.arange(GRID_W, dtype=jnp.float32), indexing='ij')
    quarter = dim // 4
    omega = 1.0 / (10000.0 ** (jnp.arange(quarter, dtype=jnp.float32) / quarter))
    def emb(p):
        ang = p.reshape(-1)[:, None] * omega[None, :]
        return jnp.concatenate([jnp.sin(ang), jnp.cos(ang)], axis=-1)
    return jnp.concatenate([emb(rr), emb(cc)], axis=-1)


def setup_inputs(seed: int = 0) -> dict:
    key = jax.random.key(seed)
    ks = jax.random.split(key, 40)
    f32 = jnp.float32
    def nrm(k, shape, s):
        return jax.random.normal(k, shape, f32) * s
    n_idx = jnp.arange(S5_STATE, dtype=f32)
    s5a = (DEPTH, 2, S5_GROUPS, S5_STATE)
    s5b = (DEPTH, 2, S5_GROUPS, S5_STATE, S5_GROUP)
    s5c = (DEPTH, 2, S5_GROUPS, S5_GROUP, S5_STATE)
    return {
        'x_prompt': nrm(ks[0], (BATCH, SEQ, D_MODEL), 1.0),
        'x_sample': nrm(ks[1], (DEC_BATCH, DEC_SEQ, D_MODEL), 1.0),
        'state_ssm': nrm(ks[2], (DEC_BATCH, DEPTH, 2, S5_GROUPS, S5_STATE, 2), 0.1),
        'c': nrm(ks[3], (DEC_BATCH, D_MODEL), 1.0),
        'c_ctx': nrm(ks[4], (D_MODEL,), 1.0),
        'w_mod': nrm(ks[5], (DEPTH, D_MODEL, N_MOD * D_MODEL), 0.5 * D_MODEL ** -0.5),
        'b_mod': nrm(ks[6], (DEPTH, N_MOD * D_MODEL), 0.02),
        'norm_g': 1.0 + nrm(ks[7], (DEPTH, 3, D_MODEL), 0.02),
        'ffn_w1': nrm(ks[8], (DEPTH, 2, D_MODEL, 2 * D_FF), D_MODEL ** -0.5),
        'ffn_w2': nrm(ks[9], (DEPTH, 2, D_FF, D_MODEL), D_FF ** -0.5),
        'w_in': nrm(ks[10], (DEPTH, D_MODEL, IN_COLS), D_MODEL ** -0.5),
        'w_gate': nrm(ks[11], (DEPTH, D_MODEL, N_BRANCH * D_MODEL), D_MODEL ** -0.5),
        'b_gate': nrm(ks[12], (DEPTH, N_BRANCH * D_MODEL), 0.02),
        's5_a_re': -0.5 + nrm(ks[13], s5a, 0.01),
        's5_a_im': math.pi * n_idx + nrm(ks[14], s5a, 0.01),
        's5_log_dt': jax.random.uniform(ks[15], (DEPTH, 2, S5_GROUPS), f32, math.log(1e-3), math.log(1e-1)),
        's5_b_re': nrm(ks[16], s5b, (2 * S5_GROUP) ** -0.5),
        's5_b_im': nrm(ks[17], s5b, (2 * S5_GROUP) ** -0.5),
        's5_c_re': nrm(ks[18], s5c, S5_STATE ** -0.5),
        's5_c_im': nrm(ks[19], s5c, S5_STATE ** -0.5),
        's5_d': nrm(ks[20], (DEPTH, S5_WIDTH), 1.0),
        's5_w_glu': nrm(ks[21], (DEPTH, S5_WIDTH, S5_WIDTH), S5_WIDTH ** -0.5),
        'w_br_a': nrm(ks[22], (DEPTH, S5_WIDTH, D_MODEL), S5_WIDTH ** -0.5),
        'conv_w': nrm(ks[23], (DEPTH, CONV_K, CONV_WIDTH), CONV_K ** -0.5),
        'conv_b': nrm(ks[24], (DEPTH, CONV_WIDTH), 0.02),
        'conv_ln_g': 1.0 + nrm(ks[25], (DEPTH, CONV_WIDTH), 0.02),
        'conv_ln_b': nrm(ks[26], (DEPTH, CONV_WIDTH), 0.02),
        'w_br_b': nrm(ks[27], (DEPTH, CONV_WIDTH, D_MODEL), CONV_WIDTH ** -0.5),
        'sg_ln_g': 1.0 + nrm(ks[28], (DEPTH, SG_WIDTH), 0.02),
        'sg_ln_b': nrm(ks[29], (DEPTH, SG_WIDTH), 0.02),
        'sg_w': nrm(ks[30], (DEPTH, SG_HEADS, SG_CHUNK, SG_CHUNK), SG_CHUNK ** -0.5),
        'sg_b': 1.0 + nrm(ks[31], (DEPTH, SG_HEADS, SG_CHUNK), 0.02),
        'w_br_c': nrm(ks[32], (DEPTH, SG_WIDTH, D_MODEL), SG_WIDTH ** -0.5),
        'w_out': nrm(ks[33], (DEPTH, D_MODEL, D_MODEL), D_MODEL ** -0.5),
        'final_g': 1.0 + nrm(ks[34], (D_MODEL,), 0.02),
    }


def reference(x_prompt, x_sample, state_ssm, c, c_ctx, w_mod, b_mod, norm_g, ffn_w1, ffn_w2,
              w_in, w_gate, b_gate, s5_a_re, s5_a_im, s5_log_dt, s5_b_re, s5_b_im, s5_c_re, s5_c_im,
              s5_d, s5_w_glu, w_br_a, conv_w, conv_b, conv_ln_g, conv_ln_b, w_br_b,
              sg_ln_g, sg_ln_b, sg_w, sg_b, w_br_c, w_out, final_g):
    P = dict(w_mod=w_mod, b_mod=b_mod, norm_g=norm_g, ffn_w1=ffn_w1, ffn_w2=ffn_w2,
             w_in=w_in, w_gate=w_gate, b_gate=b_gate, s5_a_re=s5_a_re, s5_a_im=s5_a_im,
             s5_log_dt=s5_log_dt, s5_b_re=s5_b_re, s5_b_im=s5_b_im, s5_c_re=s5_c_re, s5_c_im=s5_c_im,
             s5_d=s5_d, s5_w_glu=s5_w_glu, w_br_a=w_br_a, conv_w=conv_w, conv_b=conv_b,
             conv_ln_g=conv_ln_g, conv_ln_b=conv_ln_b, w_br_b=w_br_b, sg_ln_g=sg_ln_g,
             sg_ln_b=sg_ln_b, sg_w=sg_w, sg_b=sg_b, w_br_c=w_br_c, w_out=w_out)
    xc = x_prompt
    h_zero = jnp.zeros((x_prompt.shape[0], 2, S5_GROUPS, S5_STATE, 2), jnp.float32)
    ctx_states = []
    for l in range(DEPTH):
        xc, h_final = _layer(xc, c_ctx[None, :], P, l, h_zero)
        ctx_states.append(h_final)
    y_prompt = _rmsnorm(xc, final_g)
    new_state_ssm = jnp.stack(ctx_states, axis=1)
    xs = x_sample + _grid_pos_embed(x_sample.shape[1], D_MODEL).astype(x_sample.dtype)[None]
    for l in range(DEPTH):
        xs, _ = _layer(xs, c, P, l, state_ssm[:, l])
    y_sample = _rmsnorm(xs, final_g)
    return (y_prompt, y_sample, new_state_ssm)
```

```python
import math
import numpy as np
import concourse.bass as bass
import concourse.mybir as mybir
from concourse.bass_utils import run_bass_kernel_spmd

F32 = mybir.dt.float32
BF16 = mybir.dt.bfloat16
I32 = mybir.dt.int32
AF = mybir.ActivationFunctionType
ALU = mybir.AluOpType


class Res:
    __slots__ = ("name", "last_w", "readers", "dyn")

    def __init__(self, name, dyn=False):
        self.name = name
        self.last_w = None
        self.readers = []
        self.dyn = dyn


class Slot:
    __slots__ = ("sem", "count")

    def __init__(self, sem):
        self.sem = sem
        self.count = 0


class Ins:
    __slots__ = ("eng", "fn", "idx", "deps", "needs_inc", "inc_val", "slot", "slot_val")

    def __init__(self, eng, fn, idx):
        self.eng = eng
        self.fn = fn
        self.idx = idx
        self.deps = []
        self.needs_inc = False
        self.inc_val = None
        self.slot = None
        self.slot_val = None


ENGS = ("pe", "act", "dve", "pool", "sp")


class Prog:
    def __init__(self, nc):
        self.nc = nc
        self.streams = {e: [] for e in ENGS}
        self.sems = {}
        self._ctx = []
        self.RDYN = Res("RDYN")
        self._scopes = []
        self.dummy = None

    def enter(self, cm):
        v = cm.__enter__()
        self._ctx.append(cm)
        return v

    def sbuf(self, name, shape, dt):
        self._uid = getattr(self, "_uid", 0) + 1
        return self.enter(self.nc.sbuf_tensor(f"sb{self._uid}_{name}", list(shape), dt))

    def psum(self, name, shape, dt):
        return self.enter(self.nc.psum_tensor(name, list(shape), dt))

    def new_slot(self, name):
        cm = self.nc.semaphore(name)
        v = cm.__enter__()
        self._sem_ctx = getattr(self, "_sem_ctx", [])
        self._sem_ctx.append(cm)
        return Slot(v)

    def scope_begin(self):
        self._scopes.append(len(self._ctx))

    def scope_end(self):
        n = self._scopes.pop()
        d = self.dummy
        self.op("pool", lambda e: e.memset(d[:, 0:1], 0.0), writes=[self.RDYN])
        while len(self._ctx) > n:
            cm = self._ctx.pop()
            cm.__exit__(None, None, None)

    def _add(self, eng, fn, reads, writes):
        st = self.streams[eng]
        ins = Ins(eng, fn, len(st))
        st.append(ins)
        reads = list(reads)
        writes = list(writes)
        if any(r.dyn for r in reads) or any(w.dyn for w in writes):
            reads.append(self.RDYN)
        deps = []
        for r in reads:
            if r.last_w is not None:
                deps.append(r.last_w)
        for w in writes:
            if w.last_w is not None:
                deps.append(w.last_w)
            deps.extend(w.readers)
        seen = set()
        for d in deps:
            if d is ins or id(d) in seen:
                continue
            seen.add(id(d))
            if d.slot is None and d.eng == eng:
                if eng == "pe" or eng == "sp":
                    continue
                if ins.idx - d.idx > 1:
                    continue
            if d.slot is not None:
                ins.deps.append((d, d.slot.count))
            else:
                ins.deps.append((d, None))
                d.needs_inc = True
        for r in reads:
            r.readers.append(ins)
        for w in writes:
            w.last_w = ins
            w.readers = []
        return ins

    def op(self, eng, fn, reads=(), writes=()):
        return self._add(eng, fn, reads, writes)

    def dma(self, eng, out, in_, slot, reads=(), writes=(), **kw):
        ins = self._add(eng, lambda e: e.dma_start(out=out, in_=in_, **kw), reads, writes)
        ins.slot = slot
        slot.count += 16
        ins.slot_val = slot.count
        return ins

    def emit(self, final_waits=()):
        nc = self.nc
        for e in ENGS:
            if e == "sp":
                continue
            self.sems[e] = self.enter(nc.semaphore("sem_" + e))
        self.sems["sp"] = None
        for e in ENGS:
            c = 0
            for ins in self.streams[e]:
                if ins.needs_inc and ins.slot is None:
                    c += 1
                    ins.inc_val = c
        block = self.enter(nc.Block())

        def run(engname, e):
            waited = {}
            for ins in self.streams[engname]:
                need = {}
                for d, sval in ins.deps:
                    if d.slot is not None:
                        sem, val = d.slot.sem, sval
                    else:
                        sem, val = self.sems[d.eng], d.inc_val
                    k = id(sem)
                    if k not in need or need[k][1] < val:
                        need[k] = (sem, val)
                for k, (sem, val) in need.items():
                    if waited.get(k, 0) >= val:
                        continue
                    e.wait_ge(sem, val)
                    waited[k] = val
                r = ins.fn(e)
                if ins.slot is not None:
                    r.then_inc(ins.slot.sem, 16)
                elif ins.needs_inc:
                    r.then_inc(self.sems[engname], 1)
            if engname == "sp":
                for slot in final_waits:
                    e.wait_ge(slot.sem, slot.count)

        @block.tensor
        def _(e):
            run("pe", e)

        @block.scalar
        def _(e):
            run("act", e)

        @block.vector
        def _(e):
            run("dve", e)

        @block.gpsimd
        def _(e):
            run("pool", e)

        @block.sync
        def _(e):
            run("sp", e)

    def close(self):
        while self._ctx:
            cm = self._ctx.pop()
            cm.__exit__(None, None, None)
        for cm in reversed(getattr(self, "_sem_ctx", [])):
            cm.__exit__(None, None, None)


def CAP(ap, off, dims):
    return bass.AP(ap.tensor, ap.offset + off, [list(ap.ap[0])] + [list(d) for d in dims])


D = 1024
NT = 2048
TB = 512
NTB = 4
DFF = 2816
NF = 22
FC = 11
EPS = 1e-6
GELU = AF.Gelu_apprx_tanh
SE = "dve"
GP_OFF = [0, 286, 572, 858, 1144]
GP_LEN = 1144 + 1054

INPUT_SPECS = [
    ("xin", [1024, 2048]), ("pos", [1024, 1024]), ("cond", [1024, 2]),
    ("h0", [128, 2, 16, 2, 2]),
    ("w_mod", [2, 1024, 9216]), ("b_modT", [128, 2, 72]), ("norm_gT", [128, 2, 3, 8]), ("final_gT", [128, 8]),
    ("w1t", [2, 2, 22, 2, 128, 8, 128]), ("ffn_w2", [2, 2, 2816, 1024]),
    ("w_in", [2, 1024, 1536]), ("wgate_t", [2, 8, 128, 3, 8, 128]), ("b_gateT", [128, 2, 24]),
    ("wbr_t", [2, 8, 128, 8, 128]), ("wout_t", [2, 8, 128, 8, 128]),
    ("a_sl", [128, 2, 2, 16, 2]), ("ldt_sl", [128, 2, 16, 2]),
    ("a_cl", [128, 2, 2, 4, 2, 64]), ("ldt_cl", [128, 2, 4, 2, 64]),
    ("b_cl", [128, 2, 2, 4, 2, 64]), ("b_sl", [128, 2, 2, 16, 2, 16]), ("c_sl", [128, 2, 2, 16, 2, 16]),
    ("s5_dT", [128, 2, 4]), ("s5_w_glu", [2, 512, 512]),
    ("conv_wT", [128, 2, 2, 31]), ("conv_vT", [128, 2, 3, 2]),
    ("sg_ln", [2, 2, 256]), ("sg_wT", [128, 2, 4, 128]), ("sg_bT", [128, 2, 2, 128]),
    ("ident", [128, 128]), ("maskE", [128, 2]), ("mask2", [128, 2]), ("mask3", [128, 4, 8]),
]


def build_program(dbg=(), stop=None):
    nc = bass.Bass("TRN2", target_bir_lowering=False)
    P = Prog(nc)
    I = {}
    for name, shape in INPUT_SPECS:
        I[name] = nc.dram_tensor(name, list(shape), F32, kind="ExternalInput").ap()
    yT = nc.dram_tensor("yT", [1024, 2048], F32, kind="ExternalOutput").ap()
    nsd = nc.dram_tensor("ns", [128, 2 * 4 * 16 * 2 * 2], F32, kind="ExternalOutput").ap()
    dbg_out = {}

    xT = P.sbuf("xT", [128, 8, NT], F32)
    RX = [[Res(f"x{c}_{t}") for t in range(NTB)] for c in range(8)]

    def alloc_hT():
        return P.sbuf("hT", [128, 8, NT], BF16), [[Res(f"h{c}_{t}", dyn=True) for t in range(NTB)] for c in range(8)]
    P.dummy = P.sbuf("dummyb", [128, 4], F32)
    ones_bf = P.sbuf("ones_bf", [128, 128], BF16)
    ones256 = P.sbuf("ones256", [128, 128], F32)
    ident = P.sbuf("ident", [128, 128], F32)
    eps_t = P.sbuf("eps_t", [128, 1], F32)
    modT = P.sbuf("modT", [128, 2, 72, 2], F32)
    gsc = P.sbuf("gsc", [128, 2, 3, 8, 2], F32)
    gtv = P.sbuf("gtv", [128, 2, 3, 8, 2], F32)
    bmod = P.sbuf("bmod", [128, 2, 72], F32)
    normg = P.sbuf("normg", [128, 2, 3, 8], F32)
    finalg = P.sbuf("finalg", [128, 8], F32)
    bgate = P.sbuf("bgate", [128, 2, 24], F32)
    condT = P.sbuf("condT", [128, 8, 2], F32)
    condb = P.sbuf("condb", [128, 8, 2], BF16)
    h0t = P.sbuf("h0t", [128, 2, 16, 2, 2], F32)
    nsbuf = P.sbuf("nsbuf", [128, 2, 4, 16, 2, 2], F32)
    s5d = P.sbuf("s5d", [128, 2, 4], F32)
    convw = P.sbuf("convw", [128, 2, 2, 31], F32)
    convv = P.sbuf("convv", [128, 2, 3, 2], F32)
    maskE = P.sbuf("maskE", [128, 2], F32)
    mask2 = P.sbuf("mask2", [128, 2], F32)
    mask3 = P.sbuf("mask3", [128, 4, 8], F32)
    RC = Res("consts")
    Rmodl = [Res("mod0"), Res("mod1")]
    Rcond = Res("cond")
    Rns = Res("ns")
    banks = [P.psum(f"bank{i}", [128, 512], F32) for i in range(8)]
    RB = [Res(f"bank{i}") for i in range(8)]
    bank_ctr = [0]

    bank_reserved = [None]

    def bank():
        i = bank_ctr[0] % 8
        bank_ctr[0] += 1
        if i == bank_reserved[0]:
            i = bank_ctr[0] % 8
            bank_ctr[0] += 1
        return banks[i], RB[i]

    s_in = P.new_slot("s_in")
    s_out = P.new_slot("s_out")
    wslots = {}

    def wslot(name):
        if name not in wslots:
            wslots[name] = P.new_slot("ws_" + name)
        return wslots[name]

    P.op("pool", lambda e: e.memset(ones_bf[:], 1.0 / 1024.0), writes=[RC])
    P.op("pool", lambda e: e.memset(ones256[:], 1.0 / 256.0), writes=[RC])
    P.op("pool", lambda e: e.memset(eps_t[:], EPS), writes=[RC])
    P.op("pool", lambda e: e.memset(P.dummy[:], 0.0), writes=[RC])
    small = [(ident, "ident"), (bmod, "b_modT"), (normg, "norm_gT"), (finalg, "final_gT"), (bgate, "b_gateT"),
             (h0t, "h0"), (s5d, "s5_dT"), (convw, "conv_wT"), (convv, "conv_vT"), (maskE, "maskE"), (mask2, "mask2"),
             (mask3, "mask3")]
    for t, nm in small:
        P.dma("sp", t[:], I[nm], s_in, writes=[RC])
    P.dma("sp", condT[:], I["cond"].rearrange("(k p) c -> p k c", p=128), s_in, writes=[RC])
    for ct in range(8):
        P.dma("sp", xT[:, ct, :], I["xin"][ct * 128:(ct + 1) * 128, :], s_in, writes=RX[ct])
    P.scope_begin()
    ptmp = [P.sbuf(f"ptmp{i}", [128, 1024], F32) for i in range(2)]
    Rpt = [Res(f"ptmp{i}", dyn=True) for i in range(2)]
    for ct in range(8):
        b = ct % 2
        P.dma("sp", ptmp[b][:], I["pos"][ct * 128:(ct + 1) * 128, :], s_in, writes=[Rpt[b]])
        P.op("dve", lambda e, ct=ct, b=b: e.tensor_tensor(xT[:, ct, 1024:2048], xT[:, ct, 1024:2048], ptmp[b][:], ALU.add),
             reads=[Rpt[b]], writes=[RX[ct][2], RX[ct][3]])
    P.op("act", lambda e: e.activation(condb[:], condT[:], AF.Silu), reads=[RC], writes=[Rcond])

    P.scope_end()

    def mod_gen(l, wm, Rwm, parts=(0, 1, 2)):
        for n in parts:
            bk, rb = bank()
            bank_reserved[0] = banks.index(bk)
            for ch in range(6 * n, 6 * n + 6):
                b = ch % 2
                P.dma("pool", wm[b][:], I["w_mod"][l, :, ch * 512:(ch + 1) * 512].rearrange("(k p) c -> p k c", p=128),
                      wslot(f"wm{b}"), writes=[Rwm[b]])
                for m in range(4):
                    mt = (ch - 6 * n) * 4 + m
                    for kt in range(8):
                        P.op("pe", lambda e, bk=bk, b=b, m=m, mt=mt, kt=kt: e.matmul(
                            bk[:, mt * 2:mt * 2 + 2], lhsT=wm[b][:, kt, m * 128:(m + 1) * 128], rhs=condb[:, kt, :],
                            start=(kt == 0), stop=(kt == 7)), reads=[Rwm[b], Rcond], writes=[rb])
                yield
            P.op("dve", lambda e, bk=bk, l=l, n=n: e.tensor_tensor(
                modT[:, l, 24 * n:24 * n + 24], bk[:, 0:48].rearrange("p (m c) -> p m c", c=2),
                CAP(bmod[:, l, 24 * n:24 * n + 24], 0, [[1, 24], [0, 2]]), ALU.add), reads=[rb, RC], writes=[Rmodl[l]])
            bank_reserved[0] = None
            P.op("dve", lambda e, l=l, n=n: e.tensor_scalar_add(gsc[:, l, n], modT[:, l, (3 * n + 1) * 8:(3 * n + 2) * 8, :], 1.0),
                 reads=[Rmodl[l]], writes=[Rmodl[l]])
            P.op("dve", lambda e, l=l, n=n: e.tensor_tensor(gsc[:, l, n], gsc[:, l, n], CAP(normg[:, l, n, :], 0, [[1, 8], [0, 2]]), ALU.mult),
                 reads=[Rmodl[l], RC], writes=[Rmodl[l]])
            P.op("dve", lambda e, l=l, n=n: e.tensor_scalar_mul(gtv[:, l, n], modT[:, l, (3 * n + 2) * 8:(3 * n + 3) * 8, :],
                                                              0.5 if n != 1 else 1.0), reads=[Rmodl[l]], writes=[Rmodl[l]])

    def mod_bufs():
        wm = [P.sbuf(f"wm{i}", [128, 8, 512], BF16) for i in range(2)]
        Rwm = [Res(f"wm{i}", dyn=True) for i in range(2)]
        return wm, Rwm

    def mod_stage(l, parts=(0, 1, 2)):
        P.scope_begin()
        for _ in mod_gen(l, *mod_bufs(), parts=parts):
            pass
        P.scope_end()

    MODG = {"gen": None}

    def mod_step():
        g = MODG["gen"]
        if g is not None:
            try:
                next(g)
            except StopIteration:
                MODG["gen"] = None

    def sh_ap(l, n, ct, c):
        return modT[:, l, (3 * n) * 8 + ct, c:c + 1]

    def dump(name, ap, shape, reads, dt=F32):
        d = nc.dram_tensor("dbg_" + name, list(shape), dt, kind="ExternalOutput").ap()
        dbg_out[name] = d
        P.dma("sp", d, ap, s_out, reads=reads)

    def norm_stage(l, n, hT, RH, own_scope=False):
        if own_scope:
            P.scope_begin()
        sq, Rsq, rs, Rrs, tmp, Rtmp = norm_scratch()
        for tb in range(NTB):
            c = 0 if tb < 2 else 1
            ts = slice(tb * TB, (tb + 1) * TB)
            for ct in range(8):
                P.op("act", lambda e, ct=ct, ts=ts: e.activation(sq[:, ct, :], xT[:, ct, ts], AF.Square),
                     reads=[RX[ct][tb]], writes=[Rsq[ct]])
            bk, rb = bank()
            for ct in range(8):
                P.op("pe", lambda e, bk=bk, ct=ct: e.matmul(bk[:, :], lhsT=ones_bf[:], rhs=sq[:, ct, :], start=(ct == 0), stop=(ct == 7)),
                     reads=[Rsq[ct], RC], writes=[rb])
            r = tb % 2
            P.op("act", lambda e, bk=bk, r=r: e.activation(rs[r][:], bk[:, :], AF.Sqrt, bias=eps_t[:, 0:1], scale=1.0),
                 reads=[rb, RC], writes=[Rrs[r]])
            P.op("dve", lambda e, r=r: e.reciprocal(rs[r][:], rs[r][:]), reads=[Rrs[r]], writes=[Rrs[r]])
            for ct in range(8):
                t = ct % 2
                P.op("dve", lambda e, ct=ct, ts=ts, t=t, r=r, c=c: e.scalar_tensor_tensor(
                    out=tmp[t][:], in0=xT[:, ct, ts], scalar=gsc[:, l, n, ct, c:c + 1], in1=rs[r][:], op0=ALU.mult, op1=ALU.mult),
                    reads=[RX[ct][tb], Rmodl[l], Rrs[r]], writes=[Rtmp[t]])
                P.op("act", lambda e, ct=ct, ts=ts, t=t, c=c: e.activation(
                    hT[:, ct, ts], tmp[t][:], AF.Identity, bias=sh_ap(l, n, ct, c), scale=1.0),
                    reads=[Rtmp[t], Rmodl[l]], writes=[RH[ct][tb]])
        if own_scope:
            P.scope_end()

    def norm_scratch():
        sq = P.sbuf("sq", [128, 8, TB], BF16)
        rs = [P.sbuf(f"rs{i}", [128, TB], F32) for i in range(2)]
        tmp = [P.sbuf(f"ntmp{i}", [128, TB], F32) for i in range(2)]
        return (sq, [Res(f"sq{i}", dyn=True) for i in range(8)], rs, [Res(f"rs{i}", dyn=True) for i in range(2)],
                tmp, [Res(f"ntmp{i}", dyn=True) for i in range(2)])

    def ffn_stage(l, w, n):
        P.scope_begin()
        hT, RH = alloc_hT()
        first = (l == 0 and w == 0)
        norm_stage(l, n, hT, RH, own_scope=first)
        if first:
            MODG["gen"] = mod_gen(0, *mod_bufs(), parts=(1, 2))
        act = P.sbuf("act", [128, FC, NT], BF16)
        RA = [[Res(f"act{f}_{t}", dyn=True) for t in range(NTB)] for f in range(FC)]
        w1b = [P.sbuf(f"w1b{i}", [128, 2, 2, 8, 128], BF16) for i in range(2)]
        Rw1 = [Res(f"w1b{i}", dyn=True) for i in range(2)]
        w2b = P.sbuf("w2b", [128, FC, D], BF16)
        Rw2 = Res("w2b", dyn=True)
        sg = [P.sbuf(f"sg{i}", [128, TB], F32) for i in range(2)]
        Rsg = [Res(f"sg{i}", dyn=True) for i in range(2)]
        it = 0
        for chunk in range(2):
            P.dma("pool", w2b[:], I["ffn_w2"][l, w, chunk * FC * 128:(chunk + 1) * FC * 128, :].rearrange("(f p) d -> p f d", p=128),
                  wslot("w2b"), writes=[Rw2])
            for pair in range(6):
                nf = 2 if pair < 5 else 1
                b = it % 2
                it += 1
                f0 = chunk * FC + pair * 2
                P.dma("pool", w1b[b][:, 0:nf], I["w1t"][l, w, f0:f0 + nf].rearrange("f g p k c -> p f g k c"),
                      wslot(f"w1b{b}"), writes=[Rw1[b]])
                if first:
                    mod_step()
                for fi in range(nf):
                    f = pair * 2 + fi
                    for tb in range(NTB):
                        ts = slice(tb * TB, (tb + 1) * TB)
                        pg, rg = bank()
                        pu, ru = bank()
                        for gu, (pb_, rb_) in enumerate(((pg, rg), (pu, ru))):
                            for kt in range(8):
                                P.op("pe", lambda e, pb_=pb_, b=b, fi=fi, gu=gu, kt=kt, ts=ts: e.matmul(
                                    pb_[:, :], lhsT=w1b[b][:, fi, gu, kt, :], rhs=hT[:, kt, ts], start=(kt == 0), stop=(kt == 7)),
                                    reads=[Rw1[b], RH[kt][tb]], writes=[rb_])
                        s = (f * NTB + tb) % 2
                        P.op("act", lambda e, pg=pg, s=s: e.activation(sg[s][:], pg[:, :], AF.Silu), reads=[rg], writes=[Rsg[s]])
                        P.op("dve", lambda e, pu=pu, s=s, f=f, ts=ts: e.tensor_tensor(act[:, f, ts], sg[s][:], pu[:, :], ALU.mult),
                             reads=[Rsg[s], ru], writes=[RA[f][tb]])
            for d in range(8):
                for tb in range(NTB):
                    c = 0 if tb < 2 else 1
                    ts = slice(tb * TB, (tb + 1) * TB)
                    po, ro = bank()
                    for f in range(FC):
                        P.op("pe", lambda e, po=po, f=f, d=d, ts=ts: e.matmul(
                            po[:, :], lhsT=w2b[:, f, d * 128:(d + 1) * 128], rhs=act[:, f, ts], start=(f == 0), stop=(f == FC - 1)),
                            reads=[Rw2, RA[f][tb]], writes=[ro])
                    P.op("dve", lambda e, po=po, d=d, ts=ts, c=c: e.scalar_tensor_tensor(
                        out=xT[:, d, ts], in0=po[:, :], scalar=gtv[:, l, n, d, c:c + 1], in1=xT[:, d, ts], op0=ALU.mult, op1=ALU.add),
                        reads=[ro, Rmodl[l]], writes=[RX[d][tb]])
        while first and MODG["gen"] is not None:
            mod_step()
        P.scope_end()

    def lam_q(pref, ar, ai, ldt, shape, R_in, outs, RT):
        P.scope_begin()
        T = dict(outs)
        for nm in ["dt", "th", "mg", "t1", "t2", "s", "c", "den", "nr"]:
            T[nm] = P.sbuf(pref + nm, [128] + list(shape), F32)
        ki = P.sbuf(pref + "ki", [128] + list(shape), I32)
        two_pi = 2.0 * math.pi

        def V(nm):
            return T[nm][:]

        P.op("act", lambda e: e.activation(V("dt"), ldt, AF.Exp), reads=[R_in], writes=[RT])
        P.op("dve", lambda e: e.tensor_tensor(V("th"), ai, V("dt"), ALU.mult), reads=[R_in, RT], writes=[RT])
        P.op("dve", lambda e: e.tensor_tensor(V("mg"), ar, V("dt"), ALU.mult), reads=[R_in, RT], writes=[RT])
        P.op("act", lambda e: e.activation(V("mg"), V("mg"), AF.Exp), reads=[RT], writes=[RT])
        for which, shift in (("s", 0.0), ("c", 0.25)):
            P.op("dve", lambda e, shift=shift: e.tensor_scalar(V("t1"), V("th"), 1.0 / two_pi, shift, ALU.mult, ALU.add),
                 reads=[RT], writes=[RT])
            P.op("dve", lambda e: e.tensor_copy(ki[:], V("t1")), reads=[RT], writes=[RT])
            P.op("dve", lambda e: e.tensor_copy(V("t2"), ki[:]), reads=[RT], writes=[RT])
            P.op("dve", lambda e: e.tensor_tensor(V("t1"), V("t1"), V("t2"), ALU.subtract), reads=[RT], writes=[RT])
            P.op("dve", lambda e: e.tensor_scalar(V("t1"), V("t1"), two_pi, 3.1415925, ALU.mult, ALU.min), reads=[RT], writes=[RT])
            P.op("dve", lambda e: e.tensor_scalar_max(V("t1"), V("t1"), -3.1415925), reads=[RT], writes=[RT])
            P.op("act", lambda e, which=which: e.activation(V(which), V("t1"), AF.Sin), reads=[RT], writes=[RT])
        P.op("dve", lambda e: e.tensor_tensor(V("lr"), V("mg"), V("c"), ALU.mult), reads=[RT], writes=[RT])
        P.op("dve", lambda e: e.tensor_tensor(V("li"), V("mg"), V("s"), ALU.mult), reads=[RT], writes=[RT])
        P.op("dve", lambda e: e.tensor_scalar_add(V("nr"), V("lr"), -1.0), reads=[RT], writes=[RT])
        P.op("dve", lambda e: e.tensor_tensor(V("den"), ar, ar, ALU.mult), reads=[R_in, RT], writes=[RT])
        P.op("dve", lambda e: e.tensor_tensor(V("t1"), ai, ai, ALU.mult), reads=[R_in, RT], writes=[RT])
        P.op("dve", lambda e: e.tensor_tensor(V("den"), V("den"), V("t1"), ALU.add), reads=[RT], writes=[RT])
        P.op("dve", lambda e: e.reciprocal(V("den"), V("den")), reads=[RT], writes=[RT])
        P.op("dve", lambda e: e.tensor_tensor(V("t1"), V("nr"), ar, ALU.mult), reads=[R_in, RT], writes=[RT])
        P.op("dve", lambda e: e.tensor_tensor(V("t2"), V("li"), ai, ALU.mult), reads=[R_in, RT], writes=[RT])
        P.op("dve", lambda e: e.tensor_tensor(V("t1"), V("t1"), V("t2"), ALU.add), reads=[RT], writes=[RT])
        P.op("dve", lambda e: e.tensor_tensor(V("qr"), V("t1"), V("den"), ALU.mult), reads=[RT], writes=[RT])
        P.op("dve", lambda e: e.tensor_tensor(V("t1"), V("li"), ar, ALU.mult), reads=[R_in, RT], writes=[RT])
        P.op("dve", lambda e: e.tensor_tensor(V("t2"), V("nr"), ai, ALU.mult), reads=[R_in, RT], writes=[RT])
        P.op("dve", lambda e: e.tensor_tensor(V("t1"), V("t1"), V("t2"), ALU.subtract), reads=[RT], writes=[RT])
        P.op("dve", lambda e: e.tensor_tensor(V("qi"), V("t1"), V("den"), ALU.mult), reads=[RT], writes=[RT])
        P.scope_end()

    def cmul(eng, out_r, out_i, ar, ai, br, bi, t1, t2, reads, writes):
        P.op(eng, lambda e: e.tensor_tensor(t1, ar, br, ALU.mult), reads=reads, writes=writes)
        P.op(eng, lambda e: e.tensor_tensor(t2, ai, bi, ALU.mult), reads=reads, writes=writes)
        P.op(eng, lambda e: e.tensor_tensor(t1, t1, t2, ALU.subtract), reads=reads, writes=writes)
        P.op(eng, lambda e: e.tensor_tensor(t2, ar, bi, ALU.mult), reads=reads, writes=writes)
        P.op(eng, lambda e: e.tensor_tensor(out_i, ai, br, ALU.mult), reads=reads, writes=writes)
        P.op(eng, lambda e: e.tensor_tensor(out_i, out_i, t2, ALU.add), reads=reads, writes=writes)
        P.op(eng, lambda e: e.tensor_copy(out_r, t1), reads=reads, writes=writes)

    def cmul6(eng, out_r, out_i, ar, ai, br, bi, t1, t2, reads, writes):
        P.op(eng, lambda e: e.tensor_tensor(t1, ar, br, ALU.mult), reads=reads, writes=writes)
        P.op(eng, lambda e: e.tensor_tensor(t2, ai, bi, ALU.mult), reads=reads, writes=writes)
        P.op(eng, lambda e: e.tensor_tensor(out_r, t1, t2, ALU.subtract), reads=reads, writes=writes)
        P.op(eng, lambda e: e.tensor_tensor(t1, ar, bi, ALU.mult), reads=reads, writes=writes)
        P.op(eng, lambda e: e.tensor_tensor(t2, ai, br, ALU.mult), reads=reads, writes=writes)
        P.op(eng, lambda e: e.tensor_tensor(out_i, t1, t2, ALU.add), reads=reads, writes=writes)

    class _Stop(Exception):
        pass

    def ck(name, l):
        if stop == f"{name}{l}":
            raise _Stop()

    def mixer_stage(l):
        depth = len(P._scopes)
        try:
            _mixer(l)
            return False
        except _Stop:
            while len(P._scopes) > depth:
                P.scope_end()
            return True

    def _mixer(l):
        P.scope_begin()
        y_a = P.sbuf("y_a", [128, 4, NT], BF16)
        RYA = [[Res(f"ya{c}_{t}", dyn=True) for t in range(NTB)] for c in range(4)]
        P.scope_begin()
        U = P.sbuf("U", [128, 4, NT], BF16)
        RU = [[Res(f"U{c}_{t}", dyn=True) for t in range(NTB)] for c in range(4)]
        yg, RYG = U, RU
        P.scope_begin()
        hT, RH = alloc_hT()
        wina = P.sbuf("wina", [128, 8, 512], BF16)
        Rwina = Res("wina", dyn=True)
        P.dma("pool", wina[:], I["w_in"][l, :, 0:512].rearrange("(k p) c -> p k c", p=128), wslot("wina"), writes=[Rwina])
        norm_stage(l, 1, hT, RH)
        for ct in range(4):
            for tb in range(NTB):
                ts = slice(tb * TB, (tb + 1) * TB)
                bk, rb = bank()
                for kt in range(8):
                    P.op("pe", lambda e, bk=bk, ct=ct, kt=kt, ts=ts: e.matmul(
                        bk[:, :], lhsT=wina[:, kt, ct * 128:(ct + 1) * 128], rhs=hT[:, kt, ts], start=(kt == 0), stop=(kt == 7)),
                        reads=[Rwina, RH[kt][tb]], writes=[rb])
                P.op("act", lambda e, bk=bk, ct=ct, ts=ts: e.activation(U[:, ct, ts], bk[:, :], AF.Copy), reads=[rb], writes=[RU[ct][tb]])
        P.scope_end()
        if "za" in dbg:
            P.scope_begin()
            for ct in range(4):
                tmpd = P.sbuf(f"dbgza{ct}", [128, NT], F32)
                Rd = Res(f"dbgza{ct}", dyn=True)
                P.op("dve", lambda e, ct=ct, tmpd=tmpd: e.tensor_copy(tmpd[:], U[:, ct, :]), reads=RU[ct], writes=[Rd])
                dump(f"za{ct}", tmpd[:], [128, NT], [Rd])
            P.scope_end()
        ck("m_u", l)
        asl = P.sbuf("asl", [128, 2, 16, 2], F32)
        ldsl = P.sbuf("ldsl", [128, 16, 2], F32)
        csl = P.sbuf("csl", [128, 2, 16, 2, 16], F32)
        Rpar = Res("s5par", dyn=True)
        TS = {nm: P.sbuf("sl_" + nm, [128, 16, 2], F32) for nm in ("lr", "li", "qr", "qi")}
        RTS = Res("sl_tab", dyn=True)
        TC = {nm: P.sbuf("cl_" + nm, [128, 4, 2, 64], F32) for nm in ("lr", "li")}
        RTC = Res("cl_tab", dyn=True)
        Bc = [P.sbuf(f"Bc{i}", [128, 4, 2, 64], F32) for i in range(2)]
        Bs = [P.sbuf(f"Bs{i}", [128, 16, 2, 16], F32) for i in range(2)]
        L4c = [P.sbuf(f"L4c{i}", [128, 4, 2, 64], F32) for i in range(2)]
        RBb = Res("Bbar", dyn=True)
        P.scope_begin()
        acl = P.sbuf("acl", [128, 2, 4, 2, 64], F32)
        ldcl = P.sbuf("ldcl", [128, 4, 2, 64], F32)
        bcl = P.sbuf("bcl", [128, 2, 4, 2, 64], F32)
        bsl = P.sbuf("bsl", [128, 2, 16, 2, 16], F32)
        for t, nm in ((asl, "a_sl"), (ldsl, "ldt_sl"), (acl, "a_cl"), (ldcl, "ldt_cl"), (bcl, "b_cl"), (bsl, "b_sl"), (csl, "c_sl")):
            P.dma("sp", t[:], I[nm][:, l], s_in, writes=[Rpar])
        TCq = dict(TC)
        TCq["qr"] = P.sbuf("cl_qr", [128, 4, 2, 64], F32)
        TCq["qi"] = P.sbuf("cl_qi", [128, 4, 2, 64], F32)
        tcl = [P.sbuf(f"tcl{i}", [128, 4, 2, 64], F32) for i in range(2)]
        tsl = [P.sbuf(f"tsl{i}", [128, 16, 2, 16], F32) for i in range(2)]
        lam_q("sl_", asl[:, 0], asl[:, 1], ldsl[:], [16, 2], Rpar, TS, RTS)
        lam_q("cl_", acl[:, 0], acl[:, 1], ldcl[:], [4, 2, 64], Rpar, TCq, RTC)
        cmul("dve", Bc[0][:], Bc[1][:], TCq["qr"][:], TCq["qi"][:], bcl[:, 0], bcl[:, 1], tcl[0][:], tcl[1][:],
             [RTC, Rpar, RBb], [RBb])
        L2c = [P.sbuf(f"L2c{i}", [128, 4, 2, 64], F32) for i in range(2)]
        cmul6("dve", L2c[0][:], L2c[1][:], TC["lr"][:], TC["li"][:], TC["lr"][:], TC["li"][:], tcl[0][:], tcl[1][:], [RTC], [RTC])
        cmul6("dve", L4c[0][:], L4c[1][:], L2c[0][:], L2c[1][:], L2c[0][:], L2c[1][:], tcl[0][:], tcl[1][:], [RTC], [RTC])
        qrb = CAP(TS["qr"][:], 0, [[2, 16], [1, 2], [0, 16]])
        qib = CAP(TS["qi"][:], 0, [[2, 16], [1, 2], [0, 16]])
        cmul("dve", Bs[0][:], Bs[1][:], qrb, qib, bsl[:, 0], bsl[:, 1], tsl[0][:], tsl[1][:], [RTS, Rpar, RBb], [RBb])
        P.scope_end()
        L2 = [P.sbuf(f"L2{i}", [128, 16, 2], F32) for i in range(2)]
        L4 = [P.sbuf(f"L4{i}", [128, 16, 2], F32) for i in range(2)]
        L8 = [P.sbuf(f"L8{i}", [128, 16, 2], F32) for i in range(2)]
        t8 = [P.sbuf(f"t8{i}", [128, 4, 16, 2], F32) for i in range(2)]
        RL8 = Res("L8", dyn=True)
        cmul6("dve", L2[0][:], L2[1][:], TS["lr"][:], TS["li"][:], TS["lr"][:], TS["li"][:], t8[0][:, 0], t8[1][:, 0], [RTS, RL8], [RL8])
        cmul6("dve", L4[0][:], L4[1][:], L2[0][:], L2[1][:], L2[0][:], L2[1][:], t8[0][:, 0], t8[1][:, 0], [RL8], [RL8])
        cmul6("dve", L8[0][:], L8[1][:], L4[0][:], L4[1][:], L4[0][:], L4[1][:], t8[0][:, 0], t8[1][:, 0], [RL8], [RL8])
        LPs = [P.sbuf(f"LPs{i}", [128, 9, 16, 2], F32) for i in range(2)]
        P.op("dve", lambda e: e.memset(LPs[0][:, 0], 1.0), writes=[RL8])
        P.op("dve", lambda e: e.memset(LPs[1][:, 0], 0.0), writes=[RL8])
        P.op("dve", lambda e: e.tensor_copy(LPs[0][:, 1], TS["lr"][:]), reads=[RTS], writes=[RL8])
        P.op("dve", lambda e: e.tensor_copy(LPs[1][:, 1], TS["li"][:]), reads=[RTS], writes=[RL8])
        cmul6("dve", LPs[0][:, 2:4], LPs[1][:, 2:4], LPs[0][:, 0:2], LPs[1][:, 0:2],
              CAP(L2[0][:], 0, [[0, 2], [2, 16], [1, 2]]), CAP(L2[1][:], 0, [[0, 2], [2, 16], [1, 2]]),
              t8[0][:, 0:2], t8[1][:, 0:2], [RL8], [RL8])
        cmul6("dve", LPs[0][:, 4:8], LPs[1][:, 4:8], LPs[0][:, 0:4], LPs[1][:, 0:4],
              CAP(L4[0][:], 0, [[0, 4], [2, 16], [1, 2]]), CAP(L4[1][:], 0, [[0, 4], [2, 16], [1, 2]]),
              t8[0][:], t8[1][:], [RL8], [RL8])
        P.op("dve", lambda e: e.tensor_copy(LPs[0][:, 8], L8[0][:]), reads=[RL8], writes=[RL8])
        P.op("dve", lambda e: e.tensor_copy(LPs[1][:, 8], L8[1][:]), reads=[RL8], writes=[RL8])
        A1 = P.sbuf("A1", [128, 2, 2, 16], F32)
        A2 = P.sbuf("A2", [128, 2, 2, 16], F32)
        L8v = [CAP(L8[i][:], 0, [[1, 2], [2, 16]]) for i in range(2)]
        for ri in range(2):
            P.op("dve", lambda e, ri=ri: e.tensor_copy(A1[:, ri], L8v[0]), reads=[RL8], writes=[RL8])
            P.op("dve", lambda e, ri=ri: e.tensor_scalar_mul(A2[:, ri], L8v[1], -1.0 if ri == 0 else 1.0), reads=[RL8], writes=[RL8])
        Lh = P.sbuf("Lh", [128, 2, 2, 16], F32)
        h0v = [CAP(h0t[:, l], ri, [[2, 2], [4, 16]]) for ri in range(2)]
        lt = [P.sbuf(f"lht{i}", [128, 2, 16], F32) for i in range(2)]
        cmul("dve", Lh[:, 0], Lh[:, 1], L8v[0], L8v[1], h0v[0], h0v[1], lt[0][:], lt[1][:], [RL8, RC], [RL8])
        h0b = P.sbuf("h0b", [128, 2, 2, 16], BF16)
        for ri in range(2):
            P.op("dve", lambda e, ri=ri: e.tensor_copy(h0b[:, ri], h0v[ri]), reads=[RC], writes=[RL8])

        ck("m_tab", l)
        P.scope_begin()
        Ec = [P.sbuf(f"Ec{i}", [128, 8, 2, 64], F32) for i in range(2)]
        et = [P.sbuf(f"et{i}", [128, 4, 2, 64], F32) for i in range(2)]
        RPw = Res("Ec", dyn=True)
        EFe = P.sbuf("EFe", [128, 4096], BF16)
        Ebuf = EFe[:].rearrange("p (i d r g q) -> p i d r g q", i=8, d=2, r=2, g=2, q=64)
        RE = Res("EFe", dyn=True)
        gt = [P.sbuf(f"gt{i}", [128, 9, 4, 16], F32) for i in range(2)]
        RG = Res("gt", dyn=True)
        Gb = P.sbuf("Gb", [128, 2, 4, 2, 9, 16], BF16)
        RGb = Res("Gb", dyn=True)
        EFf = P.sbuf("EFf", [128, 4096], BF16)
        EF = EFf
        Fbuf = EFf[:].rearrange("p (a j d r g h) -> p a j d r g h", a=4, j=8, d=2, r=2, g=2, h=16)
        RF = Res("EFf", dyn=True)
        Bpad = P.sbuf("Bpad", [128, 4, 2, 2, 8, 16], BF16)
        RBp = Res("Bpad", dyn=True)
        Kblk = P.sbuf("Kblk", [128, 2, 8, 128], BF16)
        RK = Res("Kblk", dyn=True)
        XS = P.sbuf("XS", [128, 2, 2, 4, 9, 32], F32)
        RXd = [Res("XS0", dyn=True), Res("XS1", dyn=True)]
        st1 = P.sbuf("st1", [128, 2, 2, 4, 9], F32)
        st2 = P.sbuf("st2", [128, 2, 2, 4, 9], F32)
        fx1 = P.sbuf("fx1", [128, 2, 4, 32], F32)
        fx2 = P.sbuf("fx2", [128, 2, 4, 32], F32)
        cs = P.sbuf("cs", [128, 2, 4], F32)
        Sin = P.sbuf("Sin", [128, 2, 2, 4, 256], BF16)
        RSin = Res("Sin", dyn=True)
        XSZ = 2 * 4 * 9 * 32
        for ct in range(4):
            for ri in range(2):
                P.op(SE, lambda e, ri=ri, ct=ct: e.tensor_copy(Ec[ri][:, 0], Bc[ri][:, ct]), reads=[RBb], writes=[RPw])
            for k in range(1, 4):
                cmul6(SE, Ec[0][:, k], Ec[1][:, k], Ec[0][:, k - 1], Ec[1][:, k - 1], TC["lr"][:, ct], TC["li"][:, ct],
                      et[0][:, 0], et[1][:, 0], [RPw, RTC], [RPw])
            cmul6(SE, Ec[0][:, 4:8], Ec[1][:, 4:8], Ec[0][:, 0:4], Ec[1][:, 0:4],
                  CAP(L4c[0][:, ct], 0, [[0, 4], [64, 2], [1, 64]]), CAP(L4c[1][:, ct], 0, [[0, 4], [64, 2], [1, 64]]),
                  et[0][:], et[1][:], [RPw, RTC], [RPw])
            for dr in range(2):
                for ri in range(2):
                    o_ap = CAP(EFe[:, 0:1], (7 * 512 if dr == 0 else 0) + dr * 256 + ri * 128,
                               [[-512 if dr == 0 else 512, 8], [64, 2], [1, 64]])
                    P.op(SE, lambda e, o_ap=o_ap, dr=dr, ri=ri: e.tensor_tensor(
                        o_ap, CAP(Ec[ri][:, 0, dr, 0:1], 0, [[128, 8], [0, 2], [1, 64]]),
                        CAP(maskE[:], 0, [[0, 8], [1, 2], [0, 64]]), ALU.mult), reads=[RPw, RC], writes=[RE])
            for dr in range(2):
                Cr = CAP(csl[:, 0, ct * 4, dr, 0:1], 0, [[0, 9], [32, 4], [1, 16]])
                Ci = CAP(csl[:, 1, ct * 4, dr, 0:1], 0, [[0, 9], [32, 4], [1, 16]])
                Pr = CAP(LPs[0][:, 0, ct * 4, dr:dr + 1], 0, [[32, 9], [2, 4], [0, 16]])
                Pi = CAP(LPs[1][:, 0, ct * 4, dr:dr + 1], 0, [[32, 9], [2, 4], [0, 16]])
                o_r = CAP(Gb[:, 0, 0, dr, 0, 0:1], 0, [[16, 9], [288, 4], [1, 16]])
                o_i = CAP(Gb[:, 1, 0, dr, 0, 0:1], 0, [[16, 9], [288, 4], [1, 16]])
                rr, ww = [RG, RL8, Rpar], [RG]
                P.op(SE, lambda e, Cr=Cr, Pr=Pr: e.tensor_tensor(gt[0][:], Cr, Pr, ALU.mult), reads=rr, writes=ww)
                P.op(SE, lambda e, Ci=Ci, Pi=Pi: e.tensor_tensor(gt[1][:], Ci, Pi, ALU.mult), reads=rr, writes=ww)
                P.op(SE, lambda e, o_r=o_r: e.tensor_tensor(o_r, gt[0][:], gt[1][:], ALU.subtract), reads=[RG], writes=[RG, RGb])
                P.op(SE, lambda e, Cr=Cr, Pi=Pi: e.tensor_tensor(gt[0][:], Cr, Pi, ALU.mult), reads=rr, writes=ww)
                P.op(SE, lambda e, Ci=Ci, Pr=Pr: e.tensor_tensor(gt[1][:], Ci, Pr, ALU.mult), reads=rr, writes=ww)
                P.op(SE, lambda e: e.tensor_tensor(gt[0][:], gt[0][:], gt[1][:], ALU.add), reads=[RG], writes=[RG])
                P.op(SE, lambda e, o_i=o_i: e.tensor_scalar_mul(o_i, gt[0][:], -1.0), reads=[RG], writes=[RG, RGb])
            for dr in range(2):
                for ri in range(2):
                    P.op(SE, lambda e, dr=dr, ri=ri, ct=ct: e.tensor_tensor(
                        Bpad[:, :, dr, ri],
                        CAP(Bs[ri][:, ct * 4, dr, :], 0, [[32, 4], [0, 8], [1, 16]]),
                        CAP(mask3[:], 0, [[8, 4], [1, 8], [0, 16]]), ALU.mult), reads=[RBb, RC], writes=[RBp])
            if ct == 0:
                ck("m_pre", l)
            if ct == 0 and "s5e" in dbg:
                dump("Ebuf", EF[:], [128, 4096], [RE], BF16)
                dump("Bc0", Bc[0][:].rearrange("p a b c -> p (a b c)"), [128, 512], [RBb])
                dump("Bc1", Bc[1][:].rearrange("p a b c -> p (a b c)"), [128, 512], [RBb])
                dump("clr", TC["lr"][:].rearrange("p a b c -> p (a b c)"), [128, 512], [RTC])
                dump("cli", TC["li"][:].rearrange("p a b c -> p (a b c)"), [128, 512], [RTC])
            if ct == 0:
                ck("m_kblk", l)
            for gpl in range(4):
                for dr in range(2):
                    bk, rb = bank()
                    for ri in range(2):
                        for i in range(8):
                            P.op("pe", lambda e, bk=bk, gpl=gpl, dr=dr, ri=ri, i=i, ct=ct: e.matmul(
                                bk[:, ri * 256:(ri + 1) * 256],
                                lhsT=Ebuf[32 * gpl:32 * gpl + 32, i, dr, ri].rearrange("p a b -> p (a b)"),
                                rhs=CAP(U[32 * gpl:32 * gpl + 32, ct, 0:1], i, [[8, 256]]),
                                start=(i == 0), stop=(i == 7), tile_position=(32 * gpl, 0)),
                                reads=[RE] + RU[ct], writes=[rb])
                    P.op("act", lambda e, bk=bk, gpl=gpl, dr=dr: e.activation(
                        CAP(XS[:, 0, dr, gpl, 0, 0:1], 0, [[XSZ, 2], [1, 256]]),
                        bk[:, :].rearrange("p (r c) -> p r c", r=2), AF.Copy), reads=[rb], writes=[RXd[dr]])
            for dr in range(2):
                for jh in range(2):
                    bks = [bank(), bank()]
                    for gl in range(8):
                        gpl, g2 = gl // 2, gl % 2
                        bk, rb = bks[g2]
                        for ri in range(2):
                            P.op("pe", lambda e, bk=bk, gl=gl, gpl=gpl, g2=g2, ri=ri, dr=dr, jh=jh: e.matmul(
                                CAP(bk[:, 0:1], gl * 16, [[128, 4], [1, 16]]),
                                lhsT=Bpad[64 * g2:64 * g2 + 64, gpl, dr, ri].rearrange("p a b -> p (a b)"),
                                rhs=Gb[64 * g2:64 * g2 + 64, ri, gpl, dr, jh * 4:jh * 4 + 4, :],
                                start=(ri == 0), stop=(ri == 1), tile_position=(64 * g2, 0)),
                                reads=[RBp, RGb], writes=[rb])
                    for g2 in range(2):
                        bk, rb = bks[g2]
                        P.op("act", lambda e, bk=bk, dr=dr, jh=jh, g2=g2: e.activation(
                            CAP(Kblk[:, dr, jh * 4, 0:1], g2 * 16, [[128, 4], [32, 4], [1, 16]]),
                            CAP(bk[:, 0:1], g2 * 16, [[128, 4], [32, 4], [1, 16]]), AF.Copy), reads=[rb], writes=[RK])
            P.op("dve", lambda e, ct=ct: e.scalar_tensor_tensor(
                out=Kblk[:, 0, 0, :], in0=ident[:], scalar=s5d[:, l, ct:ct + 1], in1=Kblk[:, 0, 0, :],
                op0=ALU.mult, op1=ALU.add), reads=[RK, RC], writes=[RK])
            if ct == 0:
                ck("m_x", l)
            P.op(SE, lambda e: e.memset(XS[:, :, :, :, 8, :], 0.0), writes=RXd)
            for ri in range(2):
                for dr in range(2):
                    kk = 0 if dr == 0 else 31
                    P.op(SE, lambda e, ri=ri, dr=dr, kk=kk, ct=ct: e.tensor_copy(
                        XS[:, ri, dr, :, 8, kk], CAP(L8[ri][:, ct * 4, dr:dr + 1], 0, [[2, 4]])), reads=[RL8], writes=[RXd[dr]])
            for dr in range(2):
                sgm, kk = (4, 0) if dr == 0 else (7, 31)
                P.op("dve", lambda e, dr=dr, sgm=sgm, kk=kk, ct=ct: e.tensor_tensor(
                    XS[:, :, dr, :, sgm, kk], XS[:, :, dr, :, sgm, kk], Lh[:, :, dr, ct * 4:ct * 4 + 4], ALU.add),
                    reads=[RXd[dr], RL8], writes=[RXd[dr]])
            dstride = 4 * 9 * 32
            for k in range(1, 32):
                cur_ap = CAP(XS[:, 0, 0, 0, 0, 0:1], k, [[XSZ, 2], [dstride + 31 - 2 * k, 2], [288, 4], [32, 9]])
                prv_ap = CAP(XS[:, 0, 0, 0, 0, 0:1], k - 1, [[XSZ, 2], [dstride + 33 - 2 * k, 2], [288, 4], [32, 9]])
                prv_sw = CAP(XS[:, 0, 0, 0, 0, 0:1], XSZ + k - 1, [[-XSZ, 2], [dstride + 33 - 2 * k, 2], [288, 4], [32, 9]])
                a1 = CAP(A1[:, 0, 0, ct * 4:ct * 4 + 1], 0, [[32, 2], [16, 2], [1, 4], [0, 9]])
                a2 = CAP(A2[:, 0, 0, ct * 4:ct * 4 + 1], 0, [[32, 2], [16, 2], [1, 4], [0, 9]])
                P.op("dve", lambda e, prv_ap=prv_ap, a1=a1: e.tensor_tensor(st1[:], prv_ap, a1, ALU.mult), reads=RXd + [RL8], writes=RXd)
                P.op("dve", lambda e, prv_sw=prv_sw, a2=a2: e.tensor_tensor(st2[:], prv_sw, a2, ALU.mult), reads=RXd + [RL8], writes=RXd)
                P.op("dve", lambda e, cur_ap=cur_ap: e.tensor_tensor(cur_ap, cur_ap, st1[:], ALU.add), reads=RXd, writes=RXd)
                P.op("dve", lambda e, cur_ap=cur_ap: e.tensor_tensor(cur_ap, cur_ap, st2[:], ALU.add), reads=RXd, writes=RXd)
            for dr in range(2):
                order = (5, 6, 7) if dr == 0 else (6, 5, 4)
                for sgm in order:
                    src_seg, src_k = (sgm - 1, 31) if dr == 0 else (sgm + 1, 0)
                    P.op("dve", lambda e, dr=dr, src_seg=src_seg, src_k=src_k: e.tensor_scalar_mul(
                        cs[:, 0], XS[:, 1, dr, :, src_seg, src_k], -1.0), reads=[RXd[dr]], writes=[RXd[dr]])
                    P.op("dve", lambda e, dr=dr, src_seg=src_seg, src_k=src_k: e.tensor_copy(
                        cs[:, 1], XS[:, 1, dr, :, src_seg, src_k]), reads=[RXd[dr]], writes=[RXd[dr]])
                    pw_ap = CAP(XS[:, 0, dr, 0, 8, 0:1], 0, [[XSZ, 2], [288, 4], [1, 32]])
                    pw_sw = CAP(XS[:, 0, dr, 0, 8, 0:1], XSZ, [[-XSZ, 2], [288, 4], [1, 32]])
                    tgt = CAP(XS[:, 0, dr, 0, sgm, 0:1], 0, [[XSZ, 2], [288, 4], [1, 32]])
                    c_r = CAP(XS[:, 0, dr, 0, src_seg, src_k:src_k + 1], 0, [[0, 2], [288, 4], [0, 32]])
                    c_s = CAP(cs[:, 0, 0:1], 0, [[4, 2], [1, 4], [0, 32]])
                    P.op("dve", lambda e, pw_ap=pw_ap, c_r=c_r: e.tensor_tensor(fx1[:], pw_ap, c_r, ALU.mult), reads=[RXd[dr]], writes=[RXd[dr]])
                    P.op("dve", lambda e, pw_sw=pw_sw, c_s=c_s: e.tensor_tensor(fx2[:], pw_sw, c_s, ALU.mult), reads=[RXd[dr]], writes=[RXd[dr]])
                    P.op("dve", lambda e, tgt=tgt: e.tensor_tensor(tgt, tgt, fx1[:], ALU.add), reads=[RXd[dr]], writes=[RXd[dr]])
                    P.op("dve", lambda e, tgt=tgt: e.tensor_tensor(tgt, tgt, fx2[:], ALU.add), reads=[RXd[dr]], writes=[RXd[dr]])
            for dr in range(2):
                kk = 31 if dr == 0 else 0
                P.op("dve", lambda e, dr=dr, kk=kk, ct=ct: e.tensor_copy(
                    CAP(nsbuf[:, l, 0, ct * 4, dr, 0:1], 0, [[1, 2], [4, 4], [64, 4]]),
                    CAP(XS[:, 0, dr, 0, 0, kk:kk + 1], 0, [[XSZ, 2], [288, 4], [32, 4]])), reads=[RXd[dr]], writes=[Rns])
            P.op(SE, lambda e: e.memset(Sin[:], 0.0), writes=[RSin])
            P.op("dve", lambda e: e.tensor_copy(
                CAP(Sin[:, 0, 0, 0, 0:1], 1, [[2048, 2], [256, 4], [32, 4], [1, 31]]),
                CAP(XS[:, 0, 0, 0, 0, 0:1], 0, [[XSZ, 2], [288, 4], [32, 4], [1, 31]])), reads=[RXd[0]], writes=[RSin])
            P.op("dve", lambda e: e.tensor_copy(
                CAP(Sin[:, 0, 0, 0, 0:1], 129, [[2048, 2], [256, 4], [1, 127]]),
                CAP(XS[:, 0, 0, 0, 4, 0:1], 0, [[XSZ, 2], [288, 4], [1, 127]])), reads=[RXd[0]], writes=[RSin])
            P.op("dve", lambda e: e.tensor_copy(
                CAP(Sin[:, 0, 1, 0, 0:1], 0, [[2048, 2], [256, 4], [32, 4], [1, 31]]),
                CAP(XS[:, 0, 1, 0, 0, 0:1], 1, [[XSZ, 2], [288, 4], [32, 4], [1, 31]])), reads=[RXd[1]], writes=[RSin])
            P.op("dve", lambda e: e.tensor_copy(
                CAP(Sin[:, 0, 1, 0, 0:1], 128, [[2048, 2], [256, 4], [1, 127]]),
                CAP(XS[:, 0, 1, 0, 4, 0:1], 1, [[XSZ, 2], [288, 4], [1, 127]])), reads=[RXd[1]], writes=[RSin])
            P.op("dve", lambda e, ct=ct: e.tensor_copy(Sin[:, :, 0, :, 128], h0b[:, :, 0, ct * 4:ct * 4 + 4]), reads=[RL8], writes=[RSin])
            P.op("dve", lambda e, ct=ct: e.tensor_copy(Sin[:, :, 1, :, 255], h0b[:, :, 1, ct * 4:ct * 4 + 4]), reads=[RL8], writes=[RSin])
            if ct == 0 and "s5int" in dbg:
                for nm in ("lr", "li", "qr", "qi"):
                    dump("ts_" + nm, TS[nm][:].rearrange("p a b -> p (a b)"), [128, 32], [RTS])
                dump("L8r", L8[0][:].rearrange("p a b -> p (a b)"), [128, 32], [RL8])
                dump("L8i", L8[1][:].rearrange("p a b -> p (a b)"), [128, 32], [RL8])
                dump("Kblk", Kblk[:].rearrange("p a b c -> p (a b c)"), [128, 2048], [RK], BF16)
                dump("XS", XS[:].rearrange("p a b c d e -> p (a b c d e)"), [128, 4608], RXd)
                dump("Sin", Sin[:].rearrange("p a b c d -> p (a b c d)"), [128, 4096], [RSin], BF16)
                dump("Fbuf", EF[:], [128, 4096], [RE], BF16)
            if ct == 0:
                ck("m_scan", l)
            for dr in range(2):
                for ri in range(2):
                    for gpl in range(4):
                        P.op(SE, lambda e, dr=dr, ri=ri, gpl=gpl: e.tensor_tensor(
                            CAP(EFf[:, 0:1], gpl * 1024 + dr * 64 + ri * 32, [[128, 8], [16, 2], [1, 16]]),
                            CAP(Gb[:, ri, gpl, dr, 1, 0:1], 0, [[16, 8], [0, 2], [1, 16]]),
                            CAP(mask2[:], 0, [[0, 8], [1, 2], [0, 16]]), ALU.mult), reads=[RGb, RC], writes=[RF])
            for tb in range(NTB):
                ts = slice(tb * TB, (tb + 1) * TB)
                bk, rb = bank()
                bkv = bk[:, :].rearrange("p (c i) -> p c i", i=8)
                uv = U[:, ct, ts].rearrange("p (c i) -> p c i", i=8)
                for dr in range(2):
                    for j in range(8):
                        if dr == 0:
                            o_ap, r_ap = bkv[:, :, j:8], uv[:, :, 0:8 - j]
                        else:
                            o_ap, r_ap = bkv[:, :, 0:8 - j], uv[:, :, j:8]
                        P.op("pe", lambda e, o_ap=o_ap, r_ap=r_ap, dr=dr, j=j: e.matmul(
                            o_ap, lhsT=Kblk[:, dr, j, :], rhs=r_ap, start=(dr == 0 and j == 0), stop=False),
                            reads=[RK, RU[ct][tb]], writes=[rb])
                last = (3, 7, 1, 1)
                for gpl in range(4):
                    for j in range(8):
                        for dr in range(2):
                            for ri in range(2):
                                pos = j if dr == 0 else 7 - j
                                P.op("pe", lambda e, bk=bk, gpl=gpl, j=j, dr=dr, ri=ri, pos=pos, tb=tb: e.matmul(
                                    CAP(bk[32 * gpl:32 * gpl + 32, 0:1], pos, [[8, 64]]),
                                    lhsT=Fbuf[:, gpl, j, dr, ri].rearrange("p a b -> p (a b)"),
                                    rhs=Sin[:, ri, dr, gpl, tb * 64:(tb + 1) * 64],
                                    start=False, stop=((gpl, j, dr, ri) == last), tile_position=(0, 32 * gpl)),
                                    reads=[RF, RSin], writes=[rb])
                P.op("act", lambda e, bk=bk, ct=ct, ts=ts: e.activation(yg[:, ct, ts], bk[:, :], GELU), reads=[rb], writes=[RYG[ct][tb]])
            if ct == 0:
                ck("m_ct", l)
        P.scope_end()
        if "yg" in dbg:
            P.scope_begin()
            for ct in range(4):
                tmpd = P.sbuf(f"dbgyg{ct}", [128, NT], F32)
                Rd = Res(f"dbgyg{ct}", dyn=True)
                P.op("dve", lambda e, ct=ct, tmpd=tmpd: e.tensor_copy(tmpd[:], yg[:, ct, :]), reads=RYG[ct], writes=[Rd])
                dump(f"yg{ct}", tmpd[:], [128, NT], [Rd])
            P.scope_end()
        wglu = P.sbuf("wglu", [128, 4, 512], BF16)
        Rwglu = Res("wglu", dyn=True)
        P.dma("pool", wglu[:], I["s5_w_glu"][l].rearrange("(k p) c -> p k c", p=128), wslot("wglu"), writes=[Rwglu])
        sgl = [P.sbuf(f"sgl{i}", [128, TB], F32) for i in range(2)]
        Rsgl = [Res(f"sgl{i}", dyn=True) for i in range(2)]
        for tb in range(NTB):
            ts = slice(tb * TB, (tb + 1) * TB)
            for co in range(4):
                bk, rb = bank()
                for kt in range(4):
                    P.op("pe", lambda e, bk=bk, co=co, kt=kt, ts=ts: e.matmul(
                        bk[:, :], lhsT=wglu[:, kt, co * 128:(co + 1) * 128], rhs=yg[:, kt, ts], start=(kt == 0), stop=(kt == 3)),
                        reads=[Rwglu, RYG[kt][tb]], writes=[rb])
                s = co % 2
                P.op("act", lambda e, bk=bk, s=s: e.activation(sgl[s][:], bk[:, :], AF.Sigmoid), reads=[rb], writes=[Rsgl[s]])
                P.op("dve", lambda e, s=s, co=co, ts=ts: e.tensor_tensor(y_a[:, co, ts], yg[:, co, ts], sgl[s][:], ALU.mult),
                     reads=[Rsgl[s], RYG[co][tb]], writes=[RYA[co][tb]])
        P.scope_end()
        ck("m_s5", l)

        y_b = P.sbuf("y_b", [128, 2, NT], BF16)
        RYB = [[Res(f"yb{c}_{t}", dyn=True) for t in range(NTB)] for c in range(2)]
        y_c = P.sbuf("y_c", [128, 2, NT], BF16)
        RYC = [[Res(f"yc{c}_{t}", dyn=True) for t in range(NTB)] for c in range(2)]
        hT, RH = alloc_hT()
        norm_stage(l, 1, hT, RH, own_scope=True)
        P.scope_begin()
        gpad = P.sbuf("gpad", [128, 2, GP_LEN], BF16)
        Rgp = [[Res(f"gp{j}_{t}", dyn=True) for t in range(NTB)] for j in range(2)]
        u_c = P.sbuf("u_c", [128, 2, NT], BF16)
        RUC = [[Res(f"uc{j}_{t}", dyn=True) for t in range(NTB)] for j in range(2)]
        vnb = P.sbuf("vnb", [128, 16, 256], BF16)
        RV = [Res(f"vn{t}", dyn=True) for t in range(16)]
        P.op("pool", lambda e: e.memset(gpad[:], 0.0), writes=[Rgp[j][t] for j in range(2) for t in range(NTB)])
        if l == 0:
            MODG["gen"] = mod_gen(1, *mod_bufs())
        P.scope_begin()
        winb = P.sbuf("winb", [128, 8, 1024], BF16)
        Rwinb = Res("winb", dyn=True)
        P.dma("pool", winb[:, :, 0:512], I["w_in"][l, :, 512:1024].rearrange("(k p) c -> p k c", p=128), wslot("winb"), writes=[Rwinb])
        P.dma("pool", winb[:, :, 512:1024], I["w_in"][l, :, 1024:1536].rearrange("(k p) c -> p k c", p=128), wslot("winb"), writes=[Rwinb])
        sgln = P.sbuf("sgln", [128, 2, 256], F32)
        Rsgl_ = Res("sgln", dyn=True)
        P.dma("sp", sgln[:].rearrange("p a b -> p (a b)"), I["sg_ln"][l].rearrange("a b -> (a b)").partition_broadcast(128), s_in, writes=[Rsgl_])
        sb = [P.sbuf(f"sb{i}", [128, TB], F32) for i in range(2)]
        Rsb = [Res(f"sb{i}", dyn=True) for i in range(2)]

        def gp_ap(j, tb, k):
            if tb < 2:
                return CAP(gpad[:, j, 0:1], GP_OFF[2 * tb] + k, [[286, 2], [1, 256]])
            return CAP(gpad[:, j, 0:1], GP_OFF[4] + (tb - 2) * 512 + k, [[1, 512]])

        for j in range(2):
            for tb in range(NTB):
                ts = slice(tb * TB, (tb + 1) * TB)
                pa, ra = bank()
                pb2, rb2 = bank()
                for which, (pp, rr) in enumerate(((pa, ra), (pb2, rb2))):
                    c0 = which * 256 + j * 128
                    for kt in range(8):
                        P.op("pe", lambda e, pp=pp, c0=c0, kt=kt, ts=ts: e.matmul(
                            pp[:, :], lhsT=winb[:, kt, c0:c0 + 128], rhs=hT[:, kt, ts], start=(kt == 0), stop=(kt == 7)),
                            reads=[Rwinb, RH[kt][tb]], writes=[rr])
                mod_step()
                s = tb % 2
                P.op("act", lambda e, pb2=pb2, s=s: e.activation(sb[s][:], pb2[:, :], AF.Sigmoid), reads=[rb2], writes=[Rsb[s]])
                o_ap = gp_ap(j, tb, 15)
                i_ap = pa[:, :].rearrange("p (s t) -> p s t", s=2) if tb < 2 else pa[:, :]
                s_ap = sb[s][:].rearrange("p (s t) -> p s t", s=2) if tb < 2 else sb[s][:]
                P.op("dve", lambda e, o_ap=o_ap, i_ap=i_ap, s_ap=s_ap: e.tensor_tensor(o_ap, i_ap, s_ap, ALU.mult),
                     reads=[ra, Rsb[s]], writes=[Rgp[j][tb]])
        for j in range(2):
            for tb in range(NTB):
                ts = slice(tb * TB, (tb + 1) * TB)
                bk, rb = bank()
                c0 = 512 + j * 128
                for kt in range(8):
                    P.op("pe", lambda e, bk=bk, c0=c0, kt=kt, ts=ts: e.matmul(
                        bk[:, :], lhsT=winb[:, kt, c0:c0 + 128], rhs=hT[:, kt, ts], start=(kt == 0), stop=(kt == 7)),
                        reads=[Rwinb, RH[kt][tb]], writes=[rb])
                mod_step()
                P.op("act", lambda e, bk=bk, j=j, ts=ts: e.activation(u_c[:, j, ts], bk[:, :], GELU), reads=[rb], writes=[RUC[j][tb]])
        vg = [P.sbuf(f"vg{i}", [128, 256], F32) for i in range(2)]
        Rvg = [Res(f"vg{i}", dyn=True) for i in range(2)]
        bst = [P.sbuf(f"bst{i}", [128, 6], F32) for i in range(2)]
        bag = [P.sbuf(f"bag{i}", [128, 2], F32) for i in range(2)]
        for tt in range(16):
            tb = tt // 4
            tks = slice(tt * 128, (tt + 1) * 128)
            bk, rb = bank()
            for kt in range(8):
                P.op("pe", lambda e, bk=bk, kt=kt, tks=tks: e.matmul(
                    bk[:, 0:256], lhsT=hT[:, kt, tks], rhs=winb[:, kt, 768:1024], start=(kt == 0), stop=(kt == 7)),
                    reads=[Rwinb, RH[kt][tb]], writes=[rb])
            mod_step()
            s = tt % 2
            P.op("act", lambda e, bk=bk, s=s: e.activation(vg[s][:], bk[:, 0:256], GELU), reads=[rb], writes=[Rvg[s]])
            P.op("dve", lambda e, s=s: e.bn_stats(bst[s][:], vg[s][:]), reads=[Rvg[s]], writes=[Rvg[s]])
            P.op("dve", lambda e, s=s: e.bn_aggr(bag[s][:], bst[s][:]), reads=[Rvg[s]], writes=[Rvg[s]])
            P.op("act", lambda e, s=s: e.activation(bag[s][:, 1:2], bag[s][:, 1:2], AF.Sqrt, bias=eps_t[:, 0:1], scale=1.0),
                 reads=[Rvg[s], RC], writes=[Rvg[s]])
            P.op("dve", lambda e, s=s: e.reciprocal(bag[s][:, 1:2], bag[s][:, 1:2]), reads=[Rvg[s]], writes=[Rvg[s]])
            P.op("dve", lambda e, s=s: e.tensor_scalar(vg[s][:], vg[s][:], bag[s][:, 0:1], bag[s][:, 1:2], ALU.subtract, ALU.mult),
                 reads=[Rvg[s]], writes=[Rvg[s]])
            P.op("dve", lambda e, s=s: e.tensor_tensor(vg[s][:], vg[s][:], sgln[:, 0, :], ALU.mult), reads=[Rvg[s], Rsgl_], writes=[Rvg[s]])
            P.op("dve", lambda e, s=s, tt=tt: e.tensor_tensor(vnb[:, tt, :], vg[s][:], sgln[:, 1, :], ALU.add),
                 reads=[Rvg[s], Rsgl_], writes=[RV[tt]])
        P.scope_end()
        ck("m_p1", l)
        if "gpad" in dbg:
            P.scope_begin()
            for j in range(2):
                tmpd = P.sbuf(f"dbggp{j}", [128, GP_LEN], F32)
                Rd = Res(f"dbggp{j}", dyn=True)
                P.op("dve", lambda e, j=j, tmpd=tmpd: e.tensor_copy(tmpd[:], gpad[:, j, :]), reads=Rgp[j], writes=[Rd])
                dump(f"gp{j}", tmpd[:], [128, GP_LEN], [Rd])
            P.scope_end()
        P.scope_begin()
        dg = P.sbuf("dg", [128, 2, 31, 128], BF16)
        Rdg = Res("dg", dyn=True)
        for j in range(2):
            for k in range(31):
                P.op("pool", lambda e, j=j, k=k: e.tensor_scalar_mul(dg[:, j, k, :], ident[:], convw[:, l, j, k:k + 1]),
                     reads=[RC], writes=[Rdg])
        ycf = P.sbuf("ycf", [128, 2, TB], F32)
        ysq = P.sbuf("ysq", [128, 2, TB], F32)
        Rycf = [Res(f"ycf{j}", dyn=True) for j in range(2)]
        Rysq = [Res(f"ysq{j}", dyn=True) for j in range(2)]
        mean_s = P.sbuf("mean_s", [128, TB], F32)
        var_s = P.sbuf("var_s", [128, TB], F32)
        Rms = Res("mean_s", dyn=True)
        Rvs = Res("var_s", dyn=True)
        dtm = [P.sbuf(f"dtm{i}", [128, TB], F32) for i in range(2)]
        Rdtm = [Res(f"dtm{i}", dyn=True) for i in range(2)]
        for tb in range(NTB):
            ts = slice(tb * TB, (tb + 1) * TB)
            for j in range(2):
                bk, rb = bank()
                o_ap = bk[:, :].rearrange("p (s t) -> p s t", s=2) if tb < 2 else bk[:, :]
                for k in range(31):
                    P.op("pe", lambda e, o_ap=o_ap, j=j, k=k, tb=tb: e.matmul(
                        o_ap, lhsT=dg[:, j, k, :], rhs=gp_ap(j, tb, k), start=(k == 0), stop=(k == 30)),
                        reads=[Rdg, Rgp[j][tb]] + ([Rgp[j][tb - 1]] if tb == 3 else []) + ([Rgp[j][tb + 1]] if tb == 2 else []),
                        writes=[rb])
                P.op("act", lambda e, bk=bk, j=j: e.activation(ycf[:, j, :], bk[:, :], AF.Identity, bias=convv[:, l, 0, j:j + 1], scale=1.0),
                     reads=[rb, RC], writes=[Rycf[j]])
                P.op("act", lambda e, bk=bk, j=j: e.activation(ysq[:, j, :], bk[:, :], AF.Square, bias=convv[:, l, 0, j:j + 1], scale=1.0),
                     reads=[rb, RC], writes=[Rysq[j]])
            bm, rbm = bank()
            bq, rbq = bank()
            for j in range(2):
                P.op("pe", lambda e, bm=bm, j=j: e.matmul(bm[:, :], lhsT=ones256[:], rhs=ycf[:, j, :], start=(j == 0), stop=(j == 1)),
                     reads=[Rycf[j], RC], writes=[rbm])
            for j in range(2):
                P.op("pe", lambda e, bq=bq, j=j: e.matmul(bq[:, :], lhsT=ones256[:], rhs=ysq[:, j, :], start=(j == 0), stop=(j == 1)),
                     reads=[Rysq[j], RC], writes=[rbq])
            P.op("act", lambda e, bm=bm: e.activation(mean_s[:], bm[:, :], AF.Copy), reads=[rbm], writes=[Rms])
            P.op("dve", lambda e: e.tensor_tensor(var_s[:], mean_s[:], mean_s[:], ALU.mult), reads=[Rms], writes=[Rvs])
            P.op("dve", lambda e, bq=bq: e.tensor_tensor(var_s[:], bq[:, :], var_s[:], ALU.subtract), reads=[rbq, Rvs], writes=[Rvs])
            P.op("act", lambda e: e.activation(var_s[:], var_s[:], AF.Sqrt, bias=eps_t[:, 0:1], scale=1.0), reads=[Rvs, RC], writes=[Rvs])
            P.op("dve", lambda e: e.reciprocal(var_s[:], var_s[:]), reads=[Rvs], writes=[Rvs])
            for j in range(2):
                P.op("dve", lambda e, j=j: e.tensor_tensor(dtm[j][:], ycf[:, j, :], mean_s[:], ALU.subtract), reads=[Rycf[j], Rms], writes=[Rdtm[j]])
                P.op("dve", lambda e, j=j: e.tensor_tensor(dtm[j][:], dtm[j][:], var_s[:], ALU.mult), reads=[Rdtm[j], Rvs], writes=[Rdtm[j]])
                P.op("act", lambda e, j=j, ts=ts: e.activation(y_b[:, j, ts], dtm[j][:], AF.Silu, bias=convv[:, l, 2, j:j + 1],
                                                           scale=convv[:, l, 1, j:j + 1]), reads=[Rdtm[j], RC], writes=[RYB[j][tb]])
        P.scope_end()
        ck("m_p2", l)
        P.scope_begin()
        sgw = P.sbuf("sgw", [128, 4, 128], BF16)
        sgb = P.sbuf("sgb", [128, 2, 128], F32)
        Rsgp = Res("sgpar", dyn=True)
        P.dma("pool", sgw[:], I["sg_wT"][:, l], wslot("sgw"), writes=[Rsgp])
        P.dma("sp", sgb[:], I["sg_bT"][:, l], s_in, writes=[Rsgp])
        stm = [P.sbuf(f"stm{i}", [128, TB], F32) for i in range(2)]
        Rstm = [Res(f"stm{i}", dyn=True) for i in range(2)]
        for hp in range(2):
            for tb in range(NTB):
                ts = slice(tb * TB, (tb + 1) * TB)
                bk, rb = bank()
                for n4 in range(4):
                    tt = tb * 4 + n4
                    for h2 in range(2):
                        h = hp * 2 + h2
                        P.op("pe", lambda e, bk=bk, n4=n4, tt=tt, h2=h2, h=h: e.matmul(
                            bk[64 * h2:64 * h2 + 64, n4 * 128:(n4 + 1) * 128], lhsT=vnb[:, tt, h * 64:(h + 1) * 64], rhs=sgw[:, h, :],
                            start=True, stop=True, tile_position=(0, 64 * h2)), reads=[RV[tt], Rsgp], writes=[rb])
                s = tb % 2
                P.op("dve", lambda e, bk=bk, s=s, hp=hp: e.tensor_tensor(
                    stm[s][:].rearrange("p (n q) -> p n q", n=4), bk[:, :].rearrange("p (n q) -> p n q", n=4),
                    CAP(sgb[:, hp, :], 0, [[0, 4], [1, 128]]), ALU.add), reads=[rb, Rsgp], writes=[Rstm[s]])
                P.op("dve", lambda e, s=s, hp=hp, ts=ts: e.tensor_tensor(y_c[:, hp, ts], stm[s][:], u_c[:, hp, ts], ALU.mult),
                     reads=[Rstm[s], RUC[hp][tb]], writes=[RYC[hp][tb]])
        P.scope_end()
        while MODG["gen"] is not None:
            mod_step()
        P.scope_end()
        ck("m_p3", l)

        if "ybc" in dbg:
            P.scope_begin()
            for nm, buf, RR, n in (("ya", y_a, RYA, 4), ("yb", y_b, RYB, 2), ("yc", y_c, RYC, 2)):
                for ct in range(n):
                    tmpd = P.sbuf(f"dbg{nm}{ct}", [128, NT], F32)
                    Rd = Res(f"dbg{nm}{ct}", dyn=True)
                    P.op("dve", lambda e, ct=ct, tmpd=tmpd, buf=buf: e.tensor_copy(tmpd[:], buf[:, ct, :]), reads=RR[ct], writes=[Rd])
                    dump(f"{nm}{ct}", tmpd[:], [128, NT], [Rd])
            P.scope_end()

        P.scope_begin()
        mg = P.sbuf("mg", [128, 8, NT], BF16)
        RM = [[Res(f"mg{c}_{t}", dyn=True) for t in range(NTB)] for c in range(8)]
        wg = [P.sbuf(f"wg{i}", [128, 3, 8, 128], BF16) for i in range(2)]
        Rwg = [Res(f"wg{i}", dyn=True) for i in range(2)]
        wbr = [P.sbuf(f"wbr{i}", [128, 8, 128], BF16) for i in range(2)]
        Rwbr = [Res(f"wbr{i}", dyn=True) for i in range(2)]
        wo = [P.sbuf(f"wo{i}", [128, 8, 128], BF16) for i in range(2)]
        Rwo = [Res(f"wo{i}", dyn=True) for i in range(2)]
        gs = [P.sbuf(f"gs{i}", [128, TB], F32) for i in range(3)]
        Rgs = [Res(f"gs{i}", dyn=True) for i in range(3)]
        mt = [P.sbuf(f"mt{i}", [128, TB], F32) for i in range(3)]
        Rmt = [Res(f"mt{i}", dyn=True) for i in range(3)]
        ybufs = [(y_a, RYA, 4, 0), (y_b, RYB, 2, 4), (y_c, RYC, 2, 6)]
        for d in range(8):
            b = d % 2
            P.dma("pool", wg[b][:], I["wgate_t"][l, d], wslot(f"wg{b}"), writes=[Rwg[b]])
            P.dma("pool", wbr[b][:], I["wbr_t"][l, d], wslot(f"wbr{b}"), writes=[Rwbr[b]])
            for tb in range(NTB):
                ts = slice(tb * TB, (tb + 1) * TB)
                for br in range(3):
                    bg_, rg_ = bank()
                    for kt in range(8):
                        P.op("pe", lambda e, bg_=bg_, b=b, br=br, kt=kt, ts=ts: e.matmul(
                            bg_[:, :], lhsT=wg[b][:, br, kt, :], rhs=hT[:, kt, ts], start=(kt == 0), stop=(kt == 7)),
                            reads=[Rwg[b], RH[kt][tb]], writes=[rg_])
                    P.op("act", lambda e, bg_=bg_, br=br, d=d: e.activation(
                        gs[br][:], bg_[:, :], AF.Sigmoid, bias=bgate[:, l, br * 8 + d:br * 8 + d + 1], scale=1.0),
                        reads=[rg_, RC], writes=[Rgs[br]])
                    ybuf, RY, nk, k0 = ybufs[br]
                    bp_, rp_ = bank()
                    for kt in range(nk):
                        P.op("pe", lambda e, bp_=bp_, b=b, kt=kt, k0=k0, ybuf=ybuf, nk=nk, ts=ts: e.matmul(
                            bp_[:, :], lhsT=wbr[b][:, k0 + kt, :], rhs=ybuf[:, kt, ts], start=(kt == 0), stop=(kt == nk - 1)),
                            reads=[Rwbr[b], RY[kt][tb]], writes=[rp_])
                    P.op("dve", lambda e, bp_=bp_, br=br: e.tensor_tensor(mt[br][:], bp_[:, :], gs[br][:], ALU.mult),
                         reads=[rp_, Rgs[br]], writes=[Rmt[br]])
                P.op("pool", lambda e: e.tensor_tensor(mt[0][:], mt[0][:], mt[1][:], ALU.add), reads=[Rmt[1]], writes=[Rmt[0]])
                P.op("pool", lambda e, d=d, ts=ts: e.tensor_tensor(mg[:, d, ts], mt[0][:], mt[2][:], ALU.add),
                     reads=[Rmt[0], Rmt[2]], writes=[RM[d][tb]])
        for d in range(8):
            b = d % 2
            P.dma("pool", wo[b][:], I["wout_t"][l, d], wslot(f"wo{b}"), writes=[Rwo[b]])
            for tb in range(NTB):
                c = 0 if tb < 2 else 1
                ts = slice(tb * TB, (tb + 1) * TB)
                po, ro = bank()
                for kt in range(8):
                    P.op("pe", lambda e, po=po, b=b, kt=kt, ts=ts: e.matmul(
                        po[:, :], lhsT=wo[b][:, kt, :], rhs=mg[:, kt, ts], start=(kt == 0), stop=(kt == 7)),
                        reads=[Rwo[b], RM[kt][tb]], writes=[ro])
                P.op("dve", lambda e, po=po, d=d, ts=ts, c=c: e.scalar_tensor_tensor(
                    out=xT[:, d, ts], in0=po[:, :], scalar=gtv[:, l, 1, d, c:c + 1], in1=xT[:, d, ts], op0=ALU.mult, op1=ALU.add),
                    reads=[ro, Rmodl[l]], writes=[RX[d][tb]])
        P.scope_end()
        P.scope_end()

    def final_stage():
        P.scope_begin()
        sq, Rsq, rs, Rrs, tmp, Rtmp = norm_scratch()
        ot = [P.sbuf(f"ot{i}", [128, TB], F32) for i in range(4)]
        Rot = [Res(f"ot{i}", dyn=True) for i in range(4)]
        for tb in range(NTB):
            ts = slice(tb * TB, (tb + 1) * TB)
            for ct in range(8):
                P.op("act", lambda e, ct=ct, ts=ts: e.activation(sq[:, ct, :], xT[:, ct, ts], AF.Square), reads=[RX[ct][tb]], writes=[Rsq[ct]])
            bk, rb = bank()
            for ct in range(8):
                P.op("pe", lambda e, bk=bk, ct=ct: e.matmul(bk[:, :], lhsT=ones_bf[:], rhs=sq[:, ct, :], start=(ct == 0), stop=(ct == 7)),
                     reads=[Rsq[ct], RC], writes=[rb])
            r = tb % 2
            P.op("act", lambda e, bk=bk, r=r: e.activation(rs[r][:], bk[:, :], AF.Sqrt, bias=eps_t[:, 0:1], scale=1.0), reads=[rb, RC], writes=[Rrs[r]])
            P.op("dve", lambda e, r=r: e.reciprocal(rs[r][:], rs[r][:]), reads=[Rrs[r]], writes=[Rrs[r]])
            for ct in range(8):
                t = ct % 4
                P.op("dve", lambda e, ct=ct, ts=ts, t=t, r=r: e.scalar_tensor_tensor(
                    out=ot[t][:], in0=xT[:, ct, ts], scalar=finalg[:, ct:ct + 1], in1=rs[r][:], op0=ALU.mult, op1=ALU.mult),
                    reads=[RX[ct][tb], RC, Rrs[r]], writes=[Rot[t]])
                P.dma("sp", yT[ct * 128:(ct + 1) * 128, ts], ot[t][:], s_out, reads=[Rot[t]])
        P.dma("sp", nsd, nsbuf[:].rearrange("p a b c d e -> p (a b c d e)"), s_out, reads=[Rns])
        P.scope_end()

    def dump_x(tag):
        for ct in range(8):
            dump(f"{tag}_{ct}", xT[:, ct, :], [128, NT], RX[ct])

    done = False
    mod_stage(0, parts=(0,))
    for l in range(2):
        if stop == "mod":
            break
        ffn_stage(l, 0, 0)
        if f"x1_{l}" in dbg:
            dump_x(f"x1_{l}")
        if stop == f"x1_{l}":
            done = True
            break
        if mixer_stage(l):
            done = True
            break
        if f"x2_{l}" in dbg:
            dump_x(f"x2_{l}")
        if stop == f"x2_{l}":
            done = True
            break
        ffn_stage(l, 1, 2)
        if f"x3_{l}" in dbg:
            dump_x(f"x3_{l}")
        if stop == f"x3_{l}":
            done = True
            break
    final_stage()
    P.emit(final_waits=[s_out])
    P.close()
    return nc, list(dbg_out)


def _grid_pos_embed_T():
    rows = 1024 // 64
    rr, cc = np.meshgrid(np.arange(rows, dtype=np.float32), np.arange(64, dtype=np.float32), indexing="ij")
    quarter = 256
    omega = (1.0 / (10000.0 ** (np.arange(quarter, dtype=np.float32) / np.float32(quarter)))).astype(np.float32)

    def emb(p):
        ang = p.reshape(-1)[:, None].astype(np.float32) * omega[None, :]
        return np.concatenate([np.sin(ang), np.cos(ang)], axis=-1)

    pe = np.concatenate([emb(rr), emb(cc)], axis=-1).astype(np.float32)
    return np.ascontiguousarray(pe.T)


def _prep_shared(inp):
    f = lambda a: np.ascontiguousarray(np.asarray(a, dtype=np.float32))
    S = {}
    S["pos"] = _grid_pos_embed_T()
    S["w_mod"] = f(inp["w_mod"])
    S["b_modT"] = f(inp["b_mod"].reshape(2, 72, 128).transpose(2, 0, 1))
    S["norm_gT"] = f(inp["norm_g"].reshape(2, 3, 8, 128).transpose(3, 0, 1, 2))
    S["final_gT"] = f(inp["final_g"].reshape(8, 128).T)
    w1 = inp["ffn_w1"].reshape(2, 2, 8, 128, 2, 22, 128)
    S["w1t"] = f(w1.transpose(0, 1, 5, 4, 3, 2, 6))
    S["ffn_w2"] = f(inp["ffn_w2"])
    S["w_in"] = f(inp["w_in"])
    wg = inp["w_gate"].reshape(2, 8, 128, 3, 8, 128)
    S["wgate_t"] = f(wg.transpose(0, 4, 2, 3, 1, 5))
    S["b_gateT"] = f(inp["b_gate"].reshape(2, 24, 128).transpose(2, 0, 1))
    wbr = np.concatenate([inp["w_br_a"], inp["w_br_b"], inp["w_br_c"]], axis=1)
    S["wbr_t"] = f(wbr.reshape(2, 8, 128, 8, 128).transpose(0, 3, 2, 1, 4))
    S["wout_t"] = f(inp["w_out"].reshape(2, 8, 128, 8, 128).transpose(0, 3, 2, 1, 4))
    a = np.stack([inp["s5_a_re"], inp["s5_a_im"]], axis=0)
    a6 = a.reshape(2, 2, 2, 16, 2, 64)
    S["a_sl"] = f(a6.transpose(4, 5, 1, 0, 3, 2).reshape(128, 2, 2, 16, 2))
    ld = inp["s5_log_dt"].reshape(2, 2, 16, 2)
    S["ldt_sl"] = f(np.broadcast_to(ld.transpose(3, 0, 2, 1)[:, None], (2, 64, 2, 16, 2)).reshape(128, 2, 16, 2))
    a7 = a.reshape(2, 2, 2, 4, 8, 64)
    S["a_cl"] = f(np.broadcast_to(a7.transpose(4, 1, 0, 3, 2, 5)[:, None], (8, 16, 2, 2, 4, 2, 64)).reshape(128, 2, 2, 4, 2, 64))
    ld2 = inp["s5_log_dt"].reshape(2, 2, 4, 8)
    S["ldt_cl"] = f(np.broadcast_to(ld2.transpose(3, 0, 2, 1)[:, None, :, :, :, None], (8, 16, 2, 4, 2, 64)).reshape(128, 2, 4, 2, 64))
    b = np.stack([inp["s5_b_re"], inp["s5_b_im"]], axis=0)
    b7 = b.reshape(2, 2, 2, 4, 8, 64, 16)
    S["b_cl"] = f(b7.transpose(4, 6, 1, 0, 3, 2, 5).reshape(128, 2, 2, 4, 2, 64))
    b8 = b.reshape(2, 2, 2, 16, 2, 64, 16)
    S["b_sl"] = f(b8.transpose(4, 5, 1, 0, 3, 2, 6).reshape(128, 2, 2, 16, 2, 16))
    c = np.stack([inp["s5_c_re"], inp["s5_c_im"]], axis=0)
    c8 = c.reshape(2, 2, 2, 16, 2, 16, 64)
    S["c_sl"] = f(c8.transpose(4, 6, 1, 0, 3, 2, 5).reshape(128, 2, 2, 16, 2, 16))
    S["s5_dT"] = f(inp["s5_d"].reshape(2, 4, 128).transpose(2, 0, 1))
    S["s5_w_glu"] = f(inp["s5_w_glu"])
    S["conv_wT"] = f(inp["conv_w"].reshape(2, 31, 2, 128).transpose(3, 0, 2, 1))
    cv = np.stack([inp["conv_b"], inp["conv_ln_g"], inp["conv_ln_b"]], axis=1)
    S["conv_vT"] = f(cv.reshape(2, 3, 2, 128).transpose(3, 0, 1, 2))
    S["sg_ln"] = f(np.stack([inp["sg_ln_g"], inp["sg_ln_b"]], axis=1))
    S["sg_wT"] = f(inp["sg_w"].transpose(3, 0, 1, 2))
    sgb = inp["sg_b"].reshape(2, 2, 2, 128)
    S["sg_bT"] = f(np.broadcast_to(sgb.transpose(2, 0, 1, 3)[:, None], (2, 64, 2, 2, 128)).reshape(128, 2, 2, 128))
    S["ident"] = np.eye(128, dtype=np.float32)
    p = np.arange(128)
    S["maskE"] = f(((p[:, None] // 16) % 2 == np.arange(2)[None, :]))
    S["mask2"] = f(((p[:, None] // 64) == np.arange(2)[None, :]))
    S["mask3"] = f((np.arange(8)[None, None, :] == (2 * np.arange(4)[None, :, None] + (p // 64)[:, None, None])))
    return S


def _prep_core(inp, core):
    f = lambda a: np.ascontiguousarray(np.asarray(a, dtype=np.float32))
    C = {}
    xp = inp["x_prompt"][4 * core:4 * core + 4].reshape(1024, 1024)
    xs = inp["x_sample"][core]
    C["xin"] = f(np.concatenate([xp, xs], axis=0).T)
    C["cond"] = f(np.stack([inp["c_ctx"], inp["c"][core]], axis=1))
    h0 = inp["state_ssm"][core].reshape(2, 2, 16, 2, 64, 2)
    C["h0"] = f(h0.transpose(3, 4, 0, 2, 1, 5).reshape(128, 2, 16, 2, 2))
    return C


_NC_CACHE = {}


def kernel(**inputs):
    inp = {k: np.asarray(v) for k, v in inputs.items()}
    if "nc" not in _NC_CACHE:
        _NC_CACHE["nc"] = build_program()[0]
    nc = _NC_CACHE["nc"]
    S = _prep_shared(inp)
    in_maps = []
    for core in range(8):
        m = dict(S)
        m.update(_prep_core(inp, core))
        in_maps.append(m)
    res = run_bass_kernel_spmd(nc, in_maps, core_ids=list(range(8)))
    y_prompt = np.empty((32, 256, 1024), np.float32)
    y_sample = np.empty((8, 1024, 1024), np.float32)
    new_state = np.empty((32, 2, 2, 32, 64, 2), np.float32)
    for core in range(8):
        r = res.results[core]
        y = np.asarray(r["yT"]).T
        y_prompt[4 * core:4 * core + 4] = y[:1024].reshape(4, 256, 1024)
        y_sample[core] = y[1024:]
        ns = np.asarray(r["ns"]).reshape(2, 64, 2, 4, 16, 2, 2)
        new_state[4 * core:4 * core + 4] = ns.transpose(3, 2, 5, 4, 0, 1, 6).reshape(4, 2, 2, 32, 64, 2)
    return (y_prompt, y_sample, new_state)
```

```python
import math
import numpy as np
import concourse.bass as bass
import concourse.mybir as mybir
from concourse.bass_utils import run_bass_kernel_spmd

F32 = mybir.dt.float32
BF16 = mybir.dt.bfloat16
I32 = mybir.dt.int32
AF = mybir.ActivationFunctionType
ALU = mybir.AluOpType


class Res:
    __slots__ = ("name", "last_w", "readers", "dyn")

    def __init__(self, name, dyn=False):
        self.name = name
        self.last_w = None
        self.readers = []
        self.dyn = dyn


class Slot:
    __slots__ = ("sem", "count")

    def __init__(self, sem):
        self.sem = sem
        self.count = 0


class Ins:
    __slots__ = ("eng", "fn", "idx", "deps", "needs_inc", "inc_val", "slot", "slot_val")

    def __init__(self, eng, fn, idx):
        self.eng = eng
        self.fn = fn
        self.idx = idx
        self.deps = []
        self.needs_inc = False
        self.inc_val = None
        self.slot = None
        self.slot_val = None


ENGS = ("pe", "act", "dve", "pool", "sp")


class Prog:
    def __init__(self, nc):
        self.nc = nc
        self.streams = {e: [] for e in ENGS}
        self.sems = {}
        self._ctx = []
        self.RDYN = Res("RDYN")
        self._scopes = []
        self.dummy = None

    def enter(self, cm):
        v = cm.__enter__()
        self._ctx.append(cm)
        return v

    def sbuf(self, name, shape, dt):
        self._uid = getattr(self, "_uid", 0) + 1
        return self.enter(self.nc.sbuf_tensor(f"sb{self._uid}_{name}", list(shape), dt))

    def psum(self, name, shape, dt):
        return self.enter(self.nc.psum_tensor(name, list(shape), dt))

    def new_slot(self, name):
        cm = self.nc.semaphore(name)
        v = cm.__enter__()
        self._sem_ctx = getattr(self, "_sem_ctx", [])
        self._sem_ctx.append(cm)
        return Slot(v)

    def scope_begin(self):
        self._scopes.append(len(self._ctx))

    def scope_end(self):
        n = self._scopes.pop()
        d = self.dummy
        self.op("pool", lambda e: e.memset(d[:, 0:1], 0.0), writes=[self.RDYN])
        while len(self._ctx) > n:
            cm = self._ctx.pop()
            cm.__exit__(None, None, None)

    def _add(self, eng, fn, reads, writes):
        st = self.streams[eng]
        ins = Ins(eng, fn, len(st))
        st.append(ins)
        reads = list(reads)
        writes = list(writes)
        if any(r.dyn for r in reads) or any(w.dyn for w in writes):
            reads.append(self.RDYN)
        deps = []
        for r in reads:
            if r.last_w is not None:
                deps.append(r.last_w)
        for w in writes:
            if w.last_w is not None:
                deps.append(w.last_w)
            deps.extend(w.readers)
        seen = set()
        for d in deps:
            if d is ins or id(d) in seen:
                continue
            seen.add(id(d))
            if d.slot is None and d.eng == eng:
                if eng == "pe" or eng == "sp":
                    continue
                if ins.idx - d.idx > 1:
                    continue
            if d.slot is not None:
                ins.deps.append((d, d.slot.count))
            else:
                ins.deps.append((d, None))
                d.needs_inc = True
        for r in reads:
            r.readers.append(ins)
        for w in writes:
            w.last_w = ins
            w.readers = []
        return ins

    def op(self, eng, fn, reads=(), writes=()):
        return self._add(eng, fn, reads, writes)

    def dma(self, eng, out, in_, slot, reads=(), writes=(), **kw):
        ins = self._add(eng, lambda e: e.dma_start(out=out, in_=in_, **kw), reads, writes)
        ins.slot = slot
        slot.count += 16
        ins.slot_val = slot.count
        return ins

    def emit(self, final_waits=()):
        nc = self.nc
        for e in ENGS:
            if e == "sp":
                continue
            self.sems[e] = self.enter(nc.semaphore("sem_" + e))
        self.sems["sp"] = None
        for e in ENGS:
            c = 0
            for ins in self.streams[e]:
                if ins.needs_inc and ins.slot is None:
                    c += 1
                    ins.inc_val = c
        block = self.enter(nc.Block())

        def run(engname, e):
            waited = {}
            for ins in self.streams[engname]:
                need = {}
                for d, sval in ins.deps:
                    if d.slot is not None:
                        sem, val = d.slot.sem, sval
                    else:
                        sem, val = self.sems[d.eng], d.inc_val
                    k = id(sem)
                    if k not in need or need[k][1] < val:
                        need[k] = (sem, val)
                for k, (sem, val) in need.items():
                    if waited.get(k, 0) >= val:
                        continue
                    e.wait_ge(sem, val)
                    waited[k] = val
                r = ins.fn(e)
                if ins.slot is not None:
                    r.then_inc(ins.slot.sem, 16)
                elif ins.needs_inc:
                    r.then_inc(self.sems[engname], 1)
            if engname == "sp":
                for slot in final_waits:
                    e.wait_ge(slot.sem, slot.count)

        @block.tensor
        def _(e):
            run("pe", e)

        @block.scalar
        def _(e):
            run("act", e)

        @block.vector
        def _(e):
            run("dve", e)

        @block.gpsimd
        def _(e):
            run("pool", e)

        @block.sync
        def _(e):
            run("sp", e)

    def close(self):
        while self._ctx:
            cm = self._ctx.pop()
            cm.__exit__(None, None, None)
        for cm in reversed(getattr(self, "_sem_ctx", [])):
            cm.__exit__(None, None, None)


def CAP(ap, off, dims):
    return bass.AP(ap.tensor, ap.offset + off, [list(ap.ap[0])] + [list(d) for d in dims])


D = 1024
NT = 2048
TB = 512
NTB = 4
DFF = 2816
NF = 22
FC = 11
EPS = 1e-6
GELU = AF.Gelu_apprx_tanh
SE = "dve"
GP_OFF = [0, 286, 572, 858, 1144]
GP_LEN = 1144 + 1054

INPUT_SPECS = [
    ("xin", [1024, 2048]), ("pos", [1024, 1024]), ("cond", [1024, 2]),
    ("h0", [128, 2, 16, 2, 2]),
    ("w_mod", [2, 1024, 9216]), ("b_modT", [128, 2, 72]), ("norm_gT", [128, 2, 3, 8]), ("final_gT", [128, 8]),
    ("w1t", [2, 2, 22, 2, 128, 8, 128]), ("ffn_w2", [2, 2, 2816, 1024]),
    ("w_in", [2, 1024, 1536]), ("wgate_t", [2, 8, 128, 3, 8, 128]), ("b_gateT", [128, 2, 24]),
    ("wbr_t", [2, 8, 128, 8, 128]), ("wout_t", [2, 8, 128, 8, 128]),
    ("a_sl", [128, 2, 2, 16, 2]), ("ldt_sl", [128, 2, 16, 2]),
    ("a_cl", [128, 2, 2, 4, 2, 64]), ("ldt_cl", [128, 2, 4, 2, 64]),
    ("b_cl", [128, 2, 2, 4, 2, 64]), ("b_sl", [128, 2, 2, 16, 2, 16]), ("c_sl", [128, 2, 2, 16, 2, 16]),
    ("s5_dT", [128, 2, 4]), ("s5_w_glu", [2, 512, 512]),
    ("conv_wT", [128, 2, 2, 31]), ("conv_vT", [128, 2, 3, 2]),
    ("sg_ln", [2, 2, 256]), ("sg_wT", [128, 2, 4, 128]), ("sg_bT", [128, 2, 2, 128]),
    ("ident", [128, 128]), ("maskE", [128, 2]), ("mask2", [128, 2]), ("mask3", [128, 4, 8]),
]


def build_program(dbg=(), stop=None):
    nc = bass.Bass("TRN2", target_bir_lowering=False)
    P = Prog(nc)
    I = {}
    for name, shape in INPUT_SPECS:
        I[name] = nc.dram_tensor(name, list(shape), F32, kind="ExternalInput").ap()
    yT = nc.dram_tensor("yT", [1024, 2048], F32, kind="ExternalOutput").ap()
    nsd = nc.dram_tensor("ns", [128, 2 * 4 * 16 * 2 * 2], F32, kind="ExternalOutput").ap()
    dbg_out = {}

    xT = P.sbuf("xT", [128, 8, NT], F32)
    RX = [[Res(f"x{c}_{t}") for t in range(NTB)] for c in range(8)]

    def alloc_hT():
        return P.sbuf("hT", [128, 8, NT], BF16), [[Res(f"h{c}_{t}", dyn=True) for t in range(NTB)] for c in range(8)]
    P.dummy = P.sbuf("dummyb", [128, 4], F32)
    ones_bf = P.sbuf("ones_bf", [128, 128], BF16)
    ones256 = P.sbuf("ones256", [128, 128], F32)
    ident = P.sbuf("ident", [128, 128], F32)
    eps_t = P.sbuf("eps_t", [128, 1], F32)
    modT = P.sbuf("modT", [128, 2, 72, 2], F32)
    gsc = P.sbuf("gsc", [128, 2, 3, 8, 2], F32)
    gtv = P.sbuf("gtv", [128, 2, 3, 8, 2], F32)
    bmod = P.sbuf("bmod", [128, 2, 72], F32)
    normg = P.sbuf("normg", [128, 2, 3, 8], F32)
    finalg = P.sbuf("finalg", [128, 8], F32)
    bgate = P.sbuf("bgate", [128, 2, 24], F32)
    condT = P.sbuf("condT", [128, 8, 2], F32)
    condb = P.sbuf("condb", [128, 8, 2], BF16)
    h0t = P.sbuf("h0t", [128, 2, 16, 2, 2], F32)
    nsbuf = P.sbuf("nsbuf", [128, 2, 4, 16, 2, 2], F32)
    s5d = P.sbuf("s5d", [128, 2, 4], F32)
    convw = P.sbuf("convw", [128, 2, 2, 31], F32)
    convv = P.sbuf("convv", [128, 2, 3, 2], F32)
    maskE = P.sbuf("maskE", [128, 2], F32)
    mask2 = P.sbuf("mask2", [128, 2], F32)
    mask3 = P.sbuf("mask3", [128, 4, 8], F32)
    RC = Res("consts")
    Rmodl = [Res("mod0"), Res("mod1")]
    Rcond = Res("cond")
    Rns = Res("ns")
    banks = [P.psum(f"bank{i}", [128, 512], F32) for i in range(8)]
    RB = [Res(f"bank{i}") for i in range(8)]
    bank_ctr = [0]

    bank_reserved = [None]

    def bank():
        i = bank_ctr[0] % 8
        bank_ctr[0] += 1
        if i == bank_reserved[0]:
            i = bank_ctr[0] % 8
            bank_ctr[0] += 1
        return banks[i], RB[i]

    s_in = P.new_slot("s_in")
    s_out = P.new_slot("s_out")
    wslots = {}

    def wslot(name):
        if name not in wslots:
            wslots[name] = P.new_slot("ws_" + name)
        return wslots[name]

    P.op("pool", lambda e: e.memset(ones_bf[:], 1.0 / 1024.0), writes=[RC])
    P.op("pool", lambda e: e.memset(ones256[:], 1.0 / 256.0), writes=[RC])
    P.op("pool", lambda e: e.memset(eps_t[:], EPS), writes=[RC])
    P.op("pool", lambda e: e.memset(P.dummy[:], 0.0), writes=[RC])
    small = [(ident, "ident"), (bmod, "b_modT"), (normg, "norm_gT"), (finalg, "final_gT"), (bgate, "b_gateT"),
             (h0t, "h0"), (s5d, "s5_dT"), (convw, "conv_wT"), (convv, "conv_vT"), (maskE, "maskE"), (mask2, "mask2"),
             (mask3, "mask3")]
    for t, nm in small:
        P.dma("sp", t[:], I[nm], s_in, writes=[RC])
    P.dma("sp", condT[:], I["cond"].rearrange("(k p) c -> p k c", p=128), s_in, writes=[RC])
    for ct in range(8):
        P.dma("sp", xT[:, ct, :], I["xin"][ct * 128:(ct + 1) * 128, :], s_in, writes=RX[ct])
    P.scope_begin()
    ptmp = [P.sbuf(f"ptmp{i}", [128, 1024], F32) for i in range(2)]
    Rpt = [Res(f"ptmp{i}", dyn=True) for i in range(2)]
    for ct in range(8):
        b = ct % 2
        P.dma("sp", ptmp[b][:], I["pos"][ct * 128:(ct + 1) * 128, :], s_in, writes=[Rpt[b]])
        P.op("dve", lambda e, ct=ct, b=b: e.tensor_tensor(xT[:, ct, 1024:2048], xT[:, ct, 1024:2048], ptmp[b][:], ALU.add),
             reads=[Rpt[b]], writes=[RX[ct][2], RX[ct][3]])
    P.op("act", lambda e: e.activation(condb[:], condT[:], AF.Silu), reads=[RC], writes=[Rcond])

    P.scope_end()

    def mod_gen(l, wm, Rwm):
        bk, rb = bank()
        bank_reserved[0] = banks.index(bk)
        for ch in range(18):
            b = ch % 2
            P.dma("pool", wm[b][:], I["w_mod"][l, :, ch * 512:(ch + 1) * 512].rearrange("(k p) c -> p k c", p=128),
                  wslot(f"wm{b}"), writes=[Rwm[b]])
            for m in range(4):
                mt = ch * 4 + m
                for kt in range(8):
                    P.op("pe", lambda e, bk=bk, b=b, m=m, mt=mt, kt=kt: e.matmul(
                        bk[:, mt * 2:mt * 2 + 2], lhsT=wm[b][:, kt, m * 128:(m + 1) * 128], rhs=condb[:, kt, :],
                        start=(kt == 0), stop=(kt == 7)), reads=[Rwm[b], Rcond], writes=[rb])
            yield
        P.op("dve", lambda e, bk=bk, l=l: e.tensor_tensor(
            modT[:, l], bk[:, 0:144].rearrange("p (m c) -> p m c", c=2),
            CAP(bmod[:, l, :], 0, [[1, 72], [0, 2]]), ALU.add), reads=[rb, RC], writes=[Rmodl[l]])
        bank_reserved[0] = None
        for n in range(3):
            P.op("dve", lambda e, l=l, n=n: e.tensor_scalar_add(gsc[:, l, n], modT[:, l, (3 * n + 1) * 8:(3 * n + 2) * 8, :], 1.0),
                 reads=[Rmodl[l]], writes=[Rmodl[l]])
            P.op("dve", lambda e, l=l, n=n: e.tensor_tensor(gsc[:, l, n], gsc[:, l, n], CAP(normg[:, l, n, :], 0, [[1, 8], [0, 2]]), ALU.mult),
                 reads=[Rmodl[l], RC], writes=[Rmodl[l]])
            P.op("dve", lambda e, l=l, n=n: e.tensor_scalar_mul(gtv[:, l, n], modT[:, l, (3 * n + 2) * 8:(3 * n + 3) * 8, :],
                                                              0.5 if n != 1 else 1.0), reads=[Rmodl[l]], writes=[Rmodl[l]])

    def mod_bufs():
        wm = [P.sbuf(f"wm{i}", [128, 8, 512], BF16) for i in range(2)]
        Rwm = [Res(f"wm{i}", dyn=True) for i in range(2)]
        return wm, Rwm

    def mod_stage(l):
        P.scope_begin()
        for _ in mod_gen(l, *mod_bufs()):
            pass
        P.scope_end()

    MODG = {"gen": None}

    def mod_step():
        g = MODG["gen"]
        if g is not None:
            try:
                next(g)
            except StopIteration:
                MODG["gen"] = None

    def sh_ap(l, n, ct, c):
        return modT[:, l, (3 * n) * 8 + ct, c:c + 1]

    def dump(name, ap, shape, reads, dt=F32):
        d = nc.dram_tensor("dbg_" + name, list(shape), dt, kind="ExternalOutput").ap()
        dbg_out[name] = d
        P.dma("sp", d, ap, s_out, reads=reads)

    def norm_stage(l, n, hT, RH, own_scope=False):
        if own_scope:
            P.scope_begin()
        sq, Rsq, rs, Rrs, tmp, Rtmp = norm_scratch()
        for tb in range(NTB):
            c = 0 if tb < 2 else 1
            ts = slice(tb * TB, (tb + 1) * TB)
            for ct in range(8):
                P.op("act", lambda e, ct=ct, ts=ts: e.activation(sq[:, ct, :], xT[:, ct, ts], AF.Square),
                     reads=[RX[ct][tb]], writes=[Rsq[ct]])
            bk, rb = bank()
            for ct in range(8):
                P.op("pe", lambda e, bk=bk, ct=ct: e.matmul(bk[:, :], lhsT=ones_bf[:], rhs=sq[:, ct, :], start=(ct == 0), stop=(ct == 7)),
                     reads=[Rsq[ct], RC], writes=[rb])
            r = tb % 2
            P.op("act", lambda e, bk=bk, r=r: e.activation(rs[r][:], bk[:, :], AF.Sqrt, bias=eps_t[:, 0:1], scale=1.0),
                 reads=[rb, RC], writes=[Rrs[r]])
            P.op("dve", lambda e, r=r: e.reciprocal(rs[r][:], rs[r][:]), reads=[Rrs[r]], writes=[Rrs[r]])
            for ct in range(8):
                t = ct % 2
                P.op("dve", lambda e, ct=ct, ts=ts, t=t, r=r, c=c: e.scalar_tensor_tensor(
                    out=tmp[t][:], in0=xT[:, ct, ts], scalar=gsc[:, l, n, ct, c:c + 1], in1=rs[r][:], op0=ALU.mult, op1=ALU.mult),
                    reads=[RX[ct][tb], Rmodl[l], Rrs[r]], writes=[Rtmp[t]])
                P.op("act", lambda e, ct=ct, ts=ts, t=t, c=c: e.activation(
                    hT[:, ct, ts], tmp[t][:], AF.Identity, bias=sh_ap(l, n, ct, c), scale=1.0),
                    reads=[Rtmp[t], Rmodl[l]], writes=[RH[ct][tb]])
        if own_scope:
            P.scope_end()

    def norm_scratch():
        sq = P.sbuf("sq", [128, 8, TB], BF16)
        rs = [P.sbuf(f"rs{i}", [128, TB], F32) for i in range(2)]
        tmp = [P.sbuf(f"ntmp{i}", [128, TB], F32) for i in range(2)]
        return (sq, [Res(f"sq{i}", dyn=True) for i in range(8)], rs, [Res(f"rs{i}", dyn=True) for i in range(2)],
                tmp, [Res(f"ntmp{i}", dyn=True) for i in range(2)])

    def ffn_stage(l, w, n):
        P.scope_begin()
        hT, RH = alloc_hT()
        norm_stage(l, n, hT, RH)
        act = P.sbuf("act", [128, FC, NT], BF16)
        RA = [[Res(f"act{f}_{t}", dyn=True) for t in range(NTB)] for f in range(FC)]
        w1b = [P.sbuf(f"w1b{i}", [128, 2, 2, 8, 128], BF16) for i in range(2)]
        Rw1 = [Res(f"w1b{i}", dyn=True) for i in range(2)]
        w2b = P.sbuf("w2b", [128, FC, D], BF16)
        Rw2 = Res("w2b", dyn=True)
        sg = [P.sbuf(f"sg{i}", [128, TB], F32) for i in range(2)]
        Rsg = [Res(f"sg{i}", dyn=True) for i in range(2)]
        it = 0
        for chunk in range(2):
            P.dma("pool", w2b[:], I["ffn_w2"][l, w, chunk * FC * 128:(chunk + 1) * FC * 128, :].rearrange("(f p) d -> p f d", p=128),
                  wslot("w2b"), writes=[Rw2])
            for pair in range(6):
                nf = 2 if pair < 5 else 1
                b = it % 2
                it += 1
                f0 = chunk * FC + pair * 2
                P.dma("pool", w1b[b][:, 0:nf], I["w1t"][l, w, f0:f0 + nf].rearrange("f g p k c -> p f g k c"),
                      wslot(f"w1b{b}"), writes=[Rw1[b]])
                for fi in range(nf):
                    f = pair * 2 + fi
                    for tb in range(NTB):
                        ts = slice(tb * TB, (tb + 1) * TB)
                        pg, rg = bank()
                        pu, ru = bank()
                        for gu, (pb_, rb_) in enumerate(((pg, rg), (pu, ru))):
                            for kt in range(8):
                                P.op("pe", lambda e, pb_=pb_, b=b, fi=fi, gu=gu, kt=kt, ts=ts: e.matmul(
                                    pb_[:, :], lhsT=w1b[b][:, fi, gu, kt, :], rhs=hT[:, kt, ts], start=(kt == 0), stop=(kt == 7)),
                                    reads=[Rw1[b], RH[kt][tb]], writes=[rb_])
                        s = (f * NTB + tb) % 2
                        P.op("act", lambda e, pg=pg, s=s: e.activation(sg[s][:], pg[:, :], AF.Silu), reads=[rg], writes=[Rsg[s]])
                        P.op("dve", lambda e, pu=pu, s=s, f=f, ts=ts: e.tensor_tensor(act[:, f, ts], sg[s][:], pu[:, :], ALU.mult),
                             reads=[Rsg[s], ru], writes=[RA[f][tb]])
            for d in range(8):
                for tb in range(NTB):
                    c = 0 if tb < 2 else 1
                    ts = slice(tb * TB, (tb + 1) * TB)
                    po, ro = bank()
                    for f in range(FC):
                        P.op("pe", lambda e, po=po, f=f, d=d, ts=ts: e.matmul(
                            po[:, :], lhsT=w2b[:, f, d * 128:(d + 1) * 128], rhs=act[:, f, ts], start=(f == 0), stop=(f == FC - 1)),
                            reads=[Rw2, RA[f][tb]], writes=[ro])
                    P.op("dve", lambda e, po=po, d=d, ts=ts, c=c: e.scalar_tensor_tensor(
                        out=xT[:, d, ts], in0=po[:, :], scalar=gtv[:, l, n, d, c:c + 1], in1=xT[:, d, ts], op0=ALU.mult, op1=ALU.add),
                        reads=[ro, Rmodl[l]], writes=[RX[d][tb]])
        P.scope_end()

    def lam_q(pref, ar, ai, ldt, shape, R_in, outs, RT):
        P.scope_begin()
        T = dict(outs)
        for nm in ["dt", "th", "mg", "t1", "t2", "s", "c", "den", "nr"]:
            T[nm] = P.sbuf(pref + nm, [128] + list(shape), F32)
        ki = P.sbuf(pref + "ki", [128] + list(shape), I32)
        two_pi = 2.0 * math.pi

        def V(nm):
            return T[nm][:]

        P.op("act", lambda e: e.activation(V("dt"), ldt, AF.Exp), reads=[R_in], writes=[RT])
        P.op("dve", lambda e: e.tensor_tensor(V("th"), ai, V("dt"), ALU.mult), reads=[R_in, RT], writes=[RT])
        P.op("dve", lambda e: e.tensor_tensor(V("mg"), ar, V("dt"), ALU.mult), reads=[R_in, RT], writes=[RT])
        P.op("act", lambda e: e.activation(V("mg"), V("mg"), AF.Exp), reads=[RT], writes=[RT])
        for which, shift in (("s", 0.0), ("c", 0.25)):
            P.op("dve", lambda e, shift=shift: e.tensor_scalar(V("t1"), V("th"), 1.0 / two_pi, shift, ALU.mult, ALU.add),
                 reads=[RT], writes=[RT])
            P.op("dve", lambda e: e.tensor_copy(ki[:], V("t1")), reads=[RT], writes=[RT])
            P.op("dve", lambda e: e.tensor_copy(V("t2"), ki[:]), reads=[RT], writes=[RT])
            P.op("dve", lambda e: e.tensor_tensor(V("t1"), V("t1"), V("t2"), ALU.subtract), reads=[RT], writes=[RT])
            P.op("dve", lambda e: e.tensor_scalar(V("t1"), V("t1"), two_pi, 3.1415925, ALU.mult, ALU.min), reads=[RT], writes=[RT])
            P.op("dve", lambda e: e.tensor_scalar_max(V("t1"), V("t1"), -3.1415925), reads=[RT], writes=[RT])
            P.op("act", lambda e, which=which: e.activation(V(which), V("t1"), AF.Sin), reads=[RT], writes=[RT])
        P.op("dve", lambda e: e.tensor_tensor(V("lr"), V("mg"), V("c"), ALU.mult), reads=[RT], writes=[RT])
        P.op("dve", lambda e: e.tensor_tensor(V("li"), V("mg"), V("s"), ALU.mult), reads=[RT], writes=[RT])
        P.op("dve", lambda e: e.tensor_scalar_add(V("nr"), V("lr"), -1.0), reads=[RT], writes=[RT])
        P.op("dve", lambda e: e.tensor_tensor(V("den"), ar, ar, ALU.mult), reads=[R_in, RT], writes=[RT])
        P.op("dve", lambda e: e.tensor_tensor(V("t1"), ai, ai, ALU.mult), reads=[R_in, RT], writes=[RT])
        P.op("dve", lambda e: e.tensor_tensor(V("den"), V("den"), V("t1"), ALU.add), reads=[RT], writes=[RT])
        P.op("dve", lambda e: e.reciprocal(V("den"), V("den")), reads=[RT], writes=[RT])
        P.op("dve", lambda e: e.tensor_tensor(V("t1"), V("nr"), ar, ALU.mult), reads=[R_in, RT], writes=[RT])
        P.op("dve", lambda e: e.tensor_tensor(V("t2"), V("li"), ai, ALU.mult), reads=[R_in, RT], writes=[RT])
        P.op("dve", lambda e: e.tensor_tensor(V("t1"), V("t1"), V("t2"), ALU.add), reads=[RT], writes=[RT])
        P.op("dve", lambda e: e.tensor_tensor(V("qr"), V("t1"), V("den"), ALU.mult), reads=[RT], writes=[RT])
        P.op("dve", lambda e: e.tensor_tensor(V("t1"), V("li"), ar, ALU.mult), reads=[R_in, RT], writes=[RT])
        P.op("dve", lambda e: e.tensor_tensor(V("t2"), V("nr"), ai, ALU.mult), reads=[R_in, RT], writes=[RT])
        P.op("dve", lambda e: e.tensor_tensor(V("t1"), V("t1"), V("t2"), ALU.subtract), reads=[RT], writes=[RT])
        P.op("dve", lambda e: e.tensor_tensor(V("qi"), V("t1"), V("den"), ALU.mult), reads=[RT], writes=[RT])
        P.scope_end()

    def cmul(eng, out_r, out_i, ar, ai, br, bi, t1, t2, reads, writes):
        P.op(eng, lambda e: e.tensor_tensor(t1, ar, br, ALU.mult), reads=reads, writes=writes)
        P.op(eng, lambda e: e.tensor_tensor(t2, ai, bi, ALU.mult), reads=reads, writes=writes)
        P.op(eng, lambda e: e.tensor_tensor(t1, t1, t2, ALU.subtract), reads=reads, writes=writes)
        P.op(eng, lambda e: e.tensor_tensor(t2, ar, bi, ALU.mult), reads=reads, writes=writes)
        P.op(eng, lambda e: e.tensor_tensor(out_i, ai, br, ALU.mult), reads=reads, writes=writes)
        P.op(eng, lambda e: e.tensor_tensor(out_i, out_i, t2, ALU.add), reads=reads, writes=writes)
        P.op(eng, lambda e: e.tensor_copy(out_r, t1), reads=reads, writes=writes)

    def cmul6(eng, out_r, out_i, ar, ai, br, bi, t1, t2, reads, writes):
        P.op(eng, lambda e: e.tensor_tensor(t1, ar, br, ALU.mult), reads=reads, writes=writes)
        P.op(eng, lambda e: e.tensor_tensor(t2, ai, bi, ALU.mult), reads=reads, writes=writes)
        P.op(eng, lambda e: e.tensor_tensor(out_r, t1, t2, ALU.subtract), reads=reads, writes=writes)
        P.op(eng, lambda e: e.tensor_tensor(t1, ar, bi, ALU.mult), reads=reads, writes=writes)
        P.op(eng, lambda e: e.tensor_tensor(t2, ai, br, ALU.mult), reads=reads, writes=writes)
        P.op(eng, lambda e: e.tensor_tensor(out_i, t1, t2, ALU.add), reads=reads, writes=writes)

    class _Stop(Exception):
        pass

    def ck(name, l):
        if stop == f"{name}{l}":
            raise _Stop()

    def mixer_stage(l):
        depth = len(P._scopes)
        try:
            _mixer(l)
            return False
        except _Stop:
            while len(P._scopes) > depth:
                P.scope_end()
            return True

    def _mixer(l):
        P.scope_begin()
        y_a = P.sbuf("y_a", [128, 4, NT], BF16)
        RYA = [[Res(f"ya{c}_{t}", dyn=True) for t in range(NTB)] for c in range(4)]
        P.scope_begin()
        U = P.sbuf("U", [128, 4, NT], BF16)
        RU = [[Res(f"U{c}_{t}", dyn=True) for t in range(NTB)] for c in range(4)]
        yg, RYG = U, RU
        P.scope_begin()
        hT, RH = alloc_hT()
        wina = P.sbuf("wina", [128, 8, 512], BF16)
        Rwina = Res("wina", dyn=True)
        P.dma("pool", wina[:], I["w_in"][l, :, 0:512].rearrange("(k p) c -> p k c", p=128), wslot("wina"), writes=[Rwina])
        norm_stage(l, 1, hT, RH)
        for ct in range(4):
            for tb in range(NTB):
                ts = slice(tb * TB, (tb + 1) * TB)
                bk, rb = bank()
                for kt in range(8):
                    P.op("pe", lambda e, bk=bk, ct=ct, kt=kt, ts=ts: e.matmul(
                        bk[:, :], lhsT=wina[:, kt, ct * 128:(ct + 1) * 128], rhs=hT[:, kt, ts], start=(kt == 0), stop=(kt == 7)),
                        reads=[Rwina, RH[kt][tb]], writes=[rb])
                P.op("act", lambda e, bk=bk, ct=ct, ts=ts: e.activation(U[:, ct, ts], bk[:, :], AF.Copy), reads=[rb], writes=[RU[ct][tb]])
        P.scope_end()
        if "za" in dbg:
            P.scope_begin()
            for ct in range(4):
                tmpd = P.sbuf(f"dbgza{ct}", [128, NT], F32)
                Rd = Res(f"dbgza{ct}", dyn=True)
                P.op("dve", lambda e, ct=ct, tmpd=tmpd: e.tensor_copy(tmpd[:], U[:, ct, :]), reads=RU[ct], writes=[Rd])
                dump(f"za{ct}", tmpd[:], [128, NT], [Rd])
            P.scope_end()
        ck("m_u", l)
        asl = P.sbuf("asl", [128, 2, 16, 2], F32)
        ldsl = P.sbuf("ldsl", [128, 16, 2], F32)
        csl = P.sbuf("csl", [128, 2, 16, 2, 16], F32)
        Rpar = Res("s5par", dyn=True)
        TS = {nm: P.sbuf("sl_" + nm, [128, 16, 2], F32) for nm in ("lr", "li", "qr", "qi")}
        RTS = Res("sl_tab", dyn=True)
        TC = {nm: P.sbuf("cl_" + nm, [128, 4, 2, 64], F32) for nm in ("lr", "li")}
        RTC = Res("cl_tab", dyn=True)
        Bc = [P.sbuf(f"Bc{i}", [128, 4, 2, 64], F32) for i in range(2)]
        Bs = [P.sbuf(f"Bs{i}", [128, 16, 2, 16], F32) for i in range(2)]
        L4c = [P.sbuf(f"L4c{i}", [128, 4, 2, 64], F32) for i in range(2)]
        RBb = Res("Bbar", dyn=True)
        P.scope_begin()
        acl = P.sbuf("acl", [128, 2, 4, 2, 64], F32)
        ldcl = P.sbuf("ldcl", [128, 4, 2, 64], F32)
        bcl = P.sbuf("bcl", [128, 2, 4, 2, 64], F32)
        bsl = P.sbuf("bsl", [128, 2, 16, 2, 16], F32)
        for t, nm in ((asl, "a_sl"), (ldsl, "ldt_sl"), (acl, "a_cl"), (ldcl, "ldt_cl"), (bcl, "b_cl"), (bsl, "b_sl"), (csl, "c_sl")):
            P.dma("sp", t[:], I[nm][:, l], s_in, writes=[Rpar])
        TCq = dict(TC)
        TCq["qr"] = P.sbuf("cl_qr", [128, 4, 2, 64], F32)
        TCq["qi"] = P.sbuf("cl_qi", [128, 4, 2, 64], F32)
        tcl = [P.sbuf(f"tcl{i}", [128, 4, 2, 64], F32) for i in range(2)]
        tsl = [P.sbuf(f"tsl{i}", [128, 16, 2, 16], F32) for i in range(2)]
        lam_q("sl_", asl[:, 0], asl[:, 1], ldsl[:], [16, 2], Rpar, TS, RTS)
        lam_q("cl_", acl[:, 0], acl[:, 1], ldcl[:], [4, 2, 64], Rpar, TCq, RTC)
        cmul("dve", Bc[0][:], Bc[1][:], TCq["qr"][:], TCq["qi"][:], bcl[:, 0], bcl[:, 1], tcl[0][:], tcl[1][:],
             [RTC, Rpar, RBb], [RBb])
        L2c = [P.sbuf(f"L2c{i}", [128, 4, 2, 64], F32) for i in range(2)]
        cmul6("dve", L2c[0][:], L2c[1][:], TC["lr"][:], TC["li"][:], TC["lr"][:], TC["li"][:], tcl[0][:], tcl[1][:], [RTC], [RTC])
        cmul6("dve", L4c[0][:], L4c[1][:], L2c[0][:], L2c[1][:], L2c[0][:], L2c[1][:], tcl[0][:], tcl[1][:], [RTC], [RTC])
        qrb = CAP(TS["qr"][:], 0, [[2, 16], [1, 2], [0, 16]])
        qib = CAP(TS["qi"][:], 0, [[2, 16], [1, 2], [0, 16]])
        cmul("dve", Bs[0][:], Bs[1][:], qrb, qib, bsl[:, 0], bsl[:, 1], tsl[0][:], tsl[1][:], [RTS, Rpar, RBb], [RBb])
        P.scope_end()
        L2 = [P.sbuf(f"L2{i}", [128, 16, 2], F32) for i in range(2)]
        L4 = [P.sbuf(f"L4{i}", [128, 16, 2], F32) for i in range(2)]
        L8 = [P.sbuf(f"L8{i}", [128, 16, 2], F32) for i in range(2)]
        t8 = [P.sbuf(f"t8{i}", [128, 4, 16, 2], F32) for i in range(2)]
        RL8 = Res("L8", dyn=True)
        cmul6("dve", L2[0][:], L2[1][:], TS["lr"][:], TS["li"][:], TS["lr"][:], TS["li"][:], t8[0][:, 0], t8[1][:, 0], [RTS, RL8], [RL8])
        cmul6("dve", L4[0][:], L4[1][:], L2[0][:], L2[1][:], L2[0][:], L2[1][:], t8[0][:, 0], t8[1][:, 0], [RL8], [RL8])
        cmul6("dve", L8[0][:], L8[1][:], L4[0][:], L4[1][:], L4[0][:], L4[1][:], t8[0][:, 0], t8[1][:, 0], [RL8], [RL8])
        LPs = [P.sbuf(f"LPs{i}", [128, 9, 16, 2], F32) for i in range(2)]
        P.op("dve", lambda e: e.memset(LPs[0][:, 0], 1.0), writes=[RL8])
        P.op("dve", lambda e: e.memset(LPs[1][:, 0], 0.0), writes=[RL8])
        P.op("dve", lambda e: e.tensor_copy(LPs[0][:, 1], TS["lr"][:]), reads=[RTS], writes=[RL8])
        P.op("dve", lambda e: e.tensor_copy(LPs[1][:, 1], TS["li"][:]), reads=[RTS], writes=[RL8])
        cmul6("dve", LPs[0][:, 2:4], LPs[1][:, 2:4], LPs[0][:, 0:2], LPs[1][:, 0:2],
              CAP(L2[0][:], 0, [[0, 2], [2, 16], [1, 2]]), CAP(L2[1][:], 0, [[0, 2], [2, 16], [1, 2]]),
              t8[0][:, 0:2], t8[1][:, 0:2], [RL8], [RL8])
        cmul6("dve", LPs[0][:, 4:8], LPs[1][:, 4:8], LPs[0][:, 0:4], LPs[1][:, 0:4],
              CAP(L4[0][:], 0, [[0, 4], [2, 16], [1, 2]]), CAP(L4[1][:], 0, [[0, 4], [2, 16], [1, 2]]),
              t8[0][:], t8[1][:], [RL8], [RL8])
        P.op("dve", lambda e: e.tensor_copy(LPs[0][:, 8], L8[0][:]), reads=[RL8], writes=[RL8])
        P.op("dve", lambda e: e.tensor_copy(LPs[1][:, 8], L8[1][:]), reads=[RL8], writes=[RL8])
        A1 = P.sbuf("A1", [128, 2, 2, 16], F32)
        A2 = P.sbuf("A2", [128, 2, 2, 16], F32)
        L8v = [CAP(L8[i][:], 0, [[1, 2], [2, 16]]) for i in range(2)]
        for ri in range(2):
            P.op("dve", lambda e, ri=ri: e.tensor_copy(A1[:, ri], L8v[0]), reads=[RL8], writes=[RL8])
            P.op("dve", lambda e, ri=ri: e.tensor_scalar_mul(A2[:, ri], L8v[1], -1.0 if ri == 0 else 1.0), reads=[RL8], writes=[RL8])
        Lh = P.sbuf("Lh", [128, 2, 2, 16], F32)
        h0v = [CAP(h0t[:, l], ri, [[2, 2], [4, 16]]) for ri in range(2)]
        lt = [P.sbuf(f"lht{i}", [128, 2, 16], F32) for i in range(2)]
        cmul("dve", Lh[:, 0], Lh[:, 1], L8v[0], L8v[1], h0v[0], h0v[1], lt[0][:], lt[1][:], [RL8, RC], [RL8])
        h0b = P.sbuf("h0b", [128, 2, 2, 16], BF16)
        for ri in range(2):
            P.op("dve", lambda e, ri=ri: e.tensor_copy(h0b[:, ri], h0v[ri]), reads=[RC], writes=[RL8])

        ck("m_tab", l)
        P.scope_begin()
        Ec = [P.sbuf(f"Ec{i}", [128, 8, 2, 64], F32) for i in range(2)]
        tmpf = [P.sbuf(f"tmpf{i}", [128, 576], F32) for i in range(2)]
        et = [tmpf[i][:, 0:512].rearrange("p (a b c) -> p a b c", a=4, b=2, c=64) for i in range(2)]
        gt = [tmpf[i][:, 0:576].rearrange("p (a b c) -> p a b c", a=9, b=4, c=16) for i in range(2)]
        RPw = Res("Ec", dyn=True)
        EFe = P.sbuf("EFe", [128, 4096], BF16)
        Ebuf = EFe[:].rearrange("p (i d r g q) -> p i d r g q", i=8, d=2, r=2, g=2, q=64)
        RE = Res("EFe", dyn=True)
        RG = RPw
        Gbb = [P.sbuf(f"Gb{i}", [128, 2, 4, 2, 9, 16], BF16) for i in range(2)]
        RGbb = [Res(f"Gb{i}", dyn=True) for i in range(2)]
        EFf = P.sbuf("EFf", [128, 4096], BF16)
        EF = EFf
        Fbuf = EFf[:].rearrange("p (a j d r g h) -> p a j d r g h", a=4, j=8, d=2, r=2, g=2, h=16)
        RF = Res("EFf", dyn=True)
        Bpad = P.sbuf("Bpad", [128, 4, 2, 2, 8, 16], BF16)
        RBp = Res("Bpad", dyn=True)
        Kblk = P.sbuf("Kblk", [128, 2, 8, 128], BF16)
        RK = Res("Kblk", dyn=True)
        XS = P.sbuf("XS", [128, 2, 2, 4, 9, 32], F32)
        RXd = [Res("XS0", dyn=True), Res("XS1", dyn=True)]
        st1 = P.sbuf("st1", [128, 2, 2, 4, 9], F32)
        st2 = P.sbuf("st2", [128, 2, 2, 4, 9], F32)
        fx1 = P.sbuf("fx1", [128, 2, 4, 32], F32)
        fx2 = P.sbuf("fx2", [128, 2, 4, 32], F32)
        cs = P.sbuf("cs", [128, 2, 4], F32)
        Sin = P.sbuf("Sin", [128, 2, 2, 4, 256], BF16)
        RSin = Res("Sin", dyn=True)
        XSZ = 2 * 4 * 9 * 32
        SEA = "pool"

        def seg_A1(ct):
            Gb, RGb = Gbb[ct % 2], RGbb[ct % 2]
            for ri in range(2):
                P.op(SEA, lambda e, ri=ri, ct=ct: e.tensor_copy(Ec[ri][:, 0], Bc[ri][:, ct]), reads=[RBb], writes=[RPw])
            for k in range(1, 4):
                cmul6(SEA, Ec[0][:, k], Ec[1][:, k], Ec[0][:, k - 1], Ec[1][:, k - 1], TC["lr"][:, ct], TC["li"][:, ct],
                      et[0][:, 0], et[1][:, 0], [RPw, RTC], [RPw])
            cmul6(SEA, Ec[0][:, 4:8], Ec[1][:, 4:8], Ec[0][:, 0:4], Ec[1][:, 0:4],
                  CAP(L4c[0][:, ct], 0, [[0, 4], [64, 2], [1, 64]]), CAP(L4c[1][:, ct], 0, [[0, 4], [64, 2], [1, 64]]),
                  et[0][:], et[1][:], [RPw, RTC], [RPw])
            for dr in range(2):
                for ri in range(2):
                    o_ap = CAP(EFe[:, 0:1], (7 * 512 if dr == 0 else 0) + dr * 256 + ri * 128,
                               [[-512 if dr == 0 else 512, 8], [64, 2], [1, 64]])
                    P.op(SEA, lambda e, o_ap=o_ap, dr=dr, ri=ri: e.tensor_tensor(
                        o_ap, CAP(Ec[ri][:, 0, dr, 0:1], 0, [[128, 8], [0, 2], [1, 64]]),
                        CAP(maskE[:], 0, [[0, 8], [1, 2], [0, 64]]), ALU.mult), reads=[RPw, RC], writes=[RE])

        def seg_A2(ct):
            Gb, RGb = Gbb[ct % 2], RGbb[ct % 2]
            for dr in range(2):
                Cr = CAP(csl[:, 0, ct * 4, dr, 0:1], 0, [[0, 9], [32, 4], [1, 16]])
                Ci = CAP(csl[:, 1, ct * 4, dr, 0:1], 0, [[0, 9], [32, 4], [1, 16]])
                Pr = CAP(LPs[0][:, 0, ct * 4, dr:dr + 1], 0, [[32, 9], [2, 4], [0, 16]])
                Pi = CAP(LPs[1][:, 0, ct * 4, dr:dr + 1], 0, [[32, 9], [2, 4], [0, 16]])
                o_r = CAP(Gb[:, 0, 0, dr, 0, 0:1], 0, [[16, 9], [288, 4], [1, 16]])
                o_i = CAP(Gb[:, 1, 0, dr, 0, 0:1], 0, [[16, 9], [288, 4], [1, 16]])
                rr, ww = [RG, RL8, Rpar], [RG]
                P.op(SE, lambda e, Cr=Cr, Pr=Pr: e.tensor_tensor(gt[0][:], Cr, Pr, ALU.mult), reads=rr, writes=ww)
                P.op(SE, lambda e, Ci=Ci, Pi=Pi: e.tensor_tensor(gt[1][:], Ci, Pi, ALU.mult), reads=rr, writes=ww)
                P.op(SE, lambda e, o_r=o_r: e.tensor_tensor(o_r, gt[0][:], gt[1][:], ALU.subtract), reads=[RG], writes=[RG, RGb])
                P.op(SE, lambda e, Cr=Cr, Pi=Pi: e.tensor_tensor(gt[0][:], Cr, Pi, ALU.mult), reads=rr, writes=ww)
                P.op(SE, lambda e, Ci=Ci, Pr=Pr: e.tensor_tensor(gt[1][:], Ci, Pr, ALU.mult), reads=rr, writes=ww)
                P.op(SE, lambda e: e.tensor_tensor(gt[0][:], gt[0][:], gt[1][:], ALU.add), reads=[RG], writes=[RG])
                P.op(SE, lambda e, o_i=o_i: e.tensor_scalar_mul(o_i, gt[0][:], -1.0), reads=[RG], writes=[RG, RGb])
            for gpl in range(4):
                for dr in range(2):
                    bk, rb = bank()
                    for ri in range(2):
                        for i in range(8):
                            P.op("pe", lambda e, bk=bk, gpl=gpl, dr=dr, ri=ri, i=i, ct=ct: e.matmul(
                                bk[:, ri * 256:(ri + 1) * 256],
                                lhsT=Ebuf[32 * gpl:32 * gpl + 32, i, dr, ri].rearrange("p a b -> p (a b)"),
                                rhs=CAP(U[32 * gpl:32 * gpl + 32, ct, 0:1], i, [[8, 256]]),
                                start=(i == 0), stop=(i == 7), tile_position=(32 * gpl, 0)),
                                reads=[RE] + RU[ct], writes=[rb])
                    P.op("act", lambda e, bk=bk, gpl=gpl, dr=dr: e.activation(
                        CAP(XS[:, 0, dr, gpl, 0, 0:1], 0, [[XSZ, 2], [1, 256]]),
                        bk[:, :].rearrange("p (r c) -> p r c", r=2), AF.Copy), reads=[rb], writes=[RXd[dr]])
            P.op(SE, lambda e: e.memset(XS[:, :, :, :, 8, :], 0.0), writes=RXd)
            for ri in range(2):
                for dr in range(2):
                    kk = 0 if dr == 0 else 31
                    P.op(SE, lambda e, ri=ri, dr=dr, kk=kk, ct=ct: e.tensor_copy(
                        XS[:, ri, dr, :, 8, kk], CAP(L8[ri][:, ct * 4, dr:dr + 1], 0, [[2, 4]])), reads=[RL8], writes=[RXd[dr]])
            for dr in range(2):
                sgm, kk = (4, 0) if dr == 0 else (7, 31)
                P.op("dve", lambda e, dr=dr, sgm=sgm, kk=kk, ct=ct: e.tensor_tensor(
                    XS[:, :, dr, :, sgm, kk], XS[:, :, dr, :, sgm, kk], Lh[:, :, dr, ct * 4:ct * 4 + 4], ALU.add),
                    reads=[RXd[dr], RL8], writes=[RXd[dr]])

        def seg_S(ct):
            dstride = 4 * 9 * 32
            for k in range(1, 32):
                cur_ap = CAP(XS[:, 0, 0, 0, 0, 0:1], k, [[XSZ, 2], [dstride + 31 - 2 * k, 2], [288, 4], [32, 9]])
                prv_ap = CAP(XS[:, 0, 0, 0, 0, 0:1], k - 1, [[XSZ, 2], [dstride + 33 - 2 * k, 2], [288, 4], [32, 9]])
                prv_sw = CAP(XS[:, 0, 0, 0, 0, 0:1], XSZ + k - 1, [[-XSZ, 2], [dstride + 33 - 2 * k, 2], [288, 4], [32, 9]])
                a1 = CAP(A1[:, 0, 0, ct * 4:ct * 4 + 1], 0, [[32, 2], [16, 2], [1, 4], [0, 9]])
                a2 = CAP(A2[:, 0, 0, ct * 4:ct * 4 + 1], 0, [[32, 2], [16, 2], [1, 4], [0, 9]])
                P.op("dve", lambda e, prv_ap=prv_ap, a1=a1: e.tensor_tensor(st1[:], prv_ap, a1, ALU.mult), reads=RXd + [RL8], writes=RXd)
                P.op("dve", lambda e, prv_sw=prv_sw, a2=a2: e.tensor_tensor(st2[:], prv_sw, a2, ALU.mult), reads=RXd + [RL8], writes=RXd)
                P.op("dve", lambda e, cur_ap=cur_ap: e.tensor_tensor(cur_ap, cur_ap, st1[:], ALU.add), reads=RXd, writes=RXd)
                P.op("dve", lambda e, cur_ap=cur_ap: e.tensor_tensor(cur_ap, cur_ap, st2[:], ALU.add), reads=RXd, writes=RXd)
            for dr in range(2):
                order = (5, 6, 7) if dr == 0 else (6, 5, 4)
                for sgm in order:
                    src_seg, src_k = (sgm - 1, 31) if dr == 0 else (sgm + 1, 0)
                    P.op("dve", lambda e, dr=dr, src_seg=src_seg, src_k=src_k: e.tensor_scalar_mul(
                        cs[:, 0], XS[:, 1, dr, :, src_seg, src_k], -1.0), reads=[RXd[dr]], writes=[RXd[dr]])
                    P.op("dve", lambda e, dr=dr, src_seg=src_seg, src_k=src_k: e.tensor_copy(
                        cs[:, 1], XS[:, 1, dr, :, src_seg, src_k]), reads=[RXd[dr]], writes=[RXd[dr]])
                    pw_ap = CAP(XS[:, 0, dr, 0, 8, 0:1], 0, [[XSZ, 2], [288, 4], [1, 32]])
                    pw_sw = CAP(XS[:, 0, dr, 0, 8, 0:1], XSZ, [[-XSZ, 2], [288, 4], [1, 32]])
                    tgt = CAP(XS[:, 0, dr, 0, sgm, 0:1], 0, [[XSZ, 2], [288, 4], [1, 32]])
                    c_r = CAP(XS[:, 0, dr, 0, src_seg, src_k:src_k + 1], 0, [[0, 2], [288, 4], [0, 32]])
                    c_s = CAP(cs[:, 0, 0:1], 0, [[4, 2], [1, 4], [0, 32]])
                    P.op("dve", lambda e, pw_ap=pw_ap, c_r=c_r: e.tensor_tensor(fx1[:], pw_ap, c_r, ALU.mult), reads=[RXd[dr]], writes=[RXd[dr]])
                    P.op("dve", lambda e, pw_sw=pw_sw, c_s=c_s: e.tensor_tensor(fx2[:], pw_sw, c_s, ALU.mult), reads=[RXd[dr]], writes=[RXd[dr]])
                    P.op("dve", lambda e, tgt=tgt: e.tensor_tensor(tgt, tgt, fx1[:], ALU.add), reads=[RXd[dr]], writes=[RXd[dr]])
                    P.op("dve", lambda e, tgt=tgt: e.tensor_tensor(tgt, tgt, fx2[:], ALU.add), reads=[RXd[dr]], writes=[RXd[dr]])
            for dr in range(2):
                kk = 31 if dr == 0 else 0
                P.op("dve", lambda e, dr=dr, kk=kk, ct=ct: e.tensor_copy(
                    CAP(nsbuf[:, l, 0, ct * 4, dr, 0:1], 0, [[1, 2], [4, 4], [64, 4]]),
                    CAP(XS[:, 0, dr, 0, 0, kk:kk + 1], 0, [[XSZ, 2], [288, 4], [32, 4]])), reads=[RXd[dr]], writes=[Rns])

        def seg_Z1(ct):
            P.op(SE, lambda e: e.memset(Sin[:], 0.0), writes=[RSin])
            P.op("dve", lambda e: e.tensor_copy(
                CAP(Sin[:, 0, 0, 0, 0:1], 1, [[2048, 2], [256, 4], [32, 4], [1, 31]]),
                CAP(XS[:, 0, 0, 0, 0, 0:1], 0, [[XSZ, 2], [288, 4], [32, 4], [1, 31]])), reads=[RXd[0]], writes=[RSin])
            P.op("dve", lambda e: e.tensor_copy(
                CAP(Sin[:, 0, 0, 0, 0:1], 129, [[2048, 2], [256, 4], [1, 127]]),
                CAP(XS[:, 0, 0, 0, 4, 0:1], 0, [[XSZ, 2], [288, 4], [1, 127]])), reads=[RXd[0]], writes=[RSin])
            P.op("dve", lambda e: e.tensor_copy(
                CAP(Sin[:, 0, 1, 0, 0:1], 0, [[2048, 2], [256, 4], [32, 4], [1, 31]]),
                CAP(XS[:, 0, 1, 0, 0, 0:1], 1, [[XSZ, 2], [288, 4], [32, 4], [1, 31]])), reads=[RXd[1]], writes=[RSin])
            P.op("dve", lambda e: e.tensor_copy(
                CAP(Sin[:, 0, 1, 0, 0:1], 128, [[2048, 2], [256, 4], [1, 127]]),
                CAP(XS[:, 0, 1, 0, 4, 0:1], 1, [[XSZ, 2], [288, 4], [1, 127]])), reads=[RXd[1]], writes=[RSin])
            P.op("dve", lambda e, ct=ct: e.tensor_copy(Sin[:, :, 0, :, 128], h0b[:, :, 0, ct * 4:ct * 4 + 4]), reads=[RL8], writes=[RSin])
            P.op("dve", lambda e, ct=ct: e.tensor_copy(Sin[:, :, 1, :, 255], h0b[:, :, 1, ct * 4:ct * 4 + 4]), reads=[RL8], writes=[RSin])

        def seg_Z2(ct):
            Gb, RGb = Gbb[ct % 2], RGbb[ct % 2]
            for dr in range(2):
                for ri in range(2):
                    P.op(SE, lambda e, dr=dr, ri=ri, ct=ct: e.tensor_tensor(
                        Bpad[:, :, dr, ri],
                        CAP(Bs[ri][:, ct * 4, dr, :], 0, [[32, 4], [0, 8], [1, 16]]),
                        CAP(mask3[:], 0, [[8, 4], [1, 8], [0, 16]]), ALU.mult), reads=[RBb, RC], writes=[RBp])
            for dr in range(2):
                for jh in range(2):
                    bks = [bank(), bank()]
                    for gl in range(8):
                        gpl, g2 = gl // 2, gl % 2
                        bk, rb = bks[g2]
                        for ri in range(2):
                            P.op("pe", lambda e, bk=bk, gl=gl, gpl=gpl, g2=g2, ri=ri, dr=dr, jh=jh: e.matmul(
                                CAP(bk[:, 0:1], gl * 16, [[128, 4], [1, 16]]),
                                lhsT=Bpad[64 * g2:64 * g2 + 64, gpl, dr, ri].rearrange("p a b -> p (a b)"),
                                rhs=Gb[64 * g2:64 * g2 + 64, ri, gpl, dr, jh * 4:jh * 4 + 4, :],
                                start=(ri == 0), stop=(ri == 1), tile_position=(64 * g2, 0)),
                                reads=[RBp, RGb], writes=[rb])
                    for g2 in range(2):
                        bk, rb = bks[g2]
                        P.op("act", lambda e, bk=bk, dr=dr, jh=jh, g2=g2: e.activation(
                            CAP(Kblk[:, dr, jh * 4, 0:1], g2 * 16, [[128, 4], [32, 4], [1, 16]]),
                            CAP(bk[:, 0:1], g2 * 16, [[128, 4], [32, 4], [1, 16]]), AF.Copy), reads=[rb], writes=[RK])
            P.op("dve", lambda e, ct=ct: e.scalar_tensor_tensor(
                out=Kblk[:, 0, 0, :], in0=ident[:], scalar=s5d[:, l, ct:ct + 1], in1=Kblk[:, 0, 0, :],
                op0=ALU.mult, op1=ALU.add), reads=[RK, RC], writes=[RK])
            for dr in range(2):
                for ri in range(2):
                    for gpl in range(4):
                        P.op(SE, lambda e, dr=dr, ri=ri, gpl=gpl: e.tensor_tensor(
                            CAP(EFf[:, 0:1], gpl * 1024 + dr * 64 + ri * 32, [[128, 8], [16, 2], [1, 16]]),
                            CAP(Gb[:, ri, gpl, dr, 1, 0:1], 0, [[16, 8], [0, 2], [1, 16]]),
                            CAP(mask2[:], 0, [[0, 8], [1, 2], [0, 16]]), ALU.mult), reads=[RGb, RC], writes=[RF])
            for tb in range(NTB):
                ts = slice(tb * TB, (tb + 1) * TB)
                bk, rb = bank()
                bkv = bk[:, :].rearrange("p (c i) -> p c i", i=8)
                uv = U[:, ct, ts].rearrange("p (c i) -> p c i", i=8)
                for dr in range(2):
                    for j in range(8):
                        if dr == 0:
                            o_ap, r_ap = bkv[:, :, j:8], uv[:, :, 0:8 - j]
                        else:
                            o_ap, r_ap = bkv[:, :, 0:8 - j], uv[:, :, j:8]
                        P.op("pe", lambda e, o_ap=o_ap, r_ap=r_ap, dr=dr, j=j: e.matmul(
                            o_ap, lhsT=Kblk[:, dr, j, :], rhs=r_ap, start=(dr == 0 and j == 0), stop=False),
                            reads=[RK, RU[ct][tb]], writes=[rb])
                last = (3, 7, 1, 1)
                for gpl in range(4):
                    for j in range(8):
                        for dr in range(2):
                            for ri in range(2):
                                pos = j if dr == 0 else 7 - j
                                P.op("pe", lambda e, bk=bk, gpl=gpl, j=j, dr=dr, ri=ri, pos=pos, tb=tb: e.matmul(
                                    CAP(bk[32 * gpl:32 * gpl + 32, 0:1], pos, [[8, 64]]),
                                    lhsT=Fbuf[:, gpl, j, dr, ri].rearrange("p a b -> p (a b)"),
                                    rhs=Sin[:, ri, dr, gpl, tb * 64:(tb + 1) * 64],
                                    start=False, stop=((gpl, j, dr, ri) == last), tile_position=(0, 32 * gpl)),
                                    reads=[RF, RSin], writes=[rb])
                P.op("act", lambda e, bk=bk, ct=ct, ts=ts: e.activation(yg[:, ct, ts], bk[:, :], GELU), reads=[rb], writes=[RYG[ct][tb]])

        seg_A1(0)
        seg_A2(0)
        for ct in range(4):
            if ct < 3:
                seg_A1(ct + 1)
            seg_S(ct)
            seg_Z1(ct)
            if ct < 3:
                seg_A2(ct + 1)
            seg_Z2(ct)
        P.scope_end()
        if "yg" in dbg:
            P.scope_begin()
            for ct in range(4):
                tmpd = P.sbuf(f"dbgyg{ct}", [128, NT], F32)
                Rd = Res(f"dbgyg{ct}", dyn=True)
                P.op("dve", lambda e, ct=ct, tmpd=tmpd: e.tensor_copy(tmpd[:], yg[:, ct, :]), reads=RYG[ct], writes=[Rd])
                dump(f"yg{ct}", tmpd[:], [128, NT], [Rd])
            P.scope_end()
        wglu = P.sbuf("wglu", [128, 4, 512], BF16)
        Rwglu = Res("wglu", dyn=True)
        P.dma("pool", wglu[:], I["s5_w_glu"][l].rearrange("(k p) c -> p k c", p=128), wslot("wglu"), writes=[Rwglu])
        sgl = [P.sbuf(f"sgl{i}", [128, TB], F32) for i in range(2)]
        Rsgl = [Res(f"sgl{i}", dyn=True) for i in range(2)]
        for tb in range(NTB):
            ts = slice(tb * TB, (tb + 1) * TB)
            for co in range(4):
                bk, rb = bank()
                for kt in range(4):
                    P.op("pe", lambda e, bk=bk, co=co, kt=kt, ts=ts: e.matmul(
                        bk[:, :], lhsT=wglu[:, kt, co * 128:(co + 1) * 128], rhs=yg[:, kt, ts], start=(kt == 0), stop=(kt == 3)),
                        reads=[Rwglu, RYG[kt][tb]], writes=[rb])
                s = co % 2
                P.op("act", lambda e, bk=bk, s=s: e.activation(sgl[s][:], bk[:, :], AF.Sigmoid), reads=[rb], writes=[Rsgl[s]])
                P.op("dve", lambda e, s=s, co=co, ts=ts: e.tensor_tensor(y_a[:, co, ts], yg[:, co, ts], sgl[s][:], ALU.mult),
                     reads=[Rsgl[s], RYG[co][tb]], writes=[RYA[co][tb]])
        P.scope_end()
        ck("m_s5", l)

        y_b = P.sbuf("y_b", [128, 2, NT], BF16)
        RYB = [[Res(f"yb{c}_{t}", dyn=True) for t in range(NTB)] for c in range(2)]
        y_c = P.sbuf("y_c", [128, 2, NT], BF16)
        RYC = [[Res(f"yc{c}_{t}", dyn=True) for t in range(NTB)] for c in range(2)]
        hT, RH = alloc_hT()
        norm_stage(l, 1, hT, RH, own_scope=True)
        P.scope_begin()
        gpad = P.sbuf("gpad", [128, 2, GP_LEN], BF16)
        Rgp = [[Res(f"gp{j}_{t}", dyn=True) for t in range(NTB)] for j in range(2)]
        u_c = P.sbuf("u_c", [128, 2, NT], BF16)
        RUC = [[Res(f"uc{j}_{t}", dyn=True) for t in range(NTB)] for j in range(2)]
        vnb = P.sbuf("vnb", [128, 16, 256], BF16)
        RV = [Res(f"vn{t}", dyn=True) for t in range(16)]
        P.op("pool", lambda e: e.memset(gpad[:], 0.0), writes=[Rgp[j][t] for j in range(2) for t in range(NTB)])
        if l == 0:
            MODG["gen"] = mod_gen(1, *mod_bufs())
        P.scope_begin()
        winb = P.sbuf("winb", [128, 8, 1024], BF16)
        Rwinb = Res("winb", dyn=True)
        P.dma("pool", winb[:, :, 0:512], I["w_in"][l, :, 512:1024].rearrange("(k p) c -> p k c", p=128), wslot("winb"), writes=[Rwinb])
        P.dma("pool", winb[:, :, 512:1024], I["w_in"][l, :, 1024:1536].rearrange("(k p) c -> p k c", p=128), wslot("winb"), writes=[Rwinb])
        sgln = P.sbuf("sgln", [128, 2, 256], F32)
        Rsgl_ = Res("sgln", dyn=True)
        P.dma("sp", sgln[:].rearrange("p a b -> p (a b)"), I["sg_ln"][l].rearrange("a b -> (a b)").partition_broadcast(128), s_in, writes=[Rsgl_])
        sb = [P.sbuf(f"sb{i}", [128, TB], F32) for i in range(2)]
        Rsb = [Res(f"sb{i}", dyn=True) for i in range(2)]

        def gp_ap(j, tb, k):
            if tb < 2:
                return CAP(gpad[:, j, 0:1], GP_OFF[2 * tb] + k, [[286, 2], [1, 256]])
            return CAP(gpad[:, j, 0:1], GP_OFF[4] + (tb - 2) * 512 + k, [[1, 512]])

        for j in range(2):
            for tb in range(NTB):
                ts = slice(tb * TB, (tb + 1) * TB)
                pa, ra = bank()
                pb2, rb2 = bank()
                for which, (pp, rr) in enumerate(((pa, ra), (pb2, rb2))):
                    c0 = which * 256 + j * 128
                    for kt in range(8):
                        P.op("pe", lambda e, pp=pp, c0=c0, kt=kt, ts=ts: e.matmul(
                            pp[:, :], lhsT=winb[:, kt, c0:c0 + 128], rhs=hT[:, kt, ts], start=(kt == 0), stop=(kt == 7)),
                            reads=[Rwinb, RH[kt][tb]], writes=[rr])
                mod_step()
                s = tb % 2
                P.op("act", lambda e, pb2=pb2, s=s: e.activation(sb[s][:], pb2[:, :], AF.Sigmoid), reads=[rb2], writes=[Rsb[s]])
                o_ap = gp_ap(j, tb, 15)
                i_ap = pa[:, :].rearrange("p (s t) -> p s t", s=2) if tb < 2 else pa[:, :]
                s_ap = sb[s][:].rearrange("p (s t) -> p s t", s=2) if tb < 2 else sb[s][:]
                P.op("dve", lambda e, o_ap=o_ap, i_ap=i_ap, s_ap=s_ap: e.tensor_tensor(o_ap, i_ap, s_ap, ALU.mult),
                     reads=[ra, Rsb[s]], writes=[Rgp[j][tb]])
        for j in range(2):
            for tb in range(NTB):
                ts = slice(tb * TB, (tb + 1) * TB)
                bk, rb = bank()
                c0 = 512 + j * 128
                for kt in range(8):
                    P.op("pe", lambda e, bk=bk, c0=c0, kt=kt, ts=ts: e.matmul(
                        bk[:, :], lhsT=winb[:, kt, c0:c0 + 128], rhs=hT[:, kt, ts], start=(kt == 0), stop=(kt == 7)),
                        reads=[Rwinb, RH[kt][tb]], writes=[rb])
                mod_step()
                P.op("act", lambda e, bk=bk, j=j, ts=ts: e.activation(u_c[:, j, ts], bk[:, :], GELU), reads=[rb], writes=[RUC[j][tb]])
        vg = [P.sbuf(f"vg{i}", [128, 256], F32) for i in range(2)]
        Rvg = [Res(f"vg{i}", dyn=True) for i in range(2)]
        bst = [P.sbuf(f"bst{i}", [128, 6], F32) for i in range(2)]
        bag = [P.sbuf(f"bag{i}", [128, 2], F32) for i in range(2)]
        for tt in range(16):
            tb = tt // 4
            tks = slice(tt * 128, (tt + 1) * 128)
            bk, rb = bank()
            for kt in range(8):
                P.op("pe", lambda e, bk=bk, kt=kt, tks=tks: e.matmul(
                    bk[:, 0:256], lhsT=hT[:, kt, tks], rhs=winb[:, kt, 768:1024], start=(kt == 0), stop=(kt == 7)),
                    reads=[Rwinb, RH[kt][tb]], writes=[rb])
            mod_step()
            s = tt % 2
            P.op("act", lambda e, bk=bk, s=s: e.activation(vg[s][:], bk[:, 0:256], GELU), reads=[rb], writes=[Rvg[s]])
            P.op("dve", lambda e, s=s: e.bn_stats(bst[s][:], vg[s][:]), reads=[Rvg[s]], writes=[Rvg[s]])
            P.op("dve", lambda e, s=s: e.bn_aggr(bag[s][:], bst[s][:]), reads=[Rvg[s]], writes=[Rvg[s]])
            P.op("act", lambda e, s=s: e.activation(bag[s][:, 1:2], bag[s][:, 1:2], AF.Sqrt, bias=eps_t[:, 0:1], scale=1.0),
                 reads=[Rvg[s], RC], writes=[Rvg[s]])
            P.op("dve", lambda e, s=s: e.reciprocal(bag[s][:, 1:2], bag[s][:, 1:2]), reads=[Rvg[s]], writes=[Rvg[s]])
            P.op("dve", lambda e, s=s: e.tensor_scalar(vg[s][:], vg[s][:], bag[s][:, 0:1], bag[s][:, 1:2], ALU.subtract, ALU.mult),
                 reads=[Rvg[s]], writes=[Rvg[s]])
            P.op("dve", lambda e, s=s: e.tensor_tensor(vg[s][:], vg[s][:], sgln[:, 0, :], ALU.mult), reads=[Rvg[s], Rsgl_], writes=[Rvg[s]])
            P.op("dve", lambda e, s=s, tt=tt: e.tensor_tensor(vnb[:, tt, :], vg[s][:], sgln[:, 1, :], ALU.add),
                 reads=[Rvg[s], Rsgl_], writes=[RV[tt]])
        P.scope_end()
        ck("m_p1", l)
        if "gpad" in dbg:
            P.scope_begin()
            for j in range(2):
                tmpd = P.sbuf(f"dbggp{j}", [128, GP_LEN], F32)
                Rd = Res(f"dbggp{j}", dyn=True)
                P.op("dve", lambda e, j=j, tmpd=tmpd: e.tensor_copy(tmpd[:], gpad[:, j, :]), reads=Rgp[j], writes=[Rd])
                dump(f"gp{j}", tmpd[:], [128, GP_LEN], [Rd])
            P.scope_end()
        P.scope_begin()
        dg = P.sbuf("dg", [128, 2, 31, 128], BF16)
        Rdg = Res("dg", dyn=True)
        for j in range(2):
            for k in range(31):
                P.op("pool", lambda e, j=j, k=k: e.tensor_scalar_mul(dg[:, j, k, :], ident[:], convw[:, l, j, k:k + 1]),
                     reads=[RC], writes=[Rdg])
        ycf = P.sbuf("ycf", [128, 2, TB], F32)
        ysq = P.sbuf("ysq", [128, 2, TB], F32)
        Rycf = [Res(f"ycf{j}", dyn=True) for j in range(2)]
        Rysq = [Res(f"ysq{j}", dyn=True) for j in range(2)]
        mean_s = P.sbuf("mean_s", [128, TB], F32)
        var_s = P.sbuf("var_s", [128, TB], F32)
        Rms = Res("mean_s", dyn=True)
        Rvs = Res("var_s", dyn=True)
        dtm = [P.sbuf(f"dtm{i}", [128, TB], F32) for i in range(2)]
        Rdtm = [Res(f"dtm{i}", dyn=True) for i in range(2)]
        for tb in range(NTB):
            ts = slice(tb * TB, (tb + 1) * TB)
            for j in range(2):
                bk, rb = bank()
                o_ap = bk[:, :].rearrange("p (s t) -> p s t", s=2) if tb < 2 else bk[:, :]
                for k in range(31):
                    P.op("pe", lambda e, o_ap=o_ap, j=j, k=k, tb=tb: e.matmul(
                        o_ap, lhsT=dg[:, j, k, :], rhs=gp_ap(j, tb, k), start=(k == 0), stop=(k == 30)),
                        reads=[Rdg, Rgp[j][tb]] + ([Rgp[j][tb - 1]] if tb == 3 else []) + ([Rgp[j][tb + 1]] if tb == 2 else []),
                        writes=[rb])
                P.op("act", lambda e, bk=bk, j=j: e.activation(ycf[:, j, :], bk[:, :], AF.Identity, bias=convv[:, l, 0, j:j + 1], scale=1.0),
                     reads=[rb, RC], writes=[Rycf[j]])
                P.op("act", lambda e, bk=bk, j=j: e.activation(ysq[:, j, :], bk[:, :], AF.Square, bias=convv[:, l, 0, j:j + 1], scale=1.0),
                     reads=[rb, RC], writes=[Rysq[j]])
            bm, rbm = bank()
            bq, rbq = bank()
            for j in range(2):
                P.op("pe", lambda e, bm=bm, j=j: e.matmul(bm[:, :], lhsT=ones256[:], rhs=ycf[:, j, :], start=(j == 0), stop=(j == 1)),
                     reads=[Rycf[j], RC], writes=[rbm])
            for j in range(2):
                P.op("pe", lambda e, bq=bq, j=j: e.matmul(bq[:, :], lhsT=ones256[:], rhs=ysq[:, j, :], start=(j == 0), stop=(j == 1)),
                     reads=[Rysq[j], RC], writes=[rbq])
            P.op("act", lambda e, bm=bm: e.activation(mean_s[:], bm[:, :], AF.Copy), reads=[rbm], writes=[Rms])
            P.op("dve", lambda e: e.tensor_tensor(var_s[:], mean_s[:], mean_s[:], ALU.mult), reads=[Rms], writes=[Rvs])
            P.op("dve", lambda e, bq=bq: e.tensor_tensor(var_s[:], bq[:, :], var_s[:], ALU.subtract), reads=[rbq, Rvs], writes=[Rvs])
            P.op("act", lambda e: e.activation(var_s[:], var_s[:], AF.Sqrt, bias=eps_t[:, 0:1], scale=1.0), reads=[Rvs, RC], writes=[Rvs])
            P.op("dve", lambda e: e.reciprocal(var_s[:], var_s[:]), reads=[Rvs], writes=[Rvs])
            for j in range(2):
                P.op("dve", lambda e, j=j: e.tensor_tensor(dtm[j][:], ycf[:, j, :], mean_s[:], ALU.subtract), reads=[Rycf[j], Rms], writes=[Rdtm[j]])
                P.op("dve", lambda e, j=j: e.tensor_tensor(dtm[j][:], dtm[j][:], var_s[:], ALU.mult), reads=[Rdtm[j], Rvs], writes=[Rdtm[j]])
                P.op("act", lambda e, j=j, ts=ts: e.activation(y_b[:, j, ts], dtm[j][:], AF.Silu, bias=convv[:, l, 2, j:j + 1],
                                                           scale=convv[:, l, 1, j:j + 1]), reads=[Rdtm[j], RC], writes=[RYB[j][tb]])
        P.scope_end()
        ck("m_p2", l)
        P.scope_begin()
        sgw = P.sbuf("sgw", [128, 4, 128], BF16)
        sgb = P.sbuf("sgb", [128, 2, 128], F32)
        Rsgp = Res("sgpar", dyn=True)
        P.dma("pool", sgw[:], I["sg_wT"][:, l], wslot("sgw"), writes=[Rsgp])
        P.dma("sp", sgb[:], I["sg_bT"][:, l], s_in, writes=[Rsgp])
        stm = [P.sbuf(f"stm{i}", [128, TB], F32) for i in range(2)]
        Rstm = [Res(f"stm{i}", dyn=True) for i in range(2)]
        for hp in range(2):
            for tb in range(NTB):
                ts = slice(tb * TB, (tb + 1) * TB)
                bk, rb = bank()
                for n4 in range(4):
                    tt = tb * 4 + n4
                    for h2 in range(2):
                        h = hp * 2 + h2
                        P.op("pe", lambda e, bk=bk, n4=n4, tt=tt, h2=h2, h=h: e.matmul(
                            bk[64 * h2:64 * h2 + 64, n4 * 128:(n4 + 1) * 128], lhsT=vnb[:, tt, h * 64:(h + 1) * 64], rhs=sgw[:, h, :],
                            start=True, stop=True, tile_position=(0, 64 * h2)), reads=[RV[tt], Rsgp], writes=[rb])
                s = tb % 2
                P.op("dve", lambda e, bk=bk, s=s, hp=hp: e.tensor_tensor(
                    stm[s][:].rearrange("p (n q) -> p n q", n=4), bk[:, :].rearrange("p (n q) -> p n q", n=4),
                    CAP(sgb[:, hp, :], 0, [[0, 4], [1, 128]]), ALU.add), reads=[rb, Rsgp], writes=[Rstm[s]])
                P.op("dve", lambda e, s=s, hp=hp, ts=ts: e.tensor_tensor(y_c[:, hp, ts], stm[s][:], u_c[:, hp, ts], ALU.mult),
                     reads=[Rstm[s], RUC[hp][tb]], writes=[RYC[hp][tb]])
        P.scope_end()
        while MODG["gen"] is not None:
            mod_step()
        P.scope_end()
        ck("m_p3", l)

        if "ybc" in dbg:
            P.scope_begin()
            for nm, buf, RR, n in (("ya", y_a, RYA, 4), ("yb", y_b, RYB, 2), ("yc", y_c, RYC, 2)):
                for ct in range(n):
                    tmpd = P.sbuf(f"dbg{nm}{ct}", [128, NT], F32)
                    Rd = Res(f"dbg{nm}{ct}", dyn=True)
                    P.op("dve", lambda e, ct=ct, tmpd=tmpd, buf=buf: e.tensor_copy(tmpd[:], buf[:, ct, :]), reads=RR[ct], writes=[Rd])
                    dump(f"{nm}{ct}", tmpd[:], [128, NT], [Rd])
            P.scope_end()

        P.scope_begin()
        mg = P.sbuf("mg", [128, 8, NT], BF16)
        RM = [[Res(f"mg{c}_{t}", dyn=True) for t in range(NTB)] for c in range(8)]
        wg = [P.sbuf(f"wg{i}", [128, 3, 8, 128], BF16) for i in range(2)]
        Rwg = [Res(f"wg{i}", dyn=True) for i in range(2)]
        wbr = [P.sbuf(f"wbr{i}", [128, 8, 128], BF16) for i in range(2)]
        Rwbr = [Res(f"wbr{i}", dyn=True) for i in range(2)]
        wo = [P.sbuf(f"wo{i}", [128, 8, 128], BF16) for i in range(2)]
        Rwo = [Res(f"wo{i}", dyn=True) for i in range(2)]
        gs = [P.sbuf(f"gs{i}", [128, TB], F32) for i in range(3)]
        Rgs = [Res(f"gs{i}", dyn=True) for i in range(3)]
        mt = [P.sbuf(f"mt{i}", [128, TB], F32) for i in range(3)]
        Rmt = [Res(f"mt{i}", dyn=True) for i in range(3)]
        ybufs = [(y_a, RYA, 4, 0), (y_b, RYB, 2, 4), (y_c, RYC, 2, 6)]
        for d in range(8):
            b = d % 2
            P.dma("pool", wg[b][:], I["wgate_t"][l, d], wslot(f"wg{b}"), writes=[Rwg[b]])
            P.dma("pool", wbr[b][:], I["wbr_t"][l, d], wslot(f"wbr{b}"), writes=[Rwbr[b]])
            for tb in range(NTB):
                ts = slice(tb * TB, (tb + 1) * TB)
                for br in range(3):
                    bg_, rg_ = bank()
                    for kt in range(8):
                        P.op("pe", lambda e, bg_=bg_, b=b, br=br, kt=kt, ts=ts: e.matmul(
                            bg_[:, :], lhsT=wg[b][:, br, kt, :], rhs=hT[:, kt, ts], start=(kt == 0), stop=(kt == 7)),
                            reads=[Rwg[b], RH[kt][tb]], writes=[rg_])
                    P.op("act", lambda e, bg_=bg_, br=br, d=d: e.activation(
                        gs[br][:], bg_[:, :], AF.Sigmoid, bias=bgate[:, l, br * 8 + d:br * 8 + d + 1], scale=1.0),
                        reads=[rg_, RC], writes=[Rgs[br]])
                    ybuf, RY, nk, k0 = ybufs[br]
                    bp_, rp_ = bank()
                    for kt in range(nk):
                        P.op("pe", lambda e, bp_=bp_, b=b, kt=kt, k0=k0, ybuf=ybuf, nk=nk, ts=ts: e.matmul(
                            bp_[:, :], lhsT=wbr[b][:, k0 + kt, :], rhs=ybuf[:, kt, ts], start=(kt == 0), stop=(kt == nk - 1)),
                            reads=[Rwbr[b], RY[kt][tb]], writes=[rp_])
                    P.op("dve", lambda e, bp_=bp_, br=br: e.tensor_tensor(mt[br][:], bp_[:, :], gs[br][:], ALU.mult),
                         reads=[rp_, Rgs[br]], writes=[Rmt[br]])
                P.op("pool", lambda e: e.tensor_tensor(mt[0][:], mt[0][:], mt[1][:], ALU.add), reads=[Rmt[1]], writes=[Rmt[0]])
                P.op("pool", lambda e, d=d, ts=ts: e.tensor_tensor(mg[:, d, ts], mt[0][:], mt[2][:], ALU.add),
                     reads=[Rmt[0], Rmt[2]], writes=[RM[d][tb]])
        for d in range(8):
            b = d % 2
            P.dma("pool", wo[b][:], I["wout_t"][l, d], wslot(f"wo{b}"), writes=[Rwo[b]])
            for tb in range(NTB):
                c = 0 if tb < 2 else 1
                ts = slice(tb * TB, (tb + 1) * TB)
                po, ro = bank()
                for kt in range(8):
                    P.op("pe", lambda e, po=po, b=b, kt=kt, ts=ts: e.matmul(
                        po[:, :], lhsT=wo[b][:, kt, :], rhs=mg[:, kt, ts], start=(kt == 0), stop=(kt == 7)),
                        reads=[Rwo[b], RM[kt][tb]], writes=[ro])
                P.op("dve", lambda e, po=po, d=d, ts=ts, c=c: e.scalar_tensor_tensor(
                    out=xT[:, d, ts], in0=po[:, :], scalar=gtv[:, l, 1, d, c:c + 1], in1=xT[:, d, ts], op0=ALU.mult, op1=ALU.add),
                    reads=[ro, Rmodl[l]], writes=[RX[d][tb]])
        P.scope_end()
        P.scope_end()

    def final_stage():
        P.scope_begin()
        sq, Rsq, rs, Rrs, tmp, Rtmp = norm_scratch()
        ot = [P.sbuf(f"ot{i}", [128, TB], F32) for i in range(4)]
        Rot = [Res(f"ot{i}", dyn=True) for i in range(4)]
        for tb in range(NTB):
            ts = slice(tb * TB, (tb + 1) * TB)
            for ct in range(8):
                P.op("act", lambda e, ct=ct, ts=ts: e.activation(sq[:, ct, :], xT[:, ct, ts], AF.Square), reads=[RX[ct][tb]], writes=[Rsq[ct]])
            bk, rb = bank()
            for ct in range(8):
                P.op("pe", lambda e, bk=bk, ct=ct: e.matmul(bk[:, :], lhsT=ones_bf[:], rhs=sq[:, ct, :], start=(ct == 0), stop=(ct == 7)),
                     reads=[Rsq[ct], RC], writes=[rb])
            r = tb % 2
            P.op("act", lambda e, bk=bk, r=r: e.activation(rs[r][:], bk[:, :], AF.Sqrt, bias=eps_t[:, 0:1], scale=1.0), reads=[rb, RC], writes=[Rrs[r]])
            P.op("dve", lambda e, r=r: e.reciprocal(rs[r][:], rs[r][:]), reads=[Rrs[r]], writes=[Rrs[r]])
            for ct in range(8):
                t = ct % 4
                P.op("dve", lambda e, ct=ct, ts=ts, t=t, r=r: e.scalar_tensor_tensor(
                    out=ot[t][:], in0=xT[:, ct, ts], scalar=finalg[:, ct:ct + 1], in1=rs[r][:], op0=ALU.mult, op1=ALU.mult),
                    reads=[RX[ct][tb], RC, Rrs[r]], writes=[Rot[t]])
                P.dma("sp", yT[ct * 128:(ct + 1) * 128, ts], ot[t][:], s_out, reads=[Rot[t]])
        P.dma("sp", nsd, nsbuf[:].rearrange("p a b c d e -> p (a b c d e)"), s_out, reads=[Rns])
        P.scope_end()

    def dump_x(tag):
        for ct in range(8):
            dump(f"{tag}_{ct}", xT[:, ct, :], [128, NT], RX[ct])

    done = False
    mod_stage(0)
    for l in range(2):
        if stop == "mod":
            break
        ffn_stage(l, 0, 0)
        if f"x1_{l}" in dbg:
            dump_x(f"x1_{l}")
        if stop == f"x1_{l}":
            done = True
            break
        if mixer_stage(l):
            done = True
            break
        if f"x2_{l}" in dbg:
            dump_x(f"x2_{l}")
        if stop == f"x2_{l}":
            done = True
            break
        ffn_stage(l, 1, 2)
        if f"x3_{l}" in dbg:
            dump_x(f"x3_{l}")
        if stop == f"x3_{l}":
            done = True
            break
    final_stage()
    P.emit(final_waits=[s_out])
    P.close()
    return nc, list(dbg_out)


def _grid_pos_embed_T():
    rows = 1024 // 64
    rr, cc = np.meshgrid(np.arange(rows, dtype=np.float32), np.arange(64, dtype=np.float32), indexing="ij")
    quarter = 256
    omega = (1.0 / (10000.0 ** (np.arange(quarter, dtype=np.float32) / np.float32(quarter)))).astype(np.float32)

    def emb(p):
        ang = p.reshape(-1)[:, None].astype(np.float32) * omega[None, :]
        return np.concatenate([np.sin(ang), np.cos(ang)], axis=-1)

    pe = np.concatenate([emb(rr), emb(cc)], axis=-1).astype(np.float32)
    return np.ascontiguousarray(pe.T)


def _prep_shared(inp):
    f = lambda a: np.ascontiguousarray(np.asarray(a, dtype=np.float32))
    S = {}
    S["pos"] = _grid_pos_embed_T()
    S["w_mod"] = f(inp["w_mod"])
    S["b_modT"] = f(inp["b_mod"].reshape(2, 72, 128).transpose(2, 0, 1))
    S["norm_gT"] = f(inp["norm_g"].reshape(2, 3, 8, 128).transpose(3, 0, 1, 2))
    S["final_gT"] = f(inp["final_g"].reshape(8, 128).T)
    w1 = inp["ffn_w1"].reshape(2, 2, 8, 128, 2, 22, 128)
    S["w1t"] = f(w1.transpose(0, 1, 5, 4, 3, 2, 6))
    S["ffn_w2"] = f(inp["ffn_w2"])
    S["w_in"] = f(inp["w_in"])
    wg = inp["w_gate"].reshape(2, 8, 128, 3, 8, 128)
    S["wgate_t"] = f(wg.transpose(0, 4, 2, 3, 1, 5))
    S["b_gateT"] = f(inp["b_gate"].reshape(2, 24, 128).transpose(2, 0, 1))
    wbr = np.concatenate([inp["w_br_a"], inp["w_br_b"], inp["w_br_c"]], axis=1)
    S["wbr_t"] = f(wbr.reshape(2, 8, 128, 8, 128).transpose(0, 3, 2, 1, 4))
    S["wout_t"] = f(inp["w_out"].reshape(2, 8, 128, 8, 128).transpose(0, 3, 2, 1, 4))
    a = np.stack([inp["s5_a_re"], inp["s5_a_im"]], axis=0)
    a6 = a.reshape(2, 2, 2, 16, 2, 64)
    S["a_sl"] = f(a6.transpose(4, 5, 1, 0, 3, 2).reshape(128, 2, 2, 16, 2))
    ld = inp["s5_log_dt"].reshape(2, 2, 16, 2)
    S["ldt_sl"] = f(np.broadcast_to(ld.transpose(3, 0, 2, 1)[:, None], (2, 64, 2, 16, 2)).reshape(128, 2, 16, 2))
    a7 = a.reshape(2, 2, 2, 4, 8, 64)
    S["a_cl"] = f(np.broadcast_to(a7.transpose(4, 1, 0, 3, 2, 5)[:, None], (8, 16, 2, 2, 4, 2, 64)).reshape(128, 2, 2, 4, 2, 64))
    ld2 = inp["s5_log_dt"].reshape(2, 2, 4, 8)
    S["ldt_cl"] = f(np.broadcast_to(ld2.transpose(3, 0, 2, 1)[:, None, :, :, :, None], (8, 16, 2, 4, 2, 64)).reshape(128, 2, 4, 2, 64))
    b = np.stack([inp["s5_b_re"], inp["s5_b_im"]], axis=0)
    b7 = b.reshape(2, 2, 2, 4, 8, 64, 16)
    S["b_cl"] = f(b7.transpose(4, 6, 1, 0, 3, 2, 5).reshape(128, 2, 2, 4, 2, 64))
    b8 = b.reshape(2, 2, 2, 16, 2, 64, 16)
    S["b_sl"] = f(b8.transpose(4, 5, 1, 0, 3, 2, 6).reshape(128, 2, 2, 16, 2, 16))
    c = np.stack([inp["s5_c_re"], inp["s5_c_im"]], axis=0)
    c8 = c.reshape(2, 2, 2, 16, 2, 16, 64)
    S["c_sl"] = f(c8.transpose(4, 6, 1, 0, 3, 2, 5).reshape(128, 2, 2, 16, 2, 16))
    S["s5_dT"] = f(inp["s5_d"].reshape(2, 4, 128).transpose(2, 0, 1))
    S["s5_w_glu"] = f(inp["s5_w_glu"])
    S["conv_wT"] = f(inp["conv_w"].reshape(2, 31, 2, 128).transpose(3, 0, 2, 1))
    cv = np.stack([inp["conv_b"], inp["conv_ln_g"], inp["conv_ln_b"]], axis=1)
    S["conv_vT"] = f(cv.reshape(2, 3, 2, 128).transpose(3, 0, 1, 2))
    S["sg_ln"] = f(np.stack([inp["sg_ln_g"], inp["sg_ln_b"]], axis=1))
    S["sg_wT"] = f(inp["sg_w"].transpose(3, 0, 1, 2))
    sgb = inp["sg_b"].reshape(2, 2, 2, 128)
    S["sg_bT"] = f(np.broadcast_to(sgb.transpose(2, 0, 1, 3)[:, None], (2, 64, 2, 2, 128)).reshape(128, 2, 2, 128))
    S["ident"] = np.eye(128, dtype=np.float32)
    p = np.arange(128)
    S["maskE"] = f(((p[:, None] // 16) % 2 == np.arange(2)[None, :]))
    S["mask2"] = f(((p[:, None] // 64) == np.arange(2)[None, :]))
    S["mask3"] = f((np.arange(8)[None, None, :] == (2 * np.arange(4)[None, :, None] + (p // 64)[:, None, None])))
    return S


def _prep_core(inp, core):
    f = lambda a: np.ascontiguousarray(np.asarray(a, dtype=np.float32))
    C = {}
    xp = inp["x_prompt"][4 * core:4 * core + 4].reshape(1024, 1024)
    xs = inp["x_sample"][core]
    C["xin"] = f(np.concatenate([xp, xs], axis=0).T)
    C["cond"] = f(np.stack([inp["c_ctx"], inp["c"][core]], axis=1))
    h0 = inp["state_ssm"][core].reshape(2, 2, 16, 2, 64, 2)
    C["h0"] = f(h0.transpose(3, 4, 0, 2, 1, 5).reshape(128, 2, 16, 2, 2))
    return C


_NC_CACHE = {}


def kernel(**inputs):
    inp = {k: np.asarray(v) for k, v in inputs.items()}
    if "nc" not in _NC_CACHE:
        _NC_CACHE["nc"] = build_program()[0]
    nc = _NC_CACHE["nc"]
    S = _prep_shared(inp)
    in_maps = []
    for core in range(8):
        m = dict(S)
        m.update(_prep_core(inp, core))
        in_maps.append(m)
    res = run_bass_kernel_spmd(nc, in_maps, core_ids=list(range(8)))
    y_prompt = np.empty((32, 256, 1024), np.float32)
    y_sample = np.empty((8, 1024, 1024), np.float32)
    new_state = np.empty((32, 2, 2, 32, 64, 2), np.float32)
    for core in range(8):
        r = res.results[core]
        y = np.asarray(r["yT"]).T
        y_prompt[4 * core:4 * core + 4] = y[:1024].reshape(4, 256, 1024)
        y_sample[core] = y[1024:]
        ns = np.asarray(r["ns"]).reshape(2, 64, 2, 4, 16, 2, 2)
        new_state[4 * core:4 * core + 4] = ns.transpose(3, 2, 5, 4, 0, 1, 6).reshape(4, 2, 2, 32, 64, 2)
    return (y_prompt, y_sample, new_state)
```

```python
import math
import numpy as np
import concourse.bass as bass
import concourse.mybir as mybir
from concourse.bass_utils import run_bass_kernel_spmd

F32 = mybir.dt.float32
BF16 = mybir.dt.bfloat16
I32 = mybir.dt.int32
AF = mybir.ActivationFunctionType
ALU = mybir.AluOpType


class Res:
    __slots__ = ("name", "last_w", "readers", "dyn")

    def __init__(self, name, dyn=False):
        self.name = name
        self.last_w = None
        self.readers = []
        self.dyn = dyn


class Slot:
    __slots__ = ("sem", "count")

    def __init__(self, sem):
        self.sem = sem
        self.count = 0


class Ins:
    __slots__ = ("eng", "fn", "idx", "deps", "needs_inc", "inc_val", "slot", "slot_val")

    def __init__(self, eng, fn, idx):
        self.eng = eng
        self.fn = fn
        self.idx = idx
        self.deps = []
        self.needs_inc = False
        self.inc_val = None
        self.slot = None
        self.slot_val = None


ENGS = ("pe", "act", "dve", "pool", "sp")


class Prog:
    def __init__(self, nc):
        self.nc = nc
        self.streams = {e: [] for e in ENGS}
        self.sems = {}
        self._ctx = []
        self.RDYN = Res("RDYN")
        self._scopes = []
        self.dummy = None

    def enter(self, cm):
        v = cm.__enter__()
        self._ctx.append(cm)
        return v

    def sbuf(self, name, shape, dt):
        self._uid = getattr(self, "_uid", 0) + 1
        return self.enter(self.nc.sbuf_tensor(f"sb{self._uid}_{name}", list(shape), dt))

    def psum(self, name, shape, dt):
        return self.enter(self.nc.psum_tensor(name, list(shape), dt))

    def new_slot(self, name):
        cm = self.nc.semaphore(name)
        v = cm.__enter__()
        self._sem_ctx = getattr(self, "_sem_ctx", [])
        self._sem_ctx.append(cm)
        return Slot(v)

    def scope_begin(self):
        self._scopes.append(len(self._ctx))

    def scope_end(self):
        n = self._scopes.pop()
        d = self.dummy
        self.op("pool", lambda e: e.memset(d[:, 0:1], 0.0), writes=[self.RDYN])
        while len(self._ctx) > n:
            cm = self._ctx.pop()
            cm.__exit__(None, None, None)

    def _add(self, eng, fn, reads, writes):
        st = self.streams[eng]
        ins = Ins(eng, fn, len(st))
        st.append(ins)
        reads = list(reads)
        writes = list(writes)
        if any(r.dyn for r in reads) or any(w.dyn for w in writes):
            reads.append(self.RDYN)
        deps = []
        for r in reads:
            if r.last_w is not None:
                deps.append(r.last_w)
        for w in writes:
            if w.last_w is not None:
                deps.append(w.last_w)
            deps.extend(w.readers)
        seen = set()
        for d in deps:
            if d is ins or id(d) in seen:
                continue
            seen.add(id(d))
            if d.slot is None and d.eng == eng:
                if eng == "pe" or eng == "sp":
                    continue
                if ins.idx - d.idx > 1:
                    continue
            if d.slot is not None:
                ins.deps.append((d, d.slot.count))
            else:
                ins.deps.append((d, None))
                d.needs_inc = True
        for r in reads:
            r.readers.append(ins)
        for w in writes:
            w.last_w = ins
            w.readers = []
        return ins

    def op(self, eng, fn, reads=(), writes=()):
        return self._add(eng, fn, reads, writes)

    def dma(self, eng, out, in_, slot, reads=(), writes=(), **kw):
        ins = self._add(eng, lambda e: e.dma_start(out=out, in_=in_, **kw), reads, writes)
        ins.slot = slot
        slot.count += 16
        ins.slot_val = slot.count
        return ins

    def emit(self, final_waits=()):
        nc = self.nc
        for e in ENGS:
            if e == "sp":
                continue
            self.sems[e] = self.enter(nc.semaphore("sem_" + e))
        self.sems["sp"] = None
        for e in ENGS:
            c = 0
            for ins in self.streams[e]:
                if ins.needs_inc and ins.slot is None:
                    c += 1
                    ins.inc_val = c
        block = self.enter(nc.Block())

        def run(engname, e):
            waited = {}
            for ins in self.streams[engname]:
                need = {}
                for d, sval in ins.deps:
                    if d.slot is not None:
                        sem, val = d.slot.sem, sval
                    else:
                        sem, val = self.sems[d.eng], d.inc_val
                    k = id(sem)
                    if k not in need or need[k][1] < val:
                        need[k] = (sem, val)
                for k, (sem, val) in need.items():
                    if waited.get(k, 0) >= val:
                        continue
                    e.wait_ge(sem, val)
                    waited[k] = val
                r = ins.fn(e)
                if ins.slot is not None:
                    r.then_inc(ins.slot.sem, 16)
                elif ins.needs_inc:
                    r.then_inc(self.sems[engname], 1)
            if engname == "sp":
                for slot in final_waits:
                    e.wait_ge(slot.sem, slot.count)

        @block.tensor
        def _(e):
            run("pe", e)

        @block.scalar
        def _(e):
            run("act", e)

        @block.vector
        def _(e):
            run("dve", e)

        @block.gpsimd
        def _(e):
            run("pool", e)

        @block.sync
        def _(e):
            run("sp", e)

    def close(self):
        while self._ctx:
            cm = self._ctx.pop()
            cm.__exit__(None, None, None)
        for cm in reversed(getattr(self, "_sem_ctx", [])):
            cm.__exit__(None, None, None)


def CAP(ap, off, dims):
    return bass.AP(ap.tensor, ap.offset + off, [list(ap.ap[0])] + [list(d) for d in dims])


D = 1024
NT = 2048
TB = 512
NTB = 4
DFF = 2816
NF = 22
FC = 11
EPS = 1e-6
GELU = AF.Gelu_apprx_tanh
SE = "dve"
GP_OFF = [0, 286, 572, 858, 1144]
GP_LEN = 1144 + 1054

INPUT_SPECS = [
    ("xin", [1024, 2048]), ("pos", [1024, 1024]), ("cond", [1024, 2]),
    ("h0", [128, 2, 16, 2, 2]),
    ("w_mod", [2, 1024, 9216]), ("b_modT", [128, 2, 72]), ("norm_gT", [128, 2, 3, 8]), ("final_gT", [128, 8]),
    ("w1t", [2, 2, 22, 2, 128, 8, 128]), ("ffn_w2", [2, 2, 2816, 1024]),
    ("w_in", [2, 1024, 1536]), ("wgate_t", [2, 8, 128, 3, 8, 128]), ("b_gateT", [128, 2, 24]),
    ("wbr_t", [2, 8, 128, 8, 128]), ("wout_t", [2, 8, 128, 8, 128]),
    ("a_sl", [128, 2, 2, 16, 2]), ("ldt_sl", [128, 2, 16, 2]),
    ("a_cl", [128, 2, 2, 4, 2, 64]), ("ldt_cl", [128, 2, 4, 2, 64]),
    ("b_cl", [128, 2, 2, 4, 2, 64]), ("b_sl", [128, 2, 2, 16, 2, 16]), ("c_sl", [128, 2, 2, 16, 2, 16]),
    ("s5_dT", [128, 2, 4]), ("s5_w_glu", [2, 512, 512]),
    ("conv_wT", [128, 2, 2, 31]), ("conv_vT", [128, 2, 3, 2]),
    ("sg_ln", [2, 2, 256]), ("sg_wT", [128, 2, 4, 128]), ("sg_bT", [128, 2, 2, 128]),
    ("ident", [128, 128]), ("maskE", [128, 2]), ("mask2", [128, 2]), ("mask3", [128, 4, 8]),
]


def build_program(dbg=(), stop=None):
    nc = bass.Bass("TRN2", target_bir_lowering=False)
    P = Prog(nc)
    I = {}
    for name, shape in INPUT_SPECS:
        I[name] = nc.dram_tensor(name, list(shape), F32, kind="ExternalInput").ap()
    yT = nc.dram_tensor("yT", [1024, 2048], F32, kind="ExternalOutput").ap()
    nsd = nc.dram_tensor("ns", [128, 2 * 4 * 16 * 2 * 2], F32, kind="ExternalOutput").ap()
    dbg_out = {}

    xT = P.sbuf("xT", [128, 8, NT], F32)
    RX = [[Res(f"x{c}_{t}") for t in range(NTB)] for c in range(8)]

    def alloc_hT():
        return P.sbuf("hT", [128, 8, NT], BF16), [[Res(f"h{c}_{t}", dyn=True) for t in range(NTB)] for c in range(8)]
    P.dummy = P.sbuf("dummyb", [128, 4], F32)
    ones_bf = P.sbuf("ones_bf", [128, 128], BF16)
    ones256 = P.sbuf("ones256", [128, 128], F32)
    ident = P.sbuf("ident", [128, 128], F32)
    eps_t = P.sbuf("eps_t", [128, 1], F32)
    modT = P.sbuf("modT", [128, 2, 72, 2], F32)
    gsc = P.sbuf("gsc", [128, 2, 3, 8, 2], F32)
    gtv = P.sbuf("gtv", [128, 2, 3, 8, 2], F32)
    bmod = P.sbuf("bmod", [128, 2, 72], F32)
    normg = P.sbuf("normg", [128, 2, 3, 8], F32)
    finalg = P.sbuf("finalg", [128, 8], F32)
    bgate = P.sbuf("bgate", [128, 2, 24], F32)
    condT = P.sbuf("condT", [128, 8, 2], F32)
    condb = P.sbuf("condb", [128, 8, 2], BF16)
    h0t = P.sbuf("h0t", [128, 2, 16, 2, 2], F32)
    nsbuf = P.sbuf("nsbuf", [128, 2, 4, 16, 2, 2], F32)
    s5d = P.sbuf("s5d", [128, 2, 4], F32)
    convw = P.sbuf("convw", [128, 2, 2, 31], F32)
    convv = P.sbuf("convv", [128, 2, 3, 2], F32)
    maskE = P.sbuf("maskE", [128, 2], F32)
    mask2 = P.sbuf("mask2", [128, 2], F32)
    mask3 = P.sbuf("mask3", [128, 4, 8], F32)
    RC = Res("consts")
    Rmodl = [Res("mod0"), Res("mod1")]
    Rcond = Res("cond")
    Rns = Res("ns")
    banks = [P.psum(f"bank{i}", [128, 512], F32) for i in range(8)]
    RB = [Res(f"bank{i}") for i in range(8)]
    bank_ctr = [0]

    bank_reserved = [None]

    def bank():
        i = bank_ctr[0] % 8
        bank_ctr[0] += 1
        if i == bank_reserved[0]:
            i = bank_ctr[0] % 8
            bank_ctr[0] += 1
        return banks[i], RB[i]

    s_in = P.new_slot("s_in")
    s_out = P.new_slot("s_out")
    wslots = {}

    def wslot(name):
        if name not in wslots:
            wslots[name] = P.new_slot("ws_" + name)
        return wslots[name]

    P.op("pool", lambda e: e.memset(ones_bf[:], 1.0 / 1024.0), writes=[RC])
    P.op("pool", lambda e: e.memset(ones256[:], 1.0 / 256.0), writes=[RC])
    P.op("pool", lambda e: e.memset(eps_t[:], EPS), writes=[RC])
    P.op("pool", lambda e: e.memset(P.dummy[:], 0.0), writes=[RC])
    small = [(ident, "ident"), (bmod, "b_modT"), (normg, "norm_gT"), (finalg, "final_gT"), (bgate, "b_gateT"),
             (h0t, "h0"), (s5d, "s5_dT"), (convw, "conv_wT"), (convv, "conv_vT"), (maskE, "maskE"), (mask2, "mask2"),
             (mask3, "mask3")]
    for t, nm in small:
        P.dma("sp", t[:], I[nm], s_in, writes=[RC])
    P.dma("sp", condT[:], I["cond"].rearrange("(k p) c -> p k c", p=128), s_in, writes=[RC])
    for ct in range(8):
        P.dma("sp", xT[:, ct, :], I["xin"][ct * 128:(ct + 1) * 128, :], s_in, writes=RX[ct])
    P.scope_begin()
    ptmp = [P.sbuf(f"ptmp{i}", [128, 1024], F32) for i in range(2)]
    Rpt = [Res(f"ptmp{i}", dyn=True) for i in range(2)]
    for ct in range(8):
        b = ct % 2
        P.dma("sp", ptmp[b][:], I["pos"][ct * 128:(ct + 1) * 128, :], wslot(f"ptmp{b}"), writes=[Rpt[b]])
        P.op("dve", lambda e, ct=ct, b=b: e.tensor_tensor(xT[:, ct, 1024:2048], xT[:, ct, 1024:2048], ptmp[b][:], ALU.add),
             reads=[Rpt[b]], writes=[RX[ct][2], RX[ct][3]])
    P.op("act", lambda e: e.activation(condb[:], condT[:], AF.Silu), reads=[RC], writes=[Rcond])

    P.scope_end()

    def mod_gen(l, wm, Rwm):
        bk, rb = bank()
        bank_reserved[0] = banks.index(bk)
        for ch in range(18):
            b = ch % 2
            P.dma("pool", wm[b][:], I["w_mod"][l, :, ch * 512:(ch + 1) * 512].rearrange("(k p) c -> p k c", p=128),
                  wslot(f"wm{b}"), writes=[Rwm[b]])
            for m in range(4):
                mt = ch * 4 + m
                for kt in range(8):
                    P.op("pe", lambda e, bk=bk, b=b, m=m, mt=mt, kt=kt: e.matmul(
                        bk[:, mt * 2:mt * 2 + 2], lhsT=wm[b][:, kt, m * 128:(m + 1) * 128], rhs=condb[:, kt, :],
                        start=(kt == 0), stop=(kt == 7)), reads=[Rwm[b], Rcond], writes=[rb])
            yield
        P.op("dve", lambda e, bk=bk, l=l: e.tensor_tensor(
            modT[:, l], bk[:, 0:144].rearrange("p (m c) -> p m c", c=2),
            CAP(bmod[:, l, :], 0, [[1, 72], [0, 2]]), ALU.add), reads=[rb, RC], writes=[Rmodl[l]])
        bank_reserved[0] = None
        for n in range(3):
            P.op("dve", lambda e, l=l, n=n: e.tensor_scalar_add(gsc[:, l, n], modT[:, l, (3 * n + 1) * 8:(3 * n + 2) * 8, :], 1.0),
                 reads=[Rmodl[l]], writes=[Rmodl[l]])
            P.op("dve", lambda e, l=l, n=n: e.tensor_tensor(gsc[:, l, n], gsc[:, l, n], CAP(normg[:, l, n, :], 0, [[1, 8], [0, 2]]), ALU.mult),
                 reads=[Rmodl[l], RC], writes=[Rmodl[l]])
            P.op("dve", lambda e, l=l, n=n: e.tensor_scalar_mul(gtv[:, l, n], modT[:, l, (3 * n + 2) * 8:(3 * n + 3) * 8, :],
                                                              0.5 if n != 1 else 1.0), reads=[Rmodl[l]], writes=[Rmodl[l]])

    def mod_bufs():
        wm = [P.sbuf(f"wm{i}", [128, 8, 512], BF16) for i in range(2)]
        Rwm = [Res(f"wm{i}", dyn=True) for i in range(2)]
        return wm, Rwm

    def mod_stage(l):
        P.scope_begin()
        for _ in mod_gen(l, *mod_bufs()):
            pass
        P.scope_end()

    MODG = {"gen": None}

    def mod_step():
        g = MODG["gen"]
        if g is not None:
            try:
                next(g)
            except StopIteration:
                MODG["gen"] = None

    def sh_ap(l, n, ct, c):
        return modT[:, l, (3 * n) * 8 + ct, c:c + 1]

    def dump(name, ap, shape, reads, dt=F32):
        d = nc.dram_tensor("dbg_" + name, list(shape), dt, kind="ExternalOutput").ap()
        dbg_out[name] = d
        P.dma("sp", d, ap, s_out, reads=reads)

    def norm_stage(l, n, hT, RH, own_scope=False):
        if own_scope:
            P.scope_begin()
        sq, Rsq, rs, Rrs, tmp, Rtmp = norm_scratch()
        for tb in range(NTB):
            c = 0 if tb < 2 else 1
            ts = slice(tb * TB, (tb + 1) * TB)
            for ct in range(8):
                P.op("act", lambda e, ct=ct, ts=ts: e.activation(sq[:, ct, :], xT[:, ct, ts], AF.Square),
                     reads=[RX[ct][tb]], writes=[Rsq[ct]])
            bk, rb = bank()
            for ct in range(8):
                P.op("pe", lambda e, bk=bk, ct=ct: e.matmul(bk[:, :], lhsT=ones_bf[:], rhs=sq[:, ct, :], start=(ct == 0), stop=(ct == 7)),
                     reads=[Rsq[ct], RC], writes=[rb])
            r = tb % 2
            P.op("act", lambda e, bk=bk, r=r: e.activation(rs[r][:], bk[:, :], AF.Sqrt, bias=eps_t[:, 0:1], scale=1.0),
                 reads=[rb, RC], writes=[Rrs[r]])
            P.op("dve", lambda e, r=r: e.reciprocal(rs[r][:], rs[r][:]), reads=[Rrs[r]], writes=[Rrs[r]])
            for ct in range(8):
                t = ct % 2
                P.op("dve", lambda e, ct=ct, ts=ts, t=t, r=r, c=c: e.scalar_tensor_tensor(
                    out=tmp[t][:], in0=xT[:, ct, ts], scalar=gsc[:, l, n, ct, c:c + 1], in1=rs[r][:], op0=ALU.mult, op1=ALU.mult),
                    reads=[RX[ct][tb], Rmodl[l], Rrs[r]], writes=[Rtmp[t]])
                P.op("act", lambda e, ct=ct, ts=ts, t=t, c=c: e.activation(
                    hT[:, ct, ts], tmp[t][:], AF.Identity, bias=sh_ap(l, n, ct, c), scale=1.0),
                    reads=[Rtmp[t], Rmodl[l]], writes=[RH[ct][tb]])
        if own_scope:
            P.scope_end()

    def norm_scratch():
        sq = P.sbuf("sq", [128, 8, TB], BF16)
        rs = [P.sbuf(f"rs{i}", [128, TB], F32) for i in range(2)]
        tmp = [P.sbuf(f"ntmp{i}", [128, TB], F32) for i in range(2)]
        return (sq, [Res(f"sq{i}", dyn=True) for i in range(8)], rs, [Res(f"rs{i}", dyn=True) for i in range(2)],
                tmp, [Res(f"ntmp{i}", dyn=True) for i in range(2)])

    def ffn_stage(l, w, n):
        P.scope_begin()
        hT, RH = alloc_hT()
        norm_stage(l, n, hT, RH)
        act = P.sbuf("act", [128, FC, NT], BF16)
        RA = [[Res(f"act{f}_{t}", dyn=True) for t in range(NTB)] for f in range(FC)]
        w1b = [P.sbuf(f"w1b{i}", [128, 2, 2, 8, 128], BF16) for i in range(2)]
        Rw1 = [Res(f"w1b{i}", dyn=True) for i in range(2)]
        w2b = P.sbuf("w2b", [128, FC, D], BF16)
        Rw2 = Res("w2b", dyn=True)
        sg = [P.sbuf(f"sg{i}", [128, TB], F32) for i in range(2)]
        Rsg = [Res(f"sg{i}", dyn=True) for i in range(2)]
        it = 0
        for chunk in range(2):
            P.dma("pool", w2b[:], I["ffn_w2"][l, w, chunk * FC * 128:(chunk + 1) * FC * 128, :].rearrange("(f p) d -> p f d", p=128),
                  wslot("w2b"), writes=[Rw2])
            for pair in range(6):
                nf = 2 if pair < 5 else 1
                b = it % 2
                it += 1
                f0 = chunk * FC + pair * 2
                P.dma("pool", w1b[b][:, 0:nf], I["w1t"][l, w, f0:f0 + nf].rearrange("f g p k c -> p f g k c"),
                      wslot(f"w1b{b}"), writes=[Rw1[b]])
                for fi in range(nf):
                    f = pair * 2 + fi
                    for tb in range(NTB):
                        ts = slice(tb * TB, (tb + 1) * TB)
                        pg, rg = bank()
                        pu, ru = bank()
                        for gu, (pb_, rb_) in enumerate(((pg, rg), (pu, ru))):
                            for kt in range(8):
                                P.op("pe", lambda e, pb_=pb_, b=b, fi=fi, gu=gu, kt=kt, ts=ts: e.matmul(
                                    pb_[:, :], lhsT=w1b[b][:, fi, gu, kt, :], rhs=hT[:, kt, ts], start=(kt == 0), stop=(kt == 7)),
                                    reads=[Rw1[b], RH[kt][tb]], writes=[rb_])
                        s = (f * NTB + tb) % 2
                        P.op("act", lambda e, pg=pg, s=s: e.activation(sg[s][:], pg[:, :], AF.Silu), reads=[rg], writes=[Rsg[s]])
                        P.op("dve", lambda e, pu=pu, s=s, f=f, ts=ts: e.tensor_tensor(act[:, f, ts], sg[s][:], pu[:, :], ALU.mult),
                             reads=[Rsg[s], ru], writes=[RA[f][tb]])
            for d in range(8):
                for tb in range(NTB):
                    c = 0 if tb < 2 else 1
                    ts = slice(tb * TB, (tb + 1) * TB)
                    po, ro = bank()
                    for f in range(FC):
                        P.op("pe", lambda e, po=po, f=f, d=d, ts=ts: e.matmul(
                            po[:, :], lhsT=w2b[:, f, d * 128:(d + 1) * 128], rhs=act[:, f, ts], start=(f == 0), stop=(f == FC - 1)),
                            reads=[Rw2, RA[f][tb]], writes=[ro])
                    P.op("dve", lambda e, po=po, d=d, ts=ts, c=c: e.scalar_tensor_tensor(
                        out=xT[:, d, ts], in0=po[:, :], scalar=gtv[:, l, n, d, c:c + 1], in1=xT[:, d, ts], op0=ALU.mult, op1=ALU.add),
                        reads=[ro, Rmodl[l]], writes=[RX[d][tb]])
        P.scope_end()

    def lam_q(pref, ar, ai, ldt, shape, R_in, outs, RT):
        P.scope_begin()
        T = dict(outs)
        for nm in ["dt", "th", "mg", "t1", "t2", "s", "c", "den", "nr"]:
            T[nm] = P.sbuf(pref + nm, [128] + list(shape), F32)
        ki = P.sbuf(pref + "ki", [128] + list(shape), I32)
        two_pi = 2.0 * math.pi

        def V(nm):
            return T[nm][:]

        P.op("act", lambda e: e.activation(V("dt"), ldt, AF.Exp), reads=[R_in], writes=[RT])
        P.op("dve", lambda e: e.tensor_tensor(V("th"), ai, V("dt"), ALU.mult), reads=[R_in, RT], writes=[RT])
        P.op("dve", lambda e: e.tensor_tensor(V("mg"), ar, V("dt"), ALU.mult), reads=[R_in, RT], writes=[RT])
        P.op("act", lambda e: e.activation(V("mg"), V("mg"), AF.Exp), reads=[RT], writes=[RT])
        for which, shift in (("s", 0.0), ("c", 0.25)):
            P.op("dve", lambda e, shift=shift: e.tensor_scalar(V("t1"), V("th"), 1.0 / two_pi, shift, ALU.mult, ALU.add),
                 reads=[RT], writes=[RT])
            P.op("dve", lambda e: e.tensor_copy(ki[:], V("t1")), reads=[RT], writes=[RT])
            P.op("dve", lambda e: e.tensor_copy(V("t2"), ki[:]), reads=[RT], writes=[RT])
            P.op("dve", lambda e: e.tensor_tensor(V("t1"), V("t1"), V("t2"), ALU.subtract), reads=[RT], writes=[RT])
            P.op("dve", lambda e: e.tensor_scalar(V("t1"), V("t1"), two_pi, 3.1415925, ALU.mult, ALU.min), reads=[RT], writes=[RT])
            P.op("dve", lambda e: e.tensor_scalar_max(V("t1"), V("t1"), -3.1415925), reads=[RT], writes=[RT])
            P.op("act", lambda e, which=which: e.activation(V(which), V("t1"), AF.Sin), reads=[RT], writes=[RT])
        P.op("dve", lambda e: e.tensor_tensor(V("lr"), V("mg"), V("c"), ALU.mult), reads=[RT], writes=[RT])
        P.op("dve", lambda e: e.tensor_tensor(V("li"), V("mg"), V("s"), ALU.mult), reads=[RT], writes=[RT])
        P.op("dve", lambda e: e.tensor_scalar_add(V("nr"), V("lr"), -1.0), reads=[RT], writes=[RT])
        P.op("dve", lambda e: e.tensor_tensor(V("den"), ar, ar, ALU.mult), reads=[R_in, RT], writes=[RT])
        P.op("dve", lambda e: e.tensor_tensor(V("t1"), ai, ai, ALU.mult), reads=[R_in, RT], writes=[RT])
        P.op("dve", lambda e: e.tensor_tensor(V("den"), V("den"), V("t1"), ALU.add), reads=[RT], writes=[RT])
        P.op("dve", lambda e: e.reciprocal(V("den"), V("den")), reads=[RT], writes=[RT])
        P.op("dve", lambda e: e.tensor_tensor(V("t1"), V("nr"), ar, ALU.mult), reads=[R_in, RT], writes=[RT])
        P.op("dve", lambda e: e.tensor_tensor(V("t2"), V("li"), ai, ALU.mult), reads=[R_in, RT], writes=[RT])
        P.op("dve", lambda e: e.tensor_tensor(V("t1"), V("t1"), V("t2"), ALU.add), reads=[RT], writes=[RT])
        P.op("dve", lambda e: e.tensor_tensor(V("qr"), V("t1"), V("den"), ALU.mult), reads=[RT], writes=[RT])
        P.op("dve", lambda e: e.tensor_tensor(V("t1"), V("li"), ar, ALU.mult), reads=[R_in, RT], writes=[RT])
        P.op("dve", lambda e: e.tensor_tensor(V("t2"), V("nr"), ai, ALU.mult), reads=[R_in, RT], writes=[RT])
        P.op("dve", lambda e: e.tensor_tensor(V("t1"), V("t1"), V("t2"), ALU.subtract), reads=[RT], writes=[RT])
        P.op("dve", lambda e: e.tensor_tensor(V("qi"), V("t1"), V("den"), ALU.mult), reads=[RT], writes=[RT])
        P.scope_end()

    def cmul(eng, out_r, out_i, ar, ai, br, bi, t1, t2, reads, writes):
        P.op(eng, lambda e: e.tensor_tensor(t1, ar, br, ALU.mult), reads=reads, writes=writes)
        P.op(eng, lambda e: e.tensor_tensor(t2, ai, bi, ALU.mult), reads=reads, writes=writes)
        P.op(eng, lambda e: e.tensor_tensor(t1, t1, t2, ALU.subtract), reads=reads, writes=writes)
        P.op(eng, lambda e: e.tensor_tensor(t2, ar, bi, ALU.mult), reads=reads, writes=writes)
        P.op(eng, lambda e: e.tensor_tensor(out_i, ai, br, ALU.mult), reads=reads, writes=writes)
        P.op(eng, lambda e: e.tensor_tensor(out_i, out_i, t2, ALU.add), reads=reads, writes=writes)
        P.op(eng, lambda e: e.tensor_copy(out_r, t1), reads=reads, writes=writes)

    def cmul6(eng, out_r, out_i, ar, ai, br, bi, t1, t2, reads, writes):
        P.op(eng, lambda e: e.tensor_tensor(t1, ar, br, ALU.mult), reads=reads, writes=writes)
        P.op(eng, lambda e: e.tensor_tensor(t2, ai, bi, ALU.mult), reads=reads, writes=writes)
        P.op(eng, lambda e: e.tensor_tensor(out_r, t1, t2, ALU.subtract), reads=reads, writes=writes)
        P.op(eng, lambda e: e.tensor_tensor(t1, ar, bi, ALU.mult), reads=reads, writes=writes)
        P.op(eng, lambda e: e.tensor_tensor(t2, ai, br, ALU.mult), reads=reads, writes=writes)
        P.op(eng, lambda e: e.tensor_tensor(out_i, t1, t2, ALU.add), reads=reads, writes=writes)

    class _Stop(Exception):
        pass

    def ck(name, l):
        if stop == f"{name}{l}":
            raise _Stop()

    def mixer_stage(l):
        depth = len(P._scopes)
        try:
            _mixer(l)
            return False
        except _Stop:
            while len(P._scopes) > depth:
                P.scope_end()
            return True

    def _mixer(l):
        P.scope_begin()
        y_a = P.sbuf("y_a", [128, 4, NT], BF16)
        RYA = [[Res(f"ya{c}_{t}", dyn=True) for t in range(NTB)] for c in range(4)]
        P.scope_begin()
        U = P.sbuf("U", [128, 4, NT], BF16)
        RU = [[Res(f"U{c}_{t}", dyn=True) for t in range(NTB)] for c in range(4)]
        yg, RYG = U, RU
        P.scope_begin()
        hT, RH = alloc_hT()
        wina = P.sbuf("wina", [128, 8, 512], BF16)
        Rwina = Res("wina", dyn=True)
        P.dma("pool", wina[:], I["w_in"][l, :, 0:512].rearrange("(k p) c -> p k c", p=128), wslot("wina"), writes=[Rwina])
        norm_stage(l, 1, hT, RH)
        for ct in range(4):
            for tb in range(NTB):
                ts = slice(tb * TB, (tb + 1) * TB)
                bk, rb = bank()
                for kt in range(8):
                    P.op("pe", lambda e, bk=bk, ct=ct, kt=kt, ts=ts: e.matmul(
                        bk[:, :], lhsT=wina[:, kt, ct * 128:(ct + 1) * 128], rhs=hT[:, kt, ts], start=(kt == 0), stop=(kt == 7)),
                        reads=[Rwina, RH[kt][tb]], writes=[rb])
                P.op("act", lambda e, bk=bk, ct=ct, ts=ts: e.activation(U[:, ct, ts], bk[:, :], AF.Copy), reads=[rb], writes=[RU[ct][tb]])
        P.scope_end()
        if "za" in dbg:
            P.scope_begin()
            for ct in range(4):
                tmpd = P.sbuf(f"dbgza{ct}", [128, NT], F32)
                Rd = Res(f"dbgza{ct}", dyn=True)
                P.op("dve", lambda e, ct=ct, tmpd=tmpd: e.tensor_copy(tmpd[:], U[:, ct, :]), reads=RU[ct], writes=[Rd])
                dump(f"za{ct}", tmpd[:], [128, NT], [Rd])
            P.scope_end()
        ck("m_u", l)
        asl = P.sbuf("asl", [128, 2, 16, 2], F32)
        ldsl = P.sbuf("ldsl", [128, 16, 2], F32)
        csl = P.sbuf("csl", [128, 2, 16, 2, 16], F32)
        Rpar = Res("s5par", dyn=True)
        TS = {nm: P.sbuf("sl_" + nm, [128, 16, 2], F32) for nm in ("lr", "li", "qr", "qi")}
        RTS = Res("sl_tab", dyn=True)
        TC = {nm: P.sbuf("cl_" + nm, [128, 4, 2, 64], F32) for nm in ("lr", "li")}
        RTC = Res("cl_tab", dyn=True)
        Bc = [P.sbuf(f"Bc{i}", [128, 4, 2, 64], F32) for i in range(2)]
        Bs = [P.sbuf(f"Bs{i}", [128, 16, 2, 16], F32) for i in range(2)]
        L4c = [P.sbuf(f"L4c{i}", [128, 4, 2, 64], F32) for i in range(2)]
        RBb = Res("Bbar", dyn=True)
        P.scope_begin()
        acl = P.sbuf("acl", [128, 2, 4, 2, 64], F32)
        ldcl = P.sbuf("ldcl", [128, 4, 2, 64], F32)
        bcl = P.sbuf("bcl", [128, 2, 4, 2, 64], F32)
        bsl = P.sbuf("bsl", [128, 2, 16, 2, 16], F32)
        for t, nm in ((asl, "a_sl"), (ldsl, "ldt_sl"), (acl, "a_cl"), (ldcl, "ldt_cl"), (bcl, "b_cl"), (bsl, "b_sl"), (csl, "c_sl")):
            P.dma("sp", t[:], I[nm][:, l], wslot(f"s5par{l}"), writes=[Rpar])
        TCq = dict(TC)
        TCq["qr"] = P.sbuf("cl_qr", [128, 4, 2, 64], F32)
        TCq["qi"] = P.sbuf("cl_qi", [128, 4, 2, 64], F32)
        tcl = [P.sbuf(f"tcl{i}", [128, 4, 2, 64], F32) for i in range(2)]
        tsl = [P.sbuf(f"tsl{i}", [128, 16, 2, 16], F32) for i in range(2)]
        lam_q("sl_", asl[:, 0], asl[:, 1], ldsl[:], [16, 2], Rpar, TS, RTS)
        lam_q("cl_", acl[:, 0], acl[:, 1], ldcl[:], [4, 2, 64], Rpar, TCq, RTC)
        cmul("dve", Bc[0][:], Bc[1][:], TCq["qr"][:], TCq["qi"][:], bcl[:, 0], bcl[:, 1], tcl[0][:], tcl[1][:],
             [RTC, Rpar, RBb], [RBb])
        L2c = [P.sbuf(f"L2c{i}", [128, 4, 2, 64], F32) for i in range(2)]
        cmul6("dve", L2c[0][:], L2c[1][:], TC["lr"][:], TC["li"][:], TC["lr"][:], TC["li"][:], tcl[0][:], tcl[1][:], [RTC], [RTC])
        cmul6("dve", L4c[0][:], L4c[1][:], L2c[0][:], L2c[1][:], L2c[0][:], L2c[1][:], tcl[0][:], tcl[1][:], [RTC], [RTC])
        qrb = CAP(TS["qr"][:], 0, [[2, 16], [1, 2], [0, 16]])
        qib = CAP(TS["qi"][:], 0, [[2, 16], [1, 2], [0, 16]])
        cmul("dve", Bs[0][:], Bs[1][:], qrb, qib, bsl[:, 0], bsl[:, 1], tsl[0][:], tsl[1][:], [RTS, Rpar, RBb], [RBb])
        P.scope_end()
        L2 = [P.sbuf(f"L2{i}", [128, 16, 2], F32) for i in range(2)]
        L4 = [P.sbuf(f"L4{i}", [128, 16, 2], F32) for i in range(2)]
        L8 = [P.sbuf(f"L8{i}", [128, 16, 2], F32) for i in range(2)]
        t8 = [P.sbuf(f"t8{i}", [128, 4, 16, 2], F32) for i in range(2)]
        RL8 = Res("L8", dyn=True)
        cmul6("dve", L2[0][:], L2[1][:], TS["lr"][:], TS["li"][:], TS["lr"][:], TS["li"][:], t8[0][:, 0], t8[1][:, 0], [RTS, RL8], [RL8])
        cmul6("dve", L4[0][:], L4[1][:], L2[0][:], L2[1][:], L2[0][:], L2[1][:], t8[0][:, 0], t8[1][:, 0], [RL8], [RL8])
        cmul6("dve", L8[0][:], L8[1][:], L4[0][:], L4[1][:], L4[0][:], L4[1][:], t8[0][:, 0], t8[1][:, 0], [RL8], [RL8])
        LPs = [P.sbuf(f"LPs{i}", [128, 9, 16, 2], F32) for i in range(2)]
        P.op("dve", lambda e: e.memset(LPs[0][:, 0], 1.0), writes=[RL8])
        P.op("dve", lambda e: e.memset(LPs[1][:, 0], 0.0), writes=[RL8])
        P.op("dve", lambda e: e.tensor_copy(LPs[0][:, 1], TS["lr"][:]), reads=[RTS], writes=[RL8])
        P.op("dve", lambda e: e.tensor_copy(LPs[1][:, 1], TS["li"][:]), reads=[RTS], writes=[RL8])
        cmul6("dve", LPs[0][:, 2:4], LPs[1][:, 2:4], LPs[0][:, 0:2], LPs[1][:, 0:2],
              CAP(L2[0][:], 0, [[0, 2], [2, 16], [1, 2]]), CAP(L2[1][:], 0, [[0, 2], [2, 16], [1, 2]]),
              t8[0][:, 0:2], t8[1][:, 0:2], [RL8], [RL8])
        cmul6("dve", LPs[0][:, 4:8], LPs[1][:, 4:8], LPs[0][:, 0:4], LPs[1][:, 0:4],
              CAP(L4[0][:], 0, [[0, 4], [2, 16], [1, 2]]), CAP(L4[1][:], 0, [[0, 4], [2, 16], [1, 2]]),
              t8[0][:], t8[1][:], [RL8], [RL8])
        P.op("dve", lambda e: e.tensor_copy(LPs[0][:, 8], L8[0][:]), reads=[RL8], writes=[RL8])
        P.op("dve", lambda e: e.tensor_copy(LPs[1][:, 8], L8[1][:]), reads=[RL8], writes=[RL8])
        A1 = P.sbuf("A1", [128, 2, 2, 16], F32)
        A2 = P.sbuf("A2", [128, 2, 2, 16], F32)
        L8v = [CAP(L8[i][:], 0, [[1, 2], [2, 16]]) for i in range(2)]
        for ri in range(2):
            P.op("dve", lambda e, ri=ri: e.tensor_copy(A1[:, ri], L8v[0]), reads=[RL8], writes=[RL8])
            P.op("dve", lambda e, ri=ri: e.tensor_scalar_mul(A2[:, ri], L8v[1], -1.0 if ri == 0 else 1.0), reads=[RL8], writes=[RL8])
        Lh = P.sbuf("Lh", [128, 2, 2, 16], F32)
        h0v = [CAP(h0t[:, l], ri, [[2, 2], [4, 16]]) for ri in range(2)]
        lt = [P.sbuf(f"lht{i}", [128, 2, 16], F32) for i in range(2)]
        cmul("dve", Lh[:, 0], Lh[:, 1], L8v[0], L8v[1], h0v[0], h0v[1], lt[0][:], lt[1][:], [RL8, RC], [RL8])
        h0b = P.sbuf("h0b", [128, 2, 2, 16], BF16)
        for ri in range(2):
            P.op("dve", lambda e, ri=ri: e.tensor_copy(h0b[:, ri], h0v[ri]), reads=[RC], writes=[RL8])

        ck("m_tab", l)
        P.scope_begin()
        Ec = [P.sbuf(f"Ec{i}", [128, 8, 2, 64], F32) for i in range(2)]
        tmpf = [P.sbuf(f"tmpf{i}", [128, 576], F32) for i in range(2)]
        et = [tmpf[i][:, 0:512].rearrange("p (a b c) -> p a b c", a=4, b=2, c=64) for i in range(2)]
        gt = [tmpf[i][:, 0:576].rearrange("p (a b c) -> p a b c", a=9, b=4, c=16) for i in range(2)]
        RPw = Res("Ec", dyn=True)
        EFe = P.sbuf("EFe", [128, 4096], BF16)
        Ebuf = EFe[:].rearrange("p (i d r g q) -> p i d r g q", i=8, d=2, r=2, g=2, q=64)
        RE = Res("EFe", dyn=True)
        RG = RPw
        Gbb = [P.sbuf(f"Gb{i}", [128, 2, 4, 2, 9, 16], BF16) for i in range(2)]
        RGbb = [Res(f"Gb{i}", dyn=True) for i in range(2)]
        EFf = P.sbuf("EFf", [128, 4096], BF16)
        EF = EFf
        Fbuf = EFf[:].rearrange("p (a j d r g h) -> p a j d r g h", a=4, j=8, d=2, r=2, g=2, h=16)
        RF = Res("EFf", dyn=True)
        Bpad = P.sbuf("Bpad", [128, 4, 2, 2, 8, 16], BF16)
        RBp = Res("Bpad", dyn=True)
        Kblk = P.sbuf("Kblk", [128, 2, 8, 128], BF16)
        RK = Res("Kblk", dyn=True)
        XS = P.sbuf("XS", [128, 2, 2, 4, 9, 32], F32)
        RXd = [Res("XS0", dyn=True), Res("XS1", dyn=True)]
        st1 = P.sbuf("st1", [128, 2, 2, 4, 9], F32)
        st2 = P.sbuf("st2", [128, 2, 2, 4, 9], F32)
        fx1 = P.sbuf("fx1", [128, 2, 4, 32], F32)
        fx2 = P.sbuf("fx2", [128, 2, 4, 32], F32)
        cs = P.sbuf("cs", [128, 2, 4], F32)
        Sin = P.sbuf("Sin", [128, 2, 2, 4, 256], BF16)
        RSin = Res("Sin", dyn=True)
        XSZ = 2 * 4 * 9 * 32
        SEA = "pool"

        def seg_A1(ct):
            Gb, RGb = Gbb[ct % 2], RGbb[ct % 2]
            for ri in range(2):
                P.op(SEA, lambda e, ri=ri, ct=ct: e.tensor_copy(Ec[ri][:, 0], Bc[ri][:, ct]), reads=[RBb], writes=[RPw])
            for k in range(1, 4):
                cmul6(SEA, Ec[0][:, k], Ec[1][:, k], Ec[0][:, k - 1], Ec[1][:, k - 1], TC["lr"][:, ct], TC["li"][:, ct],
                      et[0][:, 0], et[1][:, 0], [RPw, RTC], [RPw])
            cmul6(SEA, Ec[0][:, 4:8], Ec[1][:, 4:8], Ec[0][:, 0:4], Ec[1][:, 0:4],
                  CAP(L4c[0][:, ct], 0, [[0, 4], [64, 2], [1, 64]]), CAP(L4c[1][:, ct], 0, [[0, 4], [64, 2], [1, 64]]),
                  et[0][:], et[1][:], [RPw, RTC], [RPw])
            for dr in range(2):
                for ri in range(2):
                    o_ap = CAP(EFe[:, 0:1], (7 * 512 if dr == 0 else 0) + dr * 256 + ri * 128,
                               [[-512 if dr == 0 else 512, 8], [64, 2], [1, 64]])
                    P.op(SEA, lambda e, o_ap=o_ap, dr=dr, ri=ri: e.tensor_tensor(
                        o_ap, CAP(Ec[ri][:, 0, dr, 0:1], 0, [[128, 8], [0, 2], [1, 64]]),
                        CAP(maskE[:], 0, [[0, 8], [1, 2], [0, 64]]), ALU.mult), reads=[RPw, RC], writes=[RE])

        def seg_A2(ct):
            Gb, RGb = Gbb[ct % 2], RGbb[ct % 2]
            for dr in range(2):
                Cr = CAP(csl[:, 0, ct * 4, dr, 0:1], 0, [[0, 9], [32, 4], [1, 16]])
                Ci = CAP(csl[:, 1, ct * 4, dr, 0:1], 0, [[0, 9], [32, 4], [1, 16]])
                Pr = CAP(LPs[0][:, 0, ct * 4, dr:dr + 1], 0, [[32, 9], [2, 4], [0, 16]])
                Pi = CAP(LPs[1][:, 0, ct * 4, dr:dr + 1], 0, [[32, 9], [2, 4], [0, 16]])
                o_r = CAP(Gb[:, 0, 0, dr, 0, 0:1], 0, [[16, 9], [288, 4], [1, 16]])
                o_i = CAP(Gb[:, 1, 0, dr, 0, 0:1], 0, [[16, 9], [288, 4], [1, 16]])
                rr, ww = [RG, RL8, Rpar], [RG]
                P.op(SE, lambda e, Cr=Cr, Pr=Pr: e.tensor_tensor(gt[0][:], Cr, Pr, ALU.mult), reads=rr, writes=ww)
                P.op(SE, lambda e, Ci=Ci, Pi=Pi: e.tensor_tensor(gt[1][:], Ci, Pi, ALU.mult), reads=rr, writes=ww)
                P.op(SE, lambda e, o_r=o_r: e.tensor_tensor(o_r, gt[0][:], gt[1][:], ALU.subtract), reads=[RG], writes=[RG, RGb])
                P.op(SE, lambda e, Cr=Cr, Pi=Pi: e.tensor_tensor(gt[0][:], Cr, Pi, ALU.mult), reads=rr, writes=ww)
                P.op(SE, lambda e, Ci=Ci, Pr=Pr: e.tensor_tensor(gt[1][:], Ci, Pr, ALU.mult), reads=rr, writes=ww)
                P.op(SE, lambda e: e.tensor_tensor(gt[0][:], gt[0][:], gt[1][:], ALU.add), reads=[RG], writes=[RG])
                P.op(SE, lambda e, o_i=o_i: e.tensor_scalar_mul(o_i, gt[0][:], -1.0), reads=[RG], writes=[RG, RGb])
            for gpl in range(4):
                for dr in range(2):
                    bk, rb = bank()
                    for ri in range(2):
                        for i in range(8):
                            P.op("pe", lambda e, bk=bk, gpl=gpl, dr=dr, ri=ri, i=i, ct=ct: e.matmul(
                                bk[:, ri * 256:(ri + 1) * 256],
                                lhsT=Ebuf[32 * gpl:32 * gpl + 32, i, dr, ri].rearrange("p a b -> p (a b)"),
                                rhs=CAP(U[32 * gpl:32 * gpl + 32, ct, 0:1], i, [[8, 256]]),
                                start=(i == 0), stop=(i == 7), tile_position=(32 * gpl, 0)),
                                reads=[RE] + RU[ct], writes=[rb])
                    P.op("act", lambda e, bk=bk, gpl=gpl, dr=dr: e.activation(
                        CAP(XS[:, 0, dr, gpl, 0, 0:1], 0, [[XSZ, 2], [1, 256]]),
                        bk[:, :].rearrange("p (r c) -> p r c", r=2), AF.Copy), reads=[rb], writes=[RXd[dr]])
            P.op(SE, lambda e: e.memset(XS[:, :, :, :, 8, :], 0.0), writes=RXd)
            for ri in range(2):
                for dr in range(2):
                    kk = 0 if dr == 0 else 31
                    P.op(SE, lambda e, ri=ri, dr=dr, kk=kk, ct=ct: e.tensor_copy(
                        XS[:, ri, dr, :, 8, kk], CAP(L8[ri][:, ct * 4, dr:dr + 1], 0, [[2, 4]])), reads=[RL8], writes=[RXd[dr]])
            for dr in range(2):
                sgm, kk = (4, 0) if dr == 0 else (7, 31)
                P.op("dve", lambda e, dr=dr, sgm=sgm, kk=kk, ct=ct: e.tensor_tensor(
                    XS[:, :, dr, :, sgm, kk], XS[:, :, dr, :, sgm, kk], Lh[:, :, dr, ct * 4:ct * 4 + 4], ALU.add),
                    reads=[RXd[dr], RL8], writes=[RXd[dr]])

        def seg_S(ct):
            dstride = 4 * 9 * 32
            for k in range(1, 32):
                cur_ap = CAP(XS[:, 0, 0, 0, 0, 0:1], k, [[XSZ, 2], [dstride + 31 - 2 * k, 2], [288, 4], [32, 9]])
                prv_ap = CAP(XS[:, 0, 0, 0, 0, 0:1], k - 1, [[XSZ, 2], [dstride + 33 - 2 * k, 2], [288, 4], [32, 9]])
                prv_sw = CAP(XS[:, 0, 0, 0, 0, 0:1], XSZ + k - 1, [[-XSZ, 2], [dstride + 33 - 2 * k, 2], [288, 4], [32, 9]])
                a1 = CAP(A1[:, 0, 0, ct * 4:ct * 4 + 1], 0, [[32, 2], [16, 2], [1, 4], [0, 9]])
                a2 = CAP(A2[:, 0, 0, ct * 4:ct * 4 + 1], 0, [[32, 2], [16, 2], [1, 4], [0, 9]])
                P.op("dve", lambda e, prv_ap=prv_ap, a1=a1: e.tensor_tensor(st1[:], prv_ap, a1, ALU.mult), reads=RXd + [RL8], writes=RXd)
                P.op("dve", lambda e, prv_sw=prv_sw, a2=a2: e.tensor_tensor(st2[:], prv_sw, a2, ALU.mult), reads=RXd + [RL8], writes=RXd)
                P.op("dve", lambda e, cur_ap=cur_ap: e.tensor_tensor(cur_ap, cur_ap, st1[:], ALU.add), reads=RXd, writes=RXd)
                P.op("dve", lambda e, cur_ap=cur_ap: e.tensor_tensor(cur_ap, cur_ap, st2[:], ALU.add), reads=RXd, writes=RXd)
            for dr in range(2):
                order = (5, 6, 7) if dr == 0 else (6, 5, 4)
                for sgm in order:
                    src_seg, src_k = (sgm - 1, 31) if dr == 0 else (sgm + 1, 0)
                    P.op("dve", lambda e, dr=dr, src_seg=src_seg, src_k=src_k: e.tensor_scalar_mul(
                        cs[:, 0], XS[:, 1, dr, :, src_seg, src_k], -1.0), reads=[RXd[dr]], writes=[RXd[dr]])
                    P.op("dve", lambda e, dr=dr, src_seg=src_seg, src_k=src_k: e.tensor_copy(
                        cs[:, 1], XS[:, 1, dr, :, src_seg, src_k]), reads=[RXd[dr]], writes=[RXd[dr]])
                    pw_ap = CAP(XS[:, 0, dr, 0, 8, 0:1], 0, [[XSZ, 2], [288, 4], [1, 32]])
                    pw_sw = CAP(XS[:, 0, dr, 0, 8, 0:1], XSZ, [[-XSZ, 2], [288, 4], [1, 32]])
                    tgt = CAP(XS[:, 0, dr, 0, sgm, 0:1], 0, [[XSZ, 2], [288, 4], [1, 32]])
                    c_r = CAP(XS[:, 0, dr, 0, src_seg, src_k:src_k + 1], 0, [[0, 2], [288, 4], [0, 32]])
                    c_s = CAP(cs[:, 0, 0:1], 0, [[4, 2], [1, 4], [0, 32]])
                    P.op("dve", lambda e, pw_ap=pw_ap, c_r=c_r: e.tensor_tensor(fx1[:], pw_ap, c_r, ALU.mult), reads=[RXd[dr]], writes=[RXd[dr]])
                    P.op("dve", lambda e, pw_sw=pw_sw, c_s=c_s: e.tensor_tensor(fx2[:], pw_sw, c_s, ALU.mult), reads=[RXd[dr]], writes=[RXd[dr]])
                    P.op("dve", lambda e, tgt=tgt: e.tensor_tensor(tgt, tgt, fx1[:], ALU.add), reads=[RXd[dr]], writes=[RXd[dr]])
                    P.op("dve", lambda e, tgt=tgt: e.tensor_tensor(tgt, tgt, fx2[:], ALU.add), reads=[RXd[dr]], writes=[RXd[dr]])
            for dr in range(2):
                kk = 31 if dr == 0 else 0
                P.op("dve", lambda e, dr=dr, kk=kk, ct=ct: e.tensor_copy(
                    CAP(nsbuf[:, l, 0, ct * 4, dr, 0:1], 0, [[1, 2], [4, 4], [64, 4]]),
                    CAP(XS[:, 0, dr, 0, 0, kk:kk + 1], 0, [[XSZ, 2], [288, 4], [32, 4]])), reads=[RXd[dr]], writes=[Rns])

        def seg_Z1(ct):
            P.op(SE, lambda e: e.memset(Sin[:], 0.0), writes=[RSin])
            P.op("dve", lambda e: e.tensor_copy(
                CAP(Sin[:, 0, 0, 0, 0:1], 1, [[2048, 2], [256, 4], [32, 4], [1, 31]]),
                CAP(XS[:, 0, 0, 0, 0, 0:1], 0, [[XSZ, 2], [288, 4], [32, 4], [1, 31]])), reads=[RXd[0]], writes=[RSin])
            P.op("dve", lambda e: e.tensor_copy(
                CAP(Sin[:, 0, 0, 0, 0:1], 129, [[2048, 2], [256, 4], [1, 127]]),
                CAP(XS[:, 0, 0, 0, 4, 0:1], 0, [[XSZ, 2], [288, 4], [1, 127]])), reads=[RXd[0]], writes=[RSin])
            P.op("dve", lambda e: e.tensor_copy(
                CAP(Sin[:, 0, 1, 0, 0:1], 0, [[2048, 2], [256, 4], [32, 4], [1, 31]]),
                CAP(XS[:, 0, 1, 0, 0, 0:1], 1, [[XSZ, 2], [288, 4], [32, 4], [1, 31]])), reads=[RXd[1]], writes=[RSin])
            P.op("dve", lambda e: e.tensor_copy(
                CAP(Sin[:, 0, 1, 0, 0:1], 128, [[2048, 2], [256, 4], [1, 127]]),
                CAP(XS[:, 0, 1, 0, 4, 0:1], 1, [[XSZ, 2], [288, 4], [1, 127]])), reads=[RXd[1]], writes=[RSin])
            P.op("dve", lambda e, ct=ct: e.tensor_copy(Sin[:, :, 0, :, 128], h0b[:, :, 0, ct * 4:ct * 4 + 4]), reads=[RL8], writes=[RSin])
            P.op("dve", lambda e, ct=ct: e.tensor_copy(Sin[:, :, 1, :, 255], h0b[:, :, 1, ct * 4:ct * 4 + 4]), reads=[RL8], writes=[RSin])

        def seg_Z2(ct):
            Gb, RGb = Gbb[ct % 2], RGbb[ct % 2]
            for dr in range(2):
                for ri in range(2):
                    P.op("pool", lambda e, dr=dr, ri=ri, ct=ct: e.tensor_tensor(
                        Bpad[:, :, dr, ri],
                        CAP(Bs[ri][:, ct * 4, dr, :], 0, [[32, 4], [0, 8], [1, 16]]),
                        CAP(mask3[:], 0, [[8, 4], [1, 8], [0, 16]]), ALU.mult), reads=[RBb, RC], writes=[RBp])
            for dr in range(2):
                for jh in range(2):
                    bks = [bank(), bank()]
                    for gl in range(8):
                        gpl, g2 = gl // 2, gl % 2
                        bk, rb = bks[g2]
                        for ri in range(2):
                            P.op("pe", lambda e, bk=bk, gl=gl, gpl=gpl, g2=g2, ri=ri, dr=dr, jh=jh: e.matmul(
                                CAP(bk[:, 0:1], gl * 16, [[128, 4], [1, 16]]),
                                lhsT=Bpad[64 * g2:64 * g2 + 64, gpl, dr, ri].rearrange("p a b -> p (a b)"),
                                rhs=Gb[64 * g2:64 * g2 + 64, ri, gpl, dr, jh * 4:jh * 4 + 4, :],
                                start=(ri == 0), stop=(ri == 1), tile_position=(64 * g2, 0)),
                                reads=[RBp, RGb], writes=[rb])
                    for g2 in range(2):
                        bk, rb = bks[g2]
                        P.op("act", lambda e, bk=bk, dr=dr, jh=jh, g2=g2: e.activation(
                            CAP(Kblk[:, dr, jh * 4, 0:1], g2 * 16, [[128, 4], [32, 4], [1, 16]]),
                            CAP(bk[:, 0:1], g2 * 16, [[128, 4], [32, 4], [1, 16]]), AF.Copy), reads=[rb], writes=[RK])
            P.op("dve", lambda e, ct=ct: e.scalar_tensor_tensor(
                out=Kblk[:, 0, 0, :], in0=ident[:], scalar=s5d[:, l, ct:ct + 1], in1=Kblk[:, 0, 0, :],
                op0=ALU.mult, op1=ALU.add), reads=[RK, RC], writes=[RK])
            for dr in range(2):
                for ri in range(2):
                    for gpl in range(4):
                        P.op("pool", lambda e, dr=dr, ri=ri, gpl=gpl: e.tensor_tensor(
                            CAP(EFf[:, 0:1], gpl * 1024 + dr * 64 + ri * 32, [[128, 8], [16, 2], [1, 16]]),
                            CAP(Gb[:, ri, gpl, dr, 1, 0:1], 0, [[16, 8], [0, 2], [1, 16]]),
                            CAP(mask2[:], 0, [[0, 8], [1, 2], [0, 16]]), ALU.mult), reads=[RGb, RC], writes=[RF])
            for tb in range(NTB):
                ts = slice(tb * TB, (tb + 1) * TB)
                bk, rb = bank()
                bkv = bk[:, :].rearrange("p (c i) -> p c i", i=8)
                uv = U[:, ct, ts].rearrange("p (c i) -> p c i", i=8)
                for dr in range(2):
                    for j in range(8):
                        if dr == 0:
                            o_ap, r_ap = bkv[:, :, j:8], uv[:, :, 0:8 - j]
                        else:
                            o_ap, r_ap = bkv[:, :, 0:8 - j], uv[:, :, j:8]
                        P.op("pe", lambda e, o_ap=o_ap, r_ap=r_ap, dr=dr, j=j: e.matmul(
                            o_ap, lhsT=Kblk[:, dr, j, :], rhs=r_ap, start=(dr == 0 and j == 0), stop=False),
                            reads=[RK, RU[ct][tb]], writes=[rb])
                last = (3, 7, 1, 1)
                for gpl in range(4):
                    for j in range(8):
                        for dr in range(2):
                            for ri in range(2):
                                pos = j if dr == 0 else 7 - j
                                P.op("pe", lambda e, bk=bk, gpl=gpl, j=j, dr=dr, ri=ri, pos=pos, tb=tb: e.matmul(
                                    CAP(bk[32 * gpl:32 * gpl + 32, 0:1], pos, [[8, 64]]),
                                    lhsT=Fbuf[:, gpl, j, dr, ri].rearrange("p a b -> p (a b)"),
                                    rhs=Sin[:, ri, dr, gpl, tb * 64:(tb + 1) * 64],
                                    start=False, stop=((gpl, j, dr, ri) == last), tile_position=(0, 32 * gpl)),
                                    reads=[RF, RSin], writes=[rb])
                P.op("act", lambda e, bk=bk, ct=ct, ts=ts: e.activation(yg[:, ct, ts], bk[:, :], GELU), reads=[rb], writes=[RYG[ct][tb]])

        seg_A1(0)
        seg_A2(0)
        for ct in range(4):
            if ct < 3:
                seg_A1(ct + 1)
            seg_S(ct)
            seg_Z1(ct)
            if ct < 3:
                seg_A2(ct + 1)
            seg_Z2(ct)
        P.scope_end()
        if "yg" in dbg:
            P.scope_begin()
            for ct in range(4):
                tmpd = P.sbuf(f"dbgyg{ct}", [128, NT], F32)
                Rd = Res(f"dbgyg{ct}", dyn=True)
                P.op("dve", lambda e, ct=ct, tmpd=tmpd: e.tensor_copy(tmpd[:], yg[:, ct, :]), reads=RYG[ct], writes=[Rd])
                dump(f"yg{ct}", tmpd[:], [128, NT], [Rd])
            P.scope_end()
        wglu = P.sbuf("wglu", [128, 4, 512], BF16)
        Rwglu = Res("wglu", dyn=True)
        P.dma("pool", wglu[:], I["s5_w_glu"][l].rearrange("(k p) c -> p k c", p=128), wslot("wglu"), writes=[Rwglu])
        sgl = [P.sbuf(f"sgl{i}", [128, TB], F32) for i in range(2)]
        Rsgl = [Res(f"sgl{i}", dyn=True) for i in range(2)]
        for tb in range(NTB):
            ts = slice(tb * TB, (tb + 1) * TB)
            for co in range(4):
                bk, rb = bank()
                for kt in range(4):
                    P.op("pe", lambda e, bk=bk, co=co, kt=kt, ts=ts: e.matmul(
                        bk[:, :], lhsT=wglu[:, kt, co * 128:(co + 1) * 128], rhs=yg[:, kt, ts], start=(kt == 0), stop=(kt == 3)),
                        reads=[Rwglu, RYG[kt][tb]], writes=[rb])
                s = co % 2
                P.op("act", lambda e, bk=bk, s=s: e.activation(sgl[s][:], bk[:, :], AF.Sigmoid), reads=[rb], writes=[Rsgl[s]])
                P.op("dve", lambda e, s=s, co=co, ts=ts: e.tensor_tensor(y_a[:, co, ts], yg[:, co, ts], sgl[s][:], ALU.mult),
                     reads=[Rsgl[s], RYG[co][tb]], writes=[RYA[co][tb]])
        P.scope_end()
        ck("m_s5", l)

        y_b = P.sbuf("y_b", [128, 2, NT], BF16)
        RYB = [[Res(f"yb{c}_{t}", dyn=True) for t in range(NTB)] for c in range(2)]
        y_c = P.sbuf("y_c", [128, 2, NT], BF16)
        RYC = [[Res(f"yc{c}_{t}", dyn=True) for t in range(NTB)] for c in range(2)]
        hT, RH = alloc_hT()
        norm_stage(l, 1, hT, RH, own_scope=True)
        P.scope_begin()
        gpad = P.sbuf("gpad", [128, 2, GP_LEN], BF16)
        Rgp = [[Res(f"gp{j}_{t}", dyn=True) for t in range(NTB)] for j in range(2)]
        u_c = P.sbuf("u_c", [128, 2, NT], BF16)
        RUC = [[Res(f"uc{j}_{t}", dyn=True) for t in range(NTB)] for j in range(2)]
        vnb = P.sbuf("vnb", [128, 16, 256], BF16)
        RV = [Res(f"vn{t}", dyn=True) for t in range(16)]
        P.op("pool", lambda e: e.memset(gpad[:], 0.0), writes=[Rgp[j][t] for j in range(2) for t in range(NTB)])
        if l == 0:
            MODG["gen"] = mod_gen(1, *mod_bufs())
        P.scope_begin()
        winb = P.sbuf("winb", [128, 8, 1024], BF16)
        Rwinb = Res("winb", dyn=True)
        P.dma("pool", winb[:, :, 0:512], I["w_in"][l, :, 512:1024].rearrange("(k p) c -> p k c", p=128), wslot("winb"), writes=[Rwinb])
        P.dma("pool", winb[:, :, 512:1024], I["w_in"][l, :, 1024:1536].rearrange("(k p) c -> p k c", p=128), wslot("winb"), writes=[Rwinb])
        sgln = P.sbuf("sgln", [128, 2, 256], F32)
        Rsgl_ = Res("sgln", dyn=True)
        P.dma("sp", sgln[:].rearrange("p a b -> p (a b)"), I["sg_ln"][l].rearrange("a b -> (a b)").partition_broadcast(128), wslot(f"sgln{l}"), writes=[Rsgl_])
        sb = [P.sbuf(f"sb{i}", [128, TB], F32) for i in range(2)]
        Rsb = [Res(f"sb{i}", dyn=True) for i in range(2)]

        def gp_ap(j, tb, k):
            if tb < 2:
                return CAP(gpad[:, j, 0:1], GP_OFF[2 * tb] + k, [[286, 2], [1, 256]])
            return CAP(gpad[:, j, 0:1], GP_OFF[4] + (tb - 2) * 512 + k, [[1, 512]])

        for j in range(2):
            for tb in range(NTB):
                ts = slice(tb * TB, (tb + 1) * TB)
                pa, ra = bank()
                pb2, rb2 = bank()
                for which, (pp, rr) in enumerate(((pa, ra), (pb2, rb2))):
                    c0 = which * 256 + j * 128
                    for kt in range(8):
                        P.op("pe", lambda e, pp=pp, c0=c0, kt=kt, ts=ts: e.matmul(
                            pp[:, :], lhsT=winb[:, kt, c0:c0 + 128], rhs=hT[:, kt, ts], start=(kt == 0), stop=(kt == 7)),
                            reads=[Rwinb, RH[kt][tb]], writes=[rr])
                mod_step()
                s = tb % 2
                P.op("act", lambda e, pb2=pb2, s=s: e.activation(sb[s][:], pb2[:, :], AF.Sigmoid), reads=[rb2], writes=[Rsb[s]])
                o_ap = gp_ap(j, tb, 15)
                i_ap = pa[:, :].rearrange("p (s t) -> p s t", s=2) if tb < 2 else pa[:, :]
                s_ap = sb[s][:].rearrange("p (s t) -> p s t", s=2) if tb < 2 else sb[s][:]
                P.op("dve", lambda e, o_ap=o_ap, i_ap=i_ap, s_ap=s_ap: e.tensor_tensor(o_ap, i_ap, s_ap, ALU.mult),
                     reads=[ra, Rsb[s]], writes=[Rgp[j][tb]])
        for j in range(2):
            for tb in range(NTB):
                ts = slice(tb * TB, (tb + 1) * TB)
                bk, rb = bank()
                c0 = 512 + j * 128
                for kt in range(8):
                    P.op("pe", lambda e, bk=bk, c0=c0, kt=kt, ts=ts: e.matmul(
                        bk[:, :], lhsT=winb[:, kt, c0:c0 + 128], rhs=hT[:, kt, ts], start=(kt == 0), stop=(kt == 7)),
                        reads=[Rwinb, RH[kt][tb]], writes=[rb])
                mod_step()
                P.op("act", lambda e, bk=bk, j=j, ts=ts: e.activation(u_c[:, j, ts], bk[:, :], GELU), reads=[rb], writes=[RUC[j][tb]])
        vg = [P.sbuf(f"vg{i}", [128, 256], F32) for i in range(2)]
        Rvg = [Res(f"vg{i}", dyn=True) for i in range(2)]
        bst = [P.sbuf(f"bst{i}", [128, 6], F32) for i in range(2)]
        bag = [P.sbuf(f"bag{i}", [128, 2], F32) for i in range(2)]
        for tt in range(16):
            tb = tt // 4
            tks = slice(tt * 128, (tt + 1) * 128)
            bk, rb = bank()
            for kt in range(8):
                P.op("pe", lambda e, bk=bk, kt=kt, tks=tks: e.matmul(
                    bk[:, 0:256], lhsT=hT[:, kt, tks], rhs=winb[:, kt, 768:1024], start=(kt == 0), stop=(kt == 7)),
                    reads=[Rwinb, RH[kt][tb]], writes=[rb])
            mod_step()
            s = tt % 2
            P.op("act", lambda e, bk=bk, s=s: e.activation(vg[s][:], bk[:, 0:256], GELU), reads=[rb], writes=[Rvg[s]])
            P.op("dve", lambda e, s=s: e.bn_stats(bst[s][:], vg[s][:]), reads=[Rvg[s]], writes=[Rvg[s]])
            P.op("dve", lambda e, s=s: e.bn_aggr(bag[s][:], bst[s][:]), reads=[Rvg[s]], writes=[Rvg[s]])
            P.op("act", lambda e, s=s: e.activation(bag[s][:, 1:2], bag[s][:, 1:2], AF.Sqrt, bias=eps_t[:, 0:1], scale=1.0),
                 reads=[Rvg[s], RC], writes=[Rvg[s]])
            P.op("dve", lambda e, s=s: e.reciprocal(bag[s][:, 1:2], bag[s][:, 1:2]), reads=[Rvg[s]], writes=[Rvg[s]])
            P.op("dve", lambda e, s=s: e.tensor_scalar(vg[s][:], vg[s][:], bag[s][:, 0:1], bag[s][:, 1:2], ALU.subtract, ALU.mult),
                 reads=[Rvg[s]], writes=[Rvg[s]])
            P.op("dve", lambda e, s=s: e.tensor_tensor(vg[s][:], vg[s][:], sgln[:, 0, :], ALU.mult), reads=[Rvg[s], Rsgl_], writes=[Rvg[s]])
            P.op("dve", lambda e, s=s, tt=tt: e.tensor_tensor(vnb[:, tt, :], vg[s][:], sgln[:, 1, :], ALU.add),
                 reads=[Rvg[s], Rsgl_], writes=[RV[tt]])
        P.scope_end()
        ck("m_p1", l)
        if "gpad" in dbg:
            P.scope_begin()
            for j in range(2):
                tmpd = P.sbuf(f"dbggp{j}", [128, GP_LEN], F32)
                Rd = Res(f"dbggp{j}", dyn=True)
                P.op("dve", lambda e, j=j, tmpd=tmpd: e.tensor_copy(tmpd[:], gpad[:, j, :]), reads=Rgp[j], writes=[Rd])
                dump(f"gp{j}", tmpd[:], [128, GP_LEN], [Rd])
            P.scope_end()
        P.scope_begin()
        dg = P.sbuf("dg", [128, 2, 31, 128], BF16)
        Rdg = Res("dg", dyn=True)
        for j in range(2):
            for k in range(31):
                P.op("pool", lambda e, j=j, k=k: e.tensor_scalar_mul(dg[:, j, k, :], ident[:], convw[:, l, j, k:k + 1]),
                     reads=[RC], writes=[Rdg])
        ycf = P.sbuf("ycf", [128, 2, TB], F32)
        ysq = P.sbuf("ysq", [128, 2, TB], F32)
        Rycf = [Res(f"ycf{j}", dyn=True) for j in range(2)]
        Rysq = [Res(f"ysq{j}", dyn=True) for j in range(2)]
        mean_s = P.sbuf("mean_s", [128, TB], F32)
        var_s = P.sbuf("var_s", [128, TB], F32)
        Rms = Res("mean_s", dyn=True)
        Rvs = Res("var_s", dyn=True)
        dtm = [P.sbuf(f"dtm{i}", [128, TB], F32) for i in range(2)]
        Rdtm = [Res(f"dtm{i}", dyn=True) for i in range(2)]
        for tb in range(NTB):
            ts = slice(tb * TB, (tb + 1) * TB)
            for j in range(2):
                bk, rb = bank()
                o_ap = bk[:, :].rearrange("p (s t) -> p s t", s=2) if tb < 2 else bk[:, :]
                for k in range(31):
                    P.op("pe", lambda e, o_ap=o_ap, j=j, k=k, tb=tb: e.matmul(
                        o_ap, lhsT=dg[:, j, k, :], rhs=gp_ap(j, tb, k), start=(k == 0), stop=(k == 30)),
                        reads=[Rdg, Rgp[j][tb]] + ([Rgp[j][tb - 1]] if tb == 3 else []) + ([Rgp[j][tb + 1]] if tb == 2 else []),
                        writes=[rb])
                P.op("act", lambda e, bk=bk, j=j: e.activation(ycf[:, j, :], bk[:, :], AF.Identity, bias=convv[:, l, 0, j:j + 1], scale=1.0),
                     reads=[rb, RC], writes=[Rycf[j]])
                P.op("act", lambda e, bk=bk, j=j: e.activation(ysq[:, j, :], bk[:, :], AF.Square, bias=convv[:, l, 0, j:j + 1], scale=1.0),
                     reads=[rb, RC], writes=[Rysq[j]])
            bm, rbm = bank()
            bq, rbq = bank()
            for j in range(2):
                P.op("pe", lambda e, bm=bm, j=j: e.matmul(bm[:, :], lhsT=ones256[:], rhs=ycf[:, j, :], start=(j == 0), stop=(j == 1)),
                     reads=[Rycf[j], RC], writes=[rbm])
            for j in range(2):
                P.op("pe", lambda e, bq=bq, j=j: e.matmul(bq[:, :], lhsT=ones256[:], rhs=ysq[:, j, :], start=(j == 0), stop=(j == 1)),
                     reads=[Rysq[j], RC], writes=[rbq])
            P.op("act", lambda e, bm=bm: e.activation(mean_s[:], bm[:, :], AF.Copy), reads=[rbm], writes=[Rms])
            P.op("dve", lambda e: e.tensor_tensor(var_s[:], mean_s[:], mean_s[:], ALU.mult), reads=[Rms], writes=[Rvs])
            P.op("dve", lambda e, bq=bq: e.tensor_tensor(var_s[:], bq[:, :], var_s[:], ALU.subtract), reads=[rbq, Rvs], writes=[Rvs])
            P.op("act", lambda e: e.activation(var_s[:], var_s[:], AF.Sqrt, bias=eps_t[:, 0:1], scale=1.0), reads=[Rvs, RC], writes=[Rvs])
            P.op("dve", lambda e: e.reciprocal(var_s[:], var_s[:]), reads=[Rvs], writes=[Rvs])
            for j in range(2):
                P.op("dve", lambda e, j=j: e.tensor_tensor(dtm[j][:], ycf[:, j, :], mean_s[:], ALU.subtract), reads=[Rycf[j], Rms], writes=[Rdtm[j]])
                P.op("dve", lambda e, j=j: e.tensor_tensor(dtm[j][:], dtm[j][:], var_s[:], ALU.mult), reads=[Rdtm[j], Rvs], writes=[Rdtm[j]])
                P.op("act", lambda e, j=j, ts=ts: e.activation(y_b[:, j, ts], dtm[j][:], AF.Silu, bias=convv[:, l, 2, j:j + 1],
                                                           scale=convv[:, l, 1, j:j + 1]), reads=[Rdtm[j], RC], writes=[RYB[j][tb]])
        P.scope_end()
        ck("m_p2", l)
        P.scope_begin()
        sgw = P.sbuf("sgw", [128, 4, 128], BF16)
        sgb = P.sbuf("sgb", [128, 2, 128], F32)
        Rsgp = Res("sgpar", dyn=True)
        P.dma("pool", sgw[:], I["sg_wT"][:, l], wslot("sgw"), writes=[Rsgp])
        P.dma("sp", sgb[:], I["sg_bT"][:, l], wslot(f"sgb{l}"), writes=[Rsgp])
        stm = [P.sbuf(f"stm{i}", [128, TB], F32) for i in range(2)]
        Rstm = [Res(f"stm{i}", dyn=True) for i in range(2)]
        for hp in range(2):
            for tb in range(NTB):
                ts = slice(tb * TB, (tb + 1) * TB)
                bk, rb = bank()
                for n4 in range(4):
                    tt = tb * 4 + n4
                    for h2 in range(2):
                        h = hp * 2 + h2
                        P.op("pe", lambda e, bk=bk, n4=n4, tt=tt, h2=h2, h=h: e.matmul(
                            bk[64 * h2:64 * h2 + 64, n4 * 128:(n4 + 1) * 128], lhsT=vnb[:, tt, h * 64:(h + 1) * 64], rhs=sgw[:, h, :],
                            start=True, stop=True, tile_position=(0, 64 * h2)), reads=[RV[tt], Rsgp], writes=[rb])
                s = tb % 2
                P.op("dve", lambda e, bk=bk, s=s, hp=hp: e.tensor_tensor(
                    stm[s][:].rearrange("p (n q) -> p n q", n=4), bk[:, :].rearrange("p (n q) -> p n q", n=4),
                    CAP(sgb[:, hp, :], 0, [[0, 4], [1, 128]]), ALU.add), reads=[rb, Rsgp], writes=[Rstm[s]])
                P.op("dve", lambda e, s=s, hp=hp, ts=ts: e.tensor_tensor(y_c[:, hp, ts], stm[s][:], u_c[:, hp, ts], ALU.mult),
                     reads=[Rstm[s], RUC[hp][tb]], writes=[RYC[hp][tb]])
        P.scope_end()
        while MODG["gen"] is not None:
            mod_step()
        P.scope_end()
        ck("m_p3", l)

        if "ybc" in dbg:
            P.scope_begin()
            for nm, buf, RR, n in (("ya", y_a, RYA, 4), ("yb", y_b, RYB, 2), ("yc", y_c, RYC, 2)):
                for ct in range(n):
                    tmpd = P.sbuf(f"dbg{nm}{ct}", [128, NT], F32)
                    Rd = Res(f"dbg{nm}{ct}", dyn=True)
                    P.op("dve", lambda e, ct=ct, tmpd=tmpd, buf=buf: e.tensor_copy(tmpd[:], buf[:, ct, :]), reads=RR[ct], writes=[Rd])
                    dump(f"{nm}{ct}", tmpd[:], [128, NT], [Rd])
            P.scope_end()

        P.scope_begin()
        mg = P.sbuf("mg", [128, 8, NT], BF16)
        RM = [[Res(f"mg{c}_{t}", dyn=True) for t in range(NTB)] for c in range(8)]
        wg = [P.sbuf(f"wg{i}", [128, 3, 8, 128], BF16) for i in range(2)]
        Rwg = [Res(f"wg{i}", dyn=True) for i in range(2)]
        wbr = [P.sbuf(f"wbr{i}", [128, 8, 128], BF16) for i in range(2)]
        Rwbr = [Res(f"wbr{i}", dyn=True) for i in range(2)]
        wo = [P.sbuf(f"wo{i}", [128, 8, 128], BF16) for i in range(2)]
        Rwo = [Res(f"wo{i}", dyn=True) for i in range(2)]
        gs = [P.sbuf(f"gs{i}", [128, TB], F32) for i in range(3)]
        Rgs = [Res(f"gs{i}", dyn=True) for i in range(3)]
        mt = [P.sbuf(f"mt{i}", [128, TB], F32) for i in range(3)]
        Rmt = [Res(f"mt{i}", dyn=True) for i in range(3)]
        ybufs = [(y_a, RYA, 4, 0), (y_b, RYB, 2, 4), (y_c, RYC, 2, 6)]
        for d in range(8):
            b = d % 2
            P.dma("pool", wg[b][:], I["wgate_t"][l, d], wslot(f"wg{b}"), writes=[Rwg[b]])
            P.dma("pool", wbr[b][:], I["wbr_t"][l, d], wslot(f"wbr{b}"), writes=[Rwbr[b]])
            for tb in range(NTB):
                ts = slice(tb * TB, (tb + 1) * TB)
                for br in range(3):
                    bg_, rg_ = bank()
                    for kt in range(8):
                        P.op("pe", lambda e, bg_=bg_, b=b, br=br, kt=kt, ts=ts: e.matmul(
                            bg_[:, :], lhsT=wg[b][:, br, kt, :], rhs=hT[:, kt, ts], start=(kt == 0), stop=(kt == 7)),
                            reads=[Rwg[b], RH[kt][tb]], writes=[rg_])
                    P.op("act", lambda e, bg_=bg_, br=br, d=d: e.activation(
                        gs[br][:], bg_[:, :], AF.Sigmoid, bias=bgate[:, l, br * 8 + d:br * 8 + d + 1], scale=1.0),
                        reads=[rg_, RC], writes=[Rgs[br]])
                    ybuf, RY, nk, k0 = ybufs[br]
                    bp_, rp_ = bank()
                    for kt in range(nk):
                        P.op("pe", lambda e, bp_=bp_, b=b, kt=kt, k0=k0, ybuf=ybuf, nk=nk, ts=ts: e.matmul(
                            bp_[:, :], lhsT=wbr[b][:, k0 + kt, :], rhs=ybuf[:, kt, ts], start=(kt == 0), stop=(kt == nk - 1)),
                            reads=[Rwbr[b], RY[kt][tb]], writes=[rp_])
                    P.op("dve", lambda e, bp_=bp_, br=br: e.tensor_tensor(mt[br][:], bp_[:, :], gs[br][:], ALU.mult),
                         reads=[rp_, Rgs[br]], writes=[Rmt[br]])
                P.op("pool", lambda e: e.tensor_tensor(mt[0][:], mt[0][:], mt[1][:], ALU.add), reads=[Rmt[1]], writes=[Rmt[0]])
                P.op("pool", lambda e, d=d, ts=ts: e.tensor_tensor(mg[:, d, ts], mt[0][:], mt[2][:], ALU.add),
                     reads=[Rmt[0], Rmt[2]], writes=[RM[d][tb]])
        for d in range(8):
            b = d % 2
            P.dma("pool", wo[b][:], I["wout_t"][l, d], wslot(f"wo{b}"), writes=[Rwo[b]])
            for tb in range(NTB):
                c = 0 if tb < 2 else 1
                ts = slice(tb * TB, (tb + 1) * TB)
                po, ro = bank()
                for kt in range(8):
                    P.op("pe", lambda e, po=po, b=b, kt=kt, ts=ts: e.matmul(
                        po[:, :], lhsT=wo[b][:, kt, :], rhs=mg[:, kt, ts], start=(kt == 0), stop=(kt == 7)),
                        reads=[Rwo[b], RM[kt][tb]], writes=[ro])
                P.op("dve", lambda e, po=po, d=d, ts=ts, c=c: e.scalar_tensor_tensor(
                    out=xT[:, d, ts], in0=po[:, :], scalar=gtv[:, l, 1, d, c:c + 1], in1=xT[:, d, ts], op0=ALU.mult, op1=ALU.add),
                    reads=[ro, Rmodl[l]], writes=[RX[d][tb]])
        P.scope_end()
        P.scope_end()

    def final_stage():
        P.scope_begin()
        sq, Rsq, rs, Rrs, tmp, Rtmp = norm_scratch()
        ot = [P.sbuf(f"ot{i}", [128, TB], F32) for i in range(4)]
        Rot = [Res(f"ot{i}", dyn=True) for i in range(4)]
        for tb in range(NTB):
            ts = slice(tb * TB, (tb + 1) * TB)
            for ct in range(8):
                P.op("act", lambda e, ct=ct, ts=ts: e.activation(sq[:, ct, :], xT[:, ct, ts], AF.Square), reads=[RX[ct][tb]], writes=[Rsq[ct]])
            bk, rb = bank()
            for ct in range(8):
                P.op("pe", lambda e, bk=bk, ct=ct: e.matmul(bk[:, :], lhsT=ones_bf[:], rhs=sq[:, ct, :], start=(ct == 0), stop=(ct == 7)),
                     reads=[Rsq[ct], RC], writes=[rb])
            r = tb % 2
            P.op("act", lambda e, bk=bk, r=r: e.activation(rs[r][:], bk[:, :], AF.Sqrt, bias=eps_t[:, 0:1], scale=1.0), reads=[rb, RC], writes=[Rrs[r]])
            P.op("dve", lambda e, r=r: e.reciprocal(rs[r][:], rs[r][:]), reads=[Rrs[r]], writes=[Rrs[r]])
            for ct in range(8):
                t = ct % 4
                P.op("dve", lambda e, ct=ct, ts=ts, t=t, r=r: e.scalar_tensor_tensor(
                    out=ot[t][:], in0=xT[:, ct, ts], scalar=finalg[:, ct:ct + 1], in1=rs[r][:], op0=ALU.mult, op1=ALU.mult),
                    reads=[RX[ct][tb], RC, Rrs[r]], writes=[Rot[t]])
                P.dma("sp", yT[ct * 128:(ct + 1) * 128, ts], ot[t][:], s_out, reads=[Rot[t]])
        P.dma("sp", nsd, nsbuf[:].rearrange("p a b c d e -> p (a b c d e)"), s_out, reads=[Rns])
        P.scope_end()

    def dump_x(tag):
        for ct in range(8):
            dump(f"{tag}_{ct}", xT[:, ct, :], [128, NT], RX[ct])

    done = False
    mod_stage(0)
    for l in range(2):
        if stop == "mod":
            break
        ffn_stage(l, 0, 0)
        if f"x1_{l}" in dbg:
            dump_x(f"x1_{l}")
        if stop == f"x1_{l}":
            done = True
            break
        if mixer_stage(l):
            done = True
            break
        if f"x2_{l}" in dbg:
            dump_x(f"x2_{l}")
        if stop == f"x2_{l}":
            done = True
            break
        ffn_stage(l, 1, 2)
        if f"x3_{l}" in dbg:
            dump_x(f"x3_{l}")
        if stop == f"x3_{l}":
            done = True
            break
    final_stage()
    P.emit(final_waits=[s_out])
    P.close()
    return nc, list(dbg_out)


def _grid_pos_embed_T():
    rows = 1024 // 64
    rr, cc = np.meshgrid(np.arange(rows, dtype=np.float32), np.arange(64, dtype=np.float32), indexing="ij")
    quarter = 256
    omega = (1.0 / (10000.0 ** (np.arange(quarter, dtype=np.float32) / np.float32(quarter)))).astype(np.float32)

    def emb(p):
        ang = p.reshape(-1)[:, None].astype(np.float32) * omega[None, :]
        return np.concatenate([np.sin(ang), np.cos(ang)], axis=-1)

    pe = np.concatenate([emb(rr), emb(cc)], axis=-1).astype(np.float32)
    return np.ascontiguousarray(pe.T)


def _prep_shared(inp):
    f = lambda a: np.ascontiguousarray(np.asarray(a, dtype=np.float32))
    S = {}
    S["pos"] = _grid_pos_embed_T()
    S["w_mod"] = f(inp["w_mod"])
    S["b_modT"] = f(inp["b_mod"].reshape(2, 72, 128).transpose(2, 0, 1))
    S["norm_gT"] = f(inp["norm_g"].reshape(2, 3, 8, 128).transpose(3, 0, 1, 2))
    S["final_gT"] = f(inp["final_g"].reshape(8, 128).T)
    w1 = inp["ffn_w1"].reshape(2, 2, 8, 128, 2, 22, 128)
    S["w1t"] = f(w1.transpose(0, 1, 5, 4, 3, 2, 6))
    S["ffn_w2"] = f(inp["ffn_w2"])
    S["w_in"] = f(inp["w_in"])
    wg = inp["w_gate"].reshape(2, 8, 128, 3, 8, 128)
    S["wgate_t"] = f(wg.transpose(0, 4, 2, 3, 1, 5))
    S["b_gateT"] = f(inp["b_gate"].reshape(2, 24, 128).transpose(2, 0, 1))
    wbr = np.concatenate([inp["w_br_a"], inp["w_br_b"], inp["w_br_c"]], axis=1)
    S["wbr_t"] = f(wbr.reshape(2, 8, 128, 8, 128).transpose(0, 3, 2, 1, 4))
    S["wout_t"] = f(inp["w_out"].reshape(2, 8, 128, 8, 128).transpose(0, 3, 2, 1, 4))
    a = np.stack([inp["s5_a_re"], inp["s5_a_im"]], axis=0)
    a6 = a.reshape(2, 2, 2, 16, 2, 64)
    S["a_sl"] = f(a6.transpose(4, 5, 1, 0, 3, 2).reshape(128, 2, 2, 16, 2))
    ld = inp["s5_log_dt"].reshape(2, 2, 16, 2)
    S["ldt_sl"] = f(np.broadcast_to(ld.transpose(3, 0, 2, 1)[:, None], (2, 64, 2, 16, 2)).reshape(128, 2, 16, 2))
    a7 = a.reshape(2, 2, 2, 4, 8, 64)
    S["a_cl"] = f(np.broadcast_to(a7.transpose(4, 1, 0, 3, 2, 5)[:, None], (8, 16, 2, 2, 4, 2, 64)).reshape(128, 2, 2, 4, 2, 64))
    ld2 = inp["s5_log_dt"].reshape(2, 2, 4, 8)
    S["ldt_cl"] = f(np.broadcast_to(ld2.transpose(3, 0, 2, 1)[:, None, :, :, :, None], (8, 16, 2, 4, 2, 64)).reshape(128, 2, 4, 2, 64))
    b = np.stack([inp["s5_b_re"], inp["s5_b_im"]], axis=0)
    b7 = b.reshape(2, 2, 2, 4, 8, 64, 16)
    S["b_cl"] = f(b7.transpose(4, 6, 1, 0, 3, 2, 5).reshape(128, 2, 2, 4, 2, 64))
    b8 = b.reshape(2, 2, 2, 16, 2, 64, 16)
    S["b_sl"] = f(b8.transpose(4, 5, 1, 0, 3, 2, 6).reshape(128, 2, 2, 16, 2, 16))
    c = np.stack([inp["s5_c_re"], inp["s5_c_im"]], axis=0)
    c8 = c.reshape(2, 2, 2, 16, 2, 16, 64)
    S["c_sl"] = f(c8.transpose(4, 6, 1, 0, 3, 2, 5).reshape(128, 2, 2, 16, 2, 16))
    S["s5_dT"] = f(inp["s5_d"].reshape(2, 4, 128).transpose(2, 0, 1))
    S["s5_w_glu"] = f(inp["s5_w_glu"])
    S["conv_wT"] = f(inp["conv_w"].reshape(2, 31, 2, 128).transpose(3, 0, 2, 1))
    cv = np.stack([inp["conv_b"], inp["conv_ln_g"], inp["conv_ln_b"]], axis=1)
    S["conv_vT"] = f(cv.reshape(2, 3, 2, 128).transpose(3, 0, 1, 2))
    S["sg_ln"] = f(np.stack([inp["sg_ln_g"], inp["sg_ln_b"]], axis=1))
    S["sg_wT"] = f(inp["sg_w"].transpose(3, 0, 1, 2))
    sgb = inp["sg_b"].reshape(2, 2, 2, 128)
    S["sg_bT"] = f(np.broadcast_to(sgb.transpose(2, 0, 1, 3)[:, None], (2, 64, 2, 2, 128)).reshape(128, 2, 2, 128))
    S["ident"] = np.eye(128, dtype=np.float32)
    p = np.arange(128)
    S["maskE"] = f(((p[:, None] // 16) % 2 == np.arange(2)[None, :]))
    S["mask2"] = f(((p[:, None] // 64) == np.arange(2)[None, :]))
    S["mask3"] = f((np.arange(8)[None, None, :] == (2 * np.arange(4)[None, :, None] + (p // 64)[:, None, None])))
    return S


def _prep_core(inp, core):
    f = lambda a: np.ascontiguousarray(np.asarray(a, dtype=np.float32))
    C = {}
    xp = inp["x_prompt"][4 * core:4 * core + 4].reshape(1024, 1024)
    xs = inp["x_sample"][core]
    C["xin"] = f(np.concatenate([xp, xs], axis=0).T)
    C["cond"] = f(np.stack([inp["c_ctx"], inp["c"][core]], axis=1))
    h0 = inp["state_ssm"][core].reshape(2, 2, 16, 2, 64, 2)
    C["h0"] = f(h0.transpose(3, 4, 0, 2, 1, 5).reshape(128, 2, 16, 2, 2))
    return C


_NC_CACHE = {}


def kernel(**inputs):
    inp = {k: np.asarray(v) for k, v in inputs.items()}
    if "nc" not in _NC_CACHE:
        _NC_CACHE["nc"] = build_program()[0]
    nc = _NC_CACHE["nc"]
    S = _prep_shared(inp)
    in_maps = []
    for core in range(8):
        m = dict(S)
        m.update(_prep_core(inp, core))
        in_maps.append(m)
    res = run_bass_kernel_spmd(nc, in_maps, core_ids=list(range(8)))
    y_prompt = np.empty((32, 256, 1024), np.float32)
    y_sample = np.empty((8, 1024, 1024), np.float32)
    new_state = np.empty((32, 2, 2, 32, 64, 2), np.float32)
    for core in range(8):
        r = res.results[core]
        y = np.asarray(r["yT"]).T
        y_prompt[4 * core:4 * core + 4] = y[:1024].reshape(4, 256, 1024)
        y_sample[core] = y[1024:]
        ns = np.asarray(r["ns"]).reshape(2, 64, 2, 4, 16, 2, 2)
        new_state[4 * core:4 * core + 4] = ns.transpose(3, 2, 5, 4, 0, 1, 6).reshape(4, 2, 2, 32, 64, 2)
    return (y_prompt, y_sample, new_state)
```

```python
import math
import numpy as np
import concourse.bass as bass
import concourse.mybir as mybir
from concourse.bass_utils import run_bass_kernel_spmd

F32 = mybir.dt.float32
BF16 = mybir.dt.bfloat16
I32 = mybir.dt.int32
AF = mybir.ActivationFunctionType
ALU = mybir.AluOpType


class Res:
    __slots__ = ("name", "last_w", "readers", "dyn")

    def __init__(self, name, dyn=False):
        self.name = name
        self.last_w = None
        self.readers = []
        self.dyn = dyn


class Slot:
    __slots__ = ("sem", "count")

    def __init__(self, sem):
        self.sem = sem
        self.count = 0


class Ins:
    __slots__ = ("eng", "fn", "idx", "deps", "needs_inc", "inc_val", "slot", "slot_val")

    def __init__(self, eng, fn, idx):
        self.eng = eng
        self.fn = fn
        self.idx = idx
        self.deps = []
        self.needs_inc = False
        self.inc_val = None
        self.slot = None
        self.slot_val = None


ENGS = ("pe", "act", "dve", "pool", "sp")


class Prog:
    def __init__(self, nc):
        self.nc = nc
        self.streams = {e: [] for e in ENGS}
        self.sems = {}
        self._ctx = []
        self.RDYN = Res("RDYN")
        self._scopes = []
        self.dummy = None

    def enter(self, cm):
        v = cm.__enter__()
        self._ctx.append(cm)
        return v

    def sbuf(self, name, shape, dt):
        self._uid = getattr(self, "_uid", 0) + 1
        return self.enter(self.nc.sbuf_tensor(f"sb{self._uid}_{name}", list(shape), dt))

    def psum(self, name, shape, dt):
        return self.enter(self.nc.psum_tensor(name, list(shape), dt))

    def new_slot(self, name):
        cm = self.nc.semaphore(name)
        v = cm.__enter__()
        self._sem_ctx = getattr(self, "_sem_ctx", [])
        self._sem_ctx.append(cm)
        return Slot(v)

    def scope_begin(self):
        self._scopes.append(len(self._ctx))

    def scope_end(self):
        n = self._scopes.pop()
        d = self.dummy
        self.op("pool", lambda e: e.memset(d[:, 0:1], 0.0), writes=[self.RDYN])
        while len(self._ctx) > n:
            cm = self._ctx.pop()
            cm.__exit__(None, None, None)

    def _add(self, eng, fn, reads, writes):
        st = self.streams[eng]
        ins = Ins(eng, fn, len(st))
        st.append(ins)
        reads = list(reads)
        writes = list(writes)
        if any(r.dyn for r in reads) or any(w.dyn for w in writes):
            reads.append(self.RDYN)
        deps = []
        for r in reads:
            if r.last_w is not None:
                deps.append(r.last_w)
        for w in writes:
            if w.last_w is not None:
                deps.append(w.last_w)
            deps.extend(w.readers)
        seen = set()
        for d in deps:
            if d is ins or id(d) in seen:
                continue
            seen.add(id(d))
            if d.slot is None and d.eng == eng:
                if eng == "pe" or eng == "sp":
                    continue
                if ins.idx - d.idx > 1:
                    continue
            if d.slot is not None:
                ins.deps.append((d, d.slot.count))
            else:
                ins.deps.append((d, None))
                d.needs_inc = True
        for r in reads:
            r.readers.append(ins)
        for w in writes:
            w.last_w = ins
            w.readers = []
        return ins

    def op(self, eng, fn, reads=(), writes=()):
        return self._add(eng, fn, reads, writes)

    def dma(self, eng, out, in_, slot, reads=(), writes=(), **kw):
        ins = self._add(eng, lambda e: e.dma_start(out=out, in_=in_, **kw), reads, writes)
        ins.slot = slot
        slot.count += 16
        ins.slot_val = slot.count
        return ins

    def emit(self, final_waits=()):
        nc = self.nc
        for e in ENGS:
            if e == "sp":
                continue
            self.sems[e] = self.enter(nc.semaphore("sem_" + e))
        self.sems["sp"] = None
        for e in ENGS:
            c = 0
            for ins in self.streams[e]:
                if ins.needs_inc and ins.slot is None:
                    c += 1
                    ins.inc_val = c
        block = self.enter(nc.Block())

        def run(engname, e):
            waited = {}
            for ins in self.streams[engname]:
                need = {}
                for d, sval in ins.deps:
                    if d.slot is not None:
                        sem, val = d.slot.sem, sval
                    else:
                        sem, val = self.sems[d.eng], d.inc_val
                    k = id(sem)
                    if k not in need or need[k][1] < val:
                        need[k] = (sem, val)
                for k, (sem, val) in need.items():
                    if waited.get(k, 0) >= val:
                        continue
                    e.wait_ge(sem, val)
                    waited[k] = val
                r = ins.fn(e)
                if ins.slot is not None:
                    r.then_inc(ins.slot.sem, 16)
                elif ins.needs_inc:
                    r.then_inc(self.sems[engname], 1)
            if engname == "sp":
                for slot in final_waits:
                    e.wait_ge(slot.sem, slot.count)

        @block.tensor
        def _(e):
            run("pe", e)

        @block.scalar
        def _(e):
            run("act", e)

        @block.vector
        def _(e):
            run("dve", e)

        @block.gpsimd
        def _(e):
            run("pool", e)

        @block.sync
        def _(e):
            run("sp", e)

    def close(self):
        while self._ctx:
            cm = self._ctx.pop()
            cm.__exit__(None, None, None)
        for cm in reversed(getattr(self, "_sem_ctx", [])):
            cm.__exit__(None, None, None)


def CAP(ap, off, dims):
    return bass.AP(ap.tensor, ap.offset + off, [list(ap.ap[0])] + [list(d) for d in dims])


D = 1024
NT = 2048
TB = 512
NTB = 4
DFF = 2816
NF = 22
FC = 11
EPS = 1e-6
GELU = AF.Gelu_apprx_tanh
SE = "dve"
GP_OFF = [0, 286, 572, 858, 1144]
GP_LEN = 1144 + 1054

INPUT_SPECS = [
    ("xin", [1024, 2048]), ("pos", [1024, 1024]), ("cond", [1024, 2]),
    ("h0", [128, 2, 16, 2, 2]),
    ("w_mod", [2, 1024, 9216]), ("b_modT", [128, 2, 72]), ("norm_gT", [128, 2, 3, 8]), ("final_gT", [128, 8]),
    ("w1t", [2, 2, 22, 2, 128, 8, 128]), ("ffn_w2", [2, 2, 2816, 1024]),
    ("w_in", [2, 1024, 1536]), ("wgate_t", [2, 8, 128, 3, 8, 128]), ("b_gateT", [128, 2, 24]),
    ("wbr_t", [2, 8, 128, 8, 128]), ("wout_t", [2, 8, 128, 8, 128]),
    ("a_sl", [128, 2, 2, 16, 2]), ("ldt_sl", [128, 2, 16, 2]),
    ("a_cl", [128, 2, 2, 4, 2, 64]), ("ldt_cl", [128, 2, 4, 2, 64]),
    ("b_cl", [128, 2, 2, 4, 2, 64]), ("b_sl", [128, 2, 2, 16, 2, 16]), ("c_sl", [128, 2, 2, 16, 2, 16]),
    ("s5_dT", [128, 2, 4]), ("s5_w_glu", [2, 512, 512]),
    ("conv_wT", [128, 2, 2, 31]), ("conv_vT", [128, 2, 3, 2]),
    ("sg_ln", [2, 2, 256]), ("sg_wT", [128, 2, 4, 128]), ("sg_bT", [128, 2, 2, 128]),
    ("ident", [128, 128]), ("maskE", [128, 2]), ("mask2", [128, 2]), ("mask3", [128, 4, 8]),
]


def build_program(dbg=(), stop=None):
    nc = bass.Bass("TRN2", target_bir_lowering=False)
    P = Prog(nc)
    I = {}
    for name, shape in INPUT_SPECS:
        I[name] = nc.dram_tensor(name, list(shape), F32, kind="ExternalInput").ap()
    yT = nc.dram_tensor("yT", [1024, 2048], F32, kind="ExternalOutput").ap()
    nsd = nc.dram_tensor("ns", [128, 2 * 4 * 16 * 2 * 2], F32, kind="ExternalOutput").ap()
    dbg_out = {}

    xT = P.sbuf("xT", [128, 8, NT], F32)
    RX = [[Res(f"x{c}_{t}") for t in range(NTB)] for c in range(8)]

    def alloc_hT():
        return P.sbuf("hT", [128, 8, NT], BF16), [[Res(f"h{c}_{t}", dyn=True) for t in range(NTB)] for c in range(8)]
    P.dummy = P.sbuf("dummyb", [128, 4], F32)
    ones_bf = P.sbuf("ones_bf", [128, 128], BF16)
    ones256 = P.sbuf("ones256", [128, 128], F32)
    ident = P.sbuf("ident", [128, 128], F32)
    eps_t = P.sbuf("eps_t", [128, 1], F32)
    modT = P.sbuf("modT", [128, 2, 72, 2], F32)
    gsc = P.sbuf("gsc", [128, 2, 3, 8, 2], F32)
    gtv = P.sbuf("gtv", [128, 2, 3, 8, 2], F32)
    bmod = P.sbuf("bmod", [128, 2, 72], F32)
    normg = P.sbuf("normg", [128, 2, 3, 8], F32)
    finalg = P.sbuf("finalg", [128, 8], F32)
    bgate = P.sbuf("bgate", [128, 2, 24], F32)
    condT = P.sbuf("condT", [128, 8, 2], F32)
    condb = P.sbuf("condb", [128, 8, 2], BF16)
    h0t = P.sbuf("h0t", [128, 2, 16, 2, 2], F32)
    nsbuf = P.sbuf("nsbuf", [128, 2, 4, 16, 2, 2], F32)
    s5d = P.sbuf("s5d", [128, 2, 4], F32)
    convw = P.sbuf("convw", [128, 2, 2, 31], F32)
    convv = P.sbuf("convv", [128, 2, 3, 2], F32)
    maskE = P.sbuf("maskE", [128, 2], F32)
    mask2 = P.sbuf("mask2", [128, 2], F32)
    mask3 = P.sbuf("mask3", [128, 4, 8], F32)
    RC = Res("consts")
    Rmodl = [Res("mod0"), Res("mod1")]
    Rcond = Res("cond")
    Rns = Res("ns")
    banks = [P.psum(f"bank{i}", [128, 512], F32) for i in range(8)]
    RB = [Res(f"bank{i}") for i in range(8)]
    bank_ctr = [0]

    bank_reserved = [None]

    def bank():
        i = bank_ctr[0] % 8
        bank_ctr[0] += 1
        if i == bank_reserved[0]:
            i = bank_ctr[0] % 8
            bank_ctr[0] += 1
        return banks[i], RB[i]

    s_in = P.new_slot("s_in")
    s_out = P.new_slot("s_out")
    wslots = {}

    def wslot(name):
        if name not in wslots:
            wslots[name] = P.new_slot("ws_" + name)
        return wslots[name]

    P.op("pool", lambda e: e.memset(ones_bf[:], 1.0 / 1024.0), writes=[RC])
    P.op("pool", lambda e: e.memset(ones256[:], 1.0 / 256.0), writes=[RC])
    P.op("pool", lambda e: e.memset(eps_t[:], EPS), writes=[RC])
    P.op("pool", lambda e: e.memset(P.dummy[:], 0.0), writes=[RC])
    small = [(ident, "ident"), (bmod, "b_modT"), (normg, "norm_gT"), (finalg, "final_gT"), (bgate, "b_gateT"),
             (h0t, "h0"), (s5d, "s5_dT"), (convw, "conv_wT"), (convv, "conv_vT"), (maskE, "maskE"), (mask2, "mask2"),
             (mask3, "mask3")]
    for t, nm in small:
        P.dma("sp", t[:], I[nm], s_in, writes=[RC])
    P.dma("sp", condT[:], I["cond"].rearrange("(k p) c -> p k c", p=128), s_in, writes=[RC])
    for ct in range(8):
        P.dma("sp", xT[:, ct, :], I["xin"][ct * 128:(ct + 1) * 128, :], s_in, writes=RX[ct])
    P.scope_begin()
    ptmp = [P.sbuf(f"ptmp{i}", [128, 1024], F32) for i in range(2)]
    Rpt = [Res(f"ptmp{i}", dyn=True) for i in range(2)]
    for ct in range(8):
        b = ct % 2
        P.dma("sp", ptmp[b][:], I["pos"][ct * 128:(ct + 1) * 128, :], wslot(f"ptmp{b}"), writes=[Rpt[b]])
        P.op("dve", lambda e, ct=ct, b=b: e.tensor_tensor(xT[:, ct, 1024:2048], xT[:, ct, 1024:2048], ptmp[b][:], ALU.add),
             reads=[Rpt[b]], writes=[RX[ct][2], RX[ct][3]])
    P.op("act", lambda e: e.activation(condb[:], condT[:], AF.Silu), reads=[RC], writes=[Rcond])

    P.scope_end()

    def mod_gen(l, wm, Rwm):
        bk, rb = bank()
        bank_reserved[0] = banks.index(bk)
        for ch in range(18):
            b = ch % 2
            P.dma("pool", wm[b][:], I["w_mod"][l, :, ch * 512:(ch + 1) * 512].rearrange("(k p) c -> p k c", p=128),
                  wslot(f"wm{b}"), writes=[Rwm[b]])
            for m in range(4):
                mt = ch * 4 + m
                for kt in range(8):
                    P.op("pe", lambda e, bk=bk, b=b, m=m, mt=mt, kt=kt: e.matmul(
                        bk[:, mt * 2:mt * 2 + 2], lhsT=wm[b][:, kt, m * 128:(m + 1) * 128], rhs=condb[:, kt, :],
                        start=(kt == 0), stop=(kt == 7)), reads=[Rwm[b], Rcond], writes=[rb])
            yield
        P.op("dve", lambda e, bk=bk, l=l: e.tensor_tensor(
            modT[:, l], bk[:, 0:144].rearrange("p (m c) -> p m c", c=2),
            CAP(bmod[:, l, :], 0, [[1, 72], [0, 2]]), ALU.add), reads=[rb, RC], writes=[Rmodl[l]])
        bank_reserved[0] = None
        for n in range(3):
            P.op("dve", lambda e, l=l, n=n: e.tensor_scalar_add(gsc[:, l, n], modT[:, l, (3 * n + 1) * 8:(3 * n + 2) * 8, :], 1.0),
                 reads=[Rmodl[l]], writes=[Rmodl[l]])
            P.op("dve", lambda e, l=l, n=n: e.tensor_tensor(gsc[:, l, n], gsc[:, l, n], CAP(normg[:, l, n, :], 0, [[1, 8], [0, 2]]), ALU.mult),
                 reads=[Rmodl[l], RC], writes=[Rmodl[l]])
            P.op("dve", lambda e, l=l, n=n: e.tensor_scalar_mul(gtv[:, l, n], modT[:, l, (3 * n + 2) * 8:(3 * n + 3) * 8, :],
                                                              0.5 if n != 1 else 1.0), reads=[Rmodl[l]], writes=[Rmodl[l]])

    def mod_bufs():
        wm = [P.sbuf(f"wm{i}", [128, 8, 512], BF16) for i in range(2)]
        Rwm = [Res(f"wm{i}", dyn=True) for i in range(2)]
        return wm, Rwm

    def mod_stage(l):
        P.scope_begin()
        for _ in mod_gen(l, *mod_bufs()):
            pass
        P.scope_end()

    MODG = {"gen": None}

    def mod_step(every=1):
        g = MODG["gen"]
        MODG["ctr"] = MODG.get("ctr", 0) + 1
        if g is not None and MODG["ctr"] % every == 0:
            try:
                next(g)
            except StopIteration:
                MODG["gen"] = None

    def sh_ap(l, n, ct, c):
        return modT[:, l, (3 * n) * 8 + ct, c:c + 1]

    def dump(name, ap, shape, reads, dt=F32):
        d = nc.dram_tensor("dbg_" + name, list(shape), dt, kind="ExternalOutput").ap()
        dbg_out[name] = d
        P.dma("sp", d, ap, s_out, reads=reads)

    def norm_stage(l, n, hT, RH, own_scope=False):
        if own_scope:
            P.scope_begin()
        sq, Rsq, rs, Rrs, tmp, Rtmp = norm_scratch()
        for tb in range(NTB):
            c = 0 if tb < 2 else 1
            ts = slice(tb * TB, (tb + 1) * TB)
            for ct in range(8):
                P.op("act", lambda e, ct=ct, ts=ts: e.activation(sq[:, ct, :], xT[:, ct, ts], AF.Square),
                     reads=[RX[ct][tb]], writes=[Rsq[ct]])
            bk, rb = bank()
            for ct in range(8):
                P.op("pe", lambda e, bk=bk, ct=ct: e.matmul(bk[:, :], lhsT=ones_bf[:], rhs=sq[:, ct, :], start=(ct == 0), stop=(ct == 7)),
                     reads=[Rsq[ct], RC], writes=[rb])
            r = tb % 2
            P.op("act", lambda e, bk=bk, r=r: e.activation(rs[r][:], bk[:, :], AF.Sqrt, bias=eps_t[:, 0:1], scale=1.0),
                 reads=[rb, RC], writes=[Rrs[r]])
            P.op("dve", lambda e, r=r: e.reciprocal(rs[r][:], rs[r][:]), reads=[Rrs[r]], writes=[Rrs[r]])
            for ct in range(8):
                t = ct % 2
                P.op("dve", lambda e, ct=ct, ts=ts, t=t, r=r, c=c: e.scalar_tensor_tensor(
                    out=tmp[t][:], in0=xT[:, ct, ts], scalar=gsc[:, l, n, ct, c:c + 1], in1=rs[r][:], op0=ALU.mult, op1=ALU.mult),
                    reads=[RX[ct][tb], Rmodl[l], Rrs[r]], writes=[Rtmp[t]])
                P.op("act", lambda e, ct=ct, ts=ts, t=t, c=c: e.activation(
                    hT[:, ct, ts], tmp[t][:], AF.Identity, bias=sh_ap(l, n, ct, c), scale=1.0),
                    reads=[Rtmp[t], Rmodl[l]], writes=[RH[ct][tb]])
        if own_scope:
            P.scope_end()

    def norm_scratch():
        sq = P.sbuf("sq", [128, 8, TB], BF16)
        rs = [P.sbuf(f"rs{i}", [128, TB], F32) for i in range(2)]
        tmp = [P.sbuf(f"ntmp{i}", [128, TB], F32) for i in range(2)]
        return (sq, [Res(f"sq{i}", dyn=True) for i in range(8)], rs, [Res(f"rs{i}", dyn=True) for i in range(2)],
                tmp, [Res(f"ntmp{i}", dyn=True) for i in range(2)])

    def ffn_stage(l, w, n):
        P.scope_begin()
        hT, RH = alloc_hT()
        norm_stage(l, n, hT, RH)
        act = P.sbuf("act", [128, FC, NT], BF16)
        RA = [[Res(f"act{f}_{t}", dyn=True) for t in range(NTB)] for f in range(FC)]
        w1b = [P.sbuf(f"w1b{i}", [128, 2, 2, 8, 128], BF16) for i in range(2)]
        Rw1 = [Res(f"w1b{i}", dyn=True) for i in range(2)]
        w2b = P.sbuf("w2b", [128, FC, D], BF16)
        Rw2 = Res("w2b", dyn=True)
        sg = [P.sbuf(f"sg{i}", [128, TB], F32) for i in range(2)]
        Rsg = [Res(f"sg{i}", dyn=True) for i in range(2)]
        it = 0
        for chunk in range(2):
            P.dma("pool", w2b[:], I["ffn_w2"][l, w, chunk * FC * 128:(chunk + 1) * FC * 128, :].rearrange("(f p) d -> p f d", p=128),
                  wslot("w2b"), writes=[Rw2])
            for pair in range(6):
                nf = 2 if pair < 5 else 1
                b = it % 2
                it += 1
                f0 = chunk * FC + pair * 2
                P.dma("pool", w1b[b][:, 0:nf], I["w1t"][l, w, f0:f0 + nf].rearrange("f g p k c -> p f g k c"),
                      wslot(f"w1b{b}"), writes=[Rw1[b]])
                for fi in range(nf):
                    f = pair * 2 + fi
                    for tb in range(NTB):
                        ts = slice(tb * TB, (tb + 1) * TB)
                        pg, rg = bank()
                        pu, ru = bank()
                        for gu, (pb_, rb_) in enumerate(((pg, rg), (pu, ru))):
                            for kt in range(8):
                                P.op("pe", lambda e, pb_=pb_, b=b, fi=fi, gu=gu, kt=kt, ts=ts: e.matmul(
                                    pb_[:, :], lhsT=w1b[b][:, fi, gu, kt, :], rhs=hT[:, kt, ts], start=(kt == 0), stop=(kt == 7)),
                                    reads=[Rw1[b], RH[kt][tb]], writes=[rb_])
                        s = (f * NTB + tb) % 2
                        P.op("act", lambda e, pg=pg, s=s: e.activation(sg[s][:], pg[:, :], AF.Silu), reads=[rg], writes=[Rsg[s]])
                        P.op("dve", lambda e, pu=pu, s=s, f=f, ts=ts: e.tensor_tensor(act[:, f, ts], sg[s][:], pu[:, :], ALU.mult),
                             reads=[Rsg[s], ru], writes=[RA[f][tb]])
            for d in range(8):
                for tb in range(NTB):
                    c = 0 if tb < 2 else 1
                    ts = slice(tb * TB, (tb + 1) * TB)
                    po, ro = bank()
                    for f in range(FC):
                        P.op("pe", lambda e, po=po, f=f, d=d, ts=ts: e.matmul(
                            po[:, :], lhsT=w2b[:, f, d * 128:(d + 1) * 128], rhs=act[:, f, ts], start=(f == 0), stop=(f == FC - 1)),
                            reads=[Rw2, RA[f][tb]], writes=[ro])
                    P.op("dve", lambda e, po=po, d=d, ts=ts, c=c: e.scalar_tensor_tensor(
                        out=xT[:, d, ts], in0=po[:, :], scalar=gtv[:, l, n, d, c:c + 1], in1=xT[:, d, ts], op0=ALU.mult, op1=ALU.add),
                        reads=[ro, Rmodl[l]], writes=[RX[d][tb]])
        P.scope_end()

    def lam_q(pref, ar, ai, ldt, shape, R_in, outs, RT):
        P.scope_begin()
        T = dict(outs)
        for nm in ["dt", "th", "mg", "t1", "t2", "s", "c", "den", "nr"]:
            T[nm] = P.sbuf(pref + nm, [128] + list(shape), F32)
        ki = P.sbuf(pref + "ki", [128] + list(shape), I32)
        two_pi = 2.0 * math.pi

        def V(nm):
            return T[nm][:]

        P.op("act", lambda e: e.activation(V("dt"), ldt, AF.Exp), reads=[R_in], writes=[RT])
        P.op("dve", lambda e: e.tensor_tensor(V("th"), ai, V("dt"), ALU.mult), reads=[R_in, RT], writes=[RT])
        P.op("dve", lambda e: e.tensor_tensor(V("mg"), ar, V("dt"), ALU.mult), reads=[R_in, RT], writes=[RT])
        P.op("act", lambda e: e.activation(V("mg"), V("mg"), AF.Exp), reads=[RT], writes=[RT])
        for which, shift in (("s", 0.0), ("c", 0.25)):
            P.op("dve", lambda e, shift=shift: e.tensor_scalar(V("t1"), V("th"), 1.0 / two_pi, shift, ALU.mult, ALU.add),
                 reads=[RT], writes=[RT])
            P.op("dve", lambda e: e.tensor_copy(ki[:], V("t1")), reads=[RT], writes=[RT])
            P.op("dve", lambda e: e.tensor_copy(V("t2"), ki[:]), reads=[RT], writes=[RT])
            P.op("dve", lambda e: e.tensor_tensor(V("t1"), V("t1"), V("t2"), ALU.subtract), reads=[RT], writes=[RT])
            P.op("dve", lambda e: e.tensor_scalar(V("t1"), V("t1"), two_pi, 3.1415925, ALU.mult, ALU.min), reads=[RT], writes=[RT])
            P.op("dve", lambda e: e.tensor_scalar_max(V("t1"), V("t1"), -3.1415925), reads=[RT], writes=[RT])
            P.op("act", lambda e, which=which: e.activation(V(which), V("t1"), AF.Sin), reads=[RT], writes=[RT])
        P.op("dve", lambda e: e.tensor_tensor(V("lr"), V("mg"), V("c"), ALU.mult), reads=[RT], writes=[RT])
        P.op("dve", lambda e: e.tensor_tensor(V("li"), V("mg"), V("s"), ALU.mult), reads=[RT], writes=[RT])
        P.op("dve", lambda e: e.tensor_scalar_add(V("nr"), V("lr"), -1.0), reads=[RT], writes=[RT])
        P.op("dve", lambda e: e.tensor_tensor(V("den"), ar, ar, ALU.mult), reads=[R_in, RT], writes=[RT])
        P.op("dve", lambda e: e.tensor_tensor(V("t1"), ai, ai, ALU.mult), reads=[R_in, RT], writes=[RT])
        P.op("dve", lambda e: e.tensor_tensor(V("den"), V("den"), V("t1"), ALU.add), reads=[RT], writes=[RT])
        P.op("dve", lambda e: e.reciprocal(V("den"), V("den")), reads=[RT], writes=[RT])
        P.op("dve", lambda e: e.tensor_tensor(V("t1"), V("nr"), ar, ALU.mult), reads=[R_in, RT], writes=[RT])
        P.op("dve", lambda e: e.tensor_tensor(V("t2"), V("li"), ai, ALU.mult), reads=[R_in, RT], writes=[RT])
        P.op("dve", lambda e: e.tensor_tensor(V("t1"), V("t1"), V("t2"), ALU.add), reads=[RT], writes=[RT])
        P.op("dve", lambda e: e.tensor_tensor(V("qr"), V("t1"), V("den"), ALU.mult), reads=[RT], writes=[RT])
        P.op("dve", lambda e: e.tensor_tensor(V("t1"), V("li"), ar, ALU.mult), reads=[R_in, RT], writes=[RT])
        P.op("dve", lambda e: e.tensor_tensor(V("t2"), V("nr"), ai, ALU.mult), reads=[R_in, RT], writes=[RT])
        P.op("dve", lambda e: e.tensor_tensor(V("t1"), V("t1"), V("t2"), ALU.subtract), reads=[RT], writes=[RT])
        P.op("dve", lambda e: e.tensor_tensor(V("qi"), V("t1"), V("den"), ALU.mult), reads=[RT], writes=[RT])
        P.scope_end()

    def cmul(eng, out_r, out_i, ar, ai, br, bi, t1, t2, reads, writes):
        P.op(eng, lambda e: e.tensor_tensor(t1, ar, br, ALU.mult), reads=reads, writes=writes)
        P.op(eng, lambda e: e.tensor_tensor(t2, ai, bi, ALU.mult), reads=reads, writes=writes)
        P.op(eng, lambda e: e.tensor_tensor(t1, t1, t2, ALU.subtract), reads=reads, writes=writes)
        P.op(eng, lambda e: e.tensor_tensor(t2, ar, bi, ALU.mult), reads=reads, writes=writes)
        P.op(eng, lambda e: e.tensor_tensor(out_i, ai, br, ALU.mult), reads=reads, writes=writes)
        P.op(eng, lambda e: e.tensor_tensor(out_i, out_i, t2, ALU.add), reads=reads, writes=writes)
        P.op(eng, lambda e: e.tensor_copy(out_r, t1), reads=reads, writes=writes)

    def cmul6(eng, out_r, out_i, ar, ai, br, bi, t1, t2, reads, writes):
        P.op(eng, lambda e: e.tensor_tensor(t1, ar, br, ALU.mult), reads=reads, writes=writes)
        P.op(eng, lambda e: e.tensor_tensor(t2, ai, bi, ALU.mult), reads=reads, writes=writes)
        P.op(eng, lambda e: e.tensor_tensor(out_r, t1, t2, ALU.subtract), reads=reads, writes=writes)
        P.op(eng, lambda e: e.tensor_tensor(t1, ar, bi, ALU.mult), reads=reads, writes=writes)
        P.op(eng, lambda e: e.tensor_tensor(t2, ai, br, ALU.mult), reads=reads, writes=writes)
        P.op(eng, lambda e: e.tensor_tensor(out_i, t1, t2, ALU.add), reads=reads, writes=writes)

    class _Stop(Exception):
        pass

    def ck(name, l):
        if stop == f"{name}{l}":
            raise _Stop()

    def mixer_stage(l):
        depth = len(P._scopes)
        try:
            _mixer(l)
            return False
        except _Stop:
            while len(P._scopes) > depth:
                P.scope_end()
            return True

    def _mixer(l):
        P.scope_begin()
        y_a = P.sbuf("y_a", [128, 4, NT], BF16)
        RYA = [[Res(f"ya{c}_{t}", dyn=True) for t in range(NTB)] for c in range(4)]
        P.scope_begin()
        U = P.sbuf("U", [128, 4, NT], BF16)
        RU = [[Res(f"U{c}_{t}", dyn=True) for t in range(NTB)] for c in range(4)]
        yg, RYG = U, RU
        P.scope_begin()
        hT, RH = alloc_hT()
        wina = P.sbuf("wina", [128, 8, 512], BF16)
        Rwina = Res("wina", dyn=True)
        P.dma("pool", wina[:], I["w_in"][l, :, 0:512].rearrange("(k p) c -> p k c", p=128), wslot("wina"), writes=[Rwina])
        norm_stage(l, 1, hT, RH)
        for ct in range(4):
            for tb in range(NTB):
                ts = slice(tb * TB, (tb + 1) * TB)
                bk, rb = bank()
                for kt in range(8):
                    P.op("pe", lambda e, bk=bk, ct=ct, kt=kt, ts=ts: e.matmul(
                        bk[:, :], lhsT=wina[:, kt, ct * 128:(ct + 1) * 128], rhs=hT[:, kt, ts], start=(kt == 0), stop=(kt == 7)),
                        reads=[Rwina, RH[kt][tb]], writes=[rb])
                P.op("act", lambda e, bk=bk, ct=ct, ts=ts: e.activation(U[:, ct, ts], bk[:, :], AF.Copy), reads=[rb], writes=[RU[ct][tb]])
        P.scope_end()
        if "za" in dbg:
            P.scope_begin()
            for ct in range(4):
                tmpd = P.sbuf(f"dbgza{ct}", [128, NT], F32)
                Rd = Res(f"dbgza{ct}", dyn=True)
                P.op("dve", lambda e, ct=ct, tmpd=tmpd: e.tensor_copy(tmpd[:], U[:, ct, :]), reads=RU[ct], writes=[Rd])
                dump(f"za{ct}", tmpd[:], [128, NT], [Rd])
            P.scope_end()
        ck("m_u", l)
        asl = P.sbuf("asl", [128, 2, 16, 2], F32)
        ldsl = P.sbuf("ldsl", [128, 16, 2], F32)
        csl = P.sbuf("csl", [128, 2, 16, 2, 16], F32)
        Rpar = Res("s5par", dyn=True)
        TS = {nm: P.sbuf("sl_" + nm, [128, 16, 2], F32) for nm in ("lr", "li", "qr", "qi")}
        RTS = Res("sl_tab", dyn=True)
        TC = {nm: P.sbuf("cl_" + nm, [128, 4, 2, 64], F32) for nm in ("lr", "li")}
        RTC = Res("cl_tab", dyn=True)
        Bc = [P.sbuf(f"Bc{i}", [128, 4, 2, 64], F32) for i in range(2)]
        Bs = [P.sbuf(f"Bs{i}", [128, 16, 2, 16], F32) for i in range(2)]
        L4c = [P.sbuf(f"L4c{i}", [128, 4, 2, 64], F32) for i in range(2)]
        RBb = Res("Bbar", dyn=True)
        P.scope_begin()
        acl = P.sbuf("acl", [128, 2, 4, 2, 64], F32)
        ldcl = P.sbuf("ldcl", [128, 4, 2, 64], F32)
        bcl = P.sbuf("bcl", [128, 2, 4, 2, 64], F32)
        bsl = P.sbuf("bsl", [128, 2, 16, 2, 16], F32)
        for t, nm in ((asl, "a_sl"), (ldsl, "ldt_sl"), (acl, "a_cl"), (ldcl, "ldt_cl"), (bcl, "b_cl"), (bsl, "b_sl"), (csl, "c_sl")):
            P.dma("sp", t[:], I[nm][:, l], wslot(f"s5par{l}"), writes=[Rpar])
        TCq = dict(TC)
        TCq["qr"] = P.sbuf("cl_qr", [128, 4, 2, 64], F32)
        TCq["qi"] = P.sbuf("cl_qi", [128, 4, 2, 64], F32)
        tcl = [P.sbuf(f"tcl{i}", [128, 4, 2, 64], F32) for i in range(2)]
        tsl = [P.sbuf(f"tsl{i}", [128, 16, 2, 16], F32) for i in range(2)]
        lam_q("sl_", asl[:, 0], asl[:, 1], ldsl[:], [16, 2], Rpar, TS, RTS)
        lam_q("cl_", acl[:, 0], acl[:, 1], ldcl[:], [4, 2, 64], Rpar, TCq, RTC)
        cmul("dve", Bc[0][:], Bc[1][:], TCq["qr"][:], TCq["qi"][:], bcl[:, 0], bcl[:, 1], tcl[0][:], tcl[1][:],
             [RTC, Rpar, RBb], [RBb])
        L2c = [P.sbuf(f"L2c{i}", [128, 4, 2, 64], F32) for i in range(2)]
        cmul6("dve", L2c[0][:], L2c[1][:], TC["lr"][:], TC["li"][:], TC["lr"][:], TC["li"][:], tcl[0][:], tcl[1][:], [RTC], [RTC])
        cmul6("dve", L4c[0][:], L4c[1][:], L2c[0][:], L2c[1][:], L2c[0][:], L2c[1][:], tcl[0][:], tcl[1][:], [RTC], [RTC])
        qrb = CAP(TS["qr"][:], 0, [[2, 16], [1, 2], [0, 16]])
        qib = CAP(TS["qi"][:], 0, [[2, 16], [1, 2], [0, 16]])
        cmul("dve", Bs[0][:], Bs[1][:], qrb, qib, bsl[:, 0], bsl[:, 1], tsl[0][:], tsl[1][:], [RTS, Rpar, RBb], [RBb])
        P.scope_end()
        L2 = [P.sbuf(f"L2{i}", [128, 16, 2], F32) for i in range(2)]
        L4 = [P.sbuf(f"L4{i}", [128, 16, 2], F32) for i in range(2)]
        L8 = [P.sbuf(f"L8{i}", [128, 16, 2], F32) for i in range(2)]
        t8 = [P.sbuf(f"t8{i}", [128, 4, 16, 2], F32) for i in range(2)]
        RL8 = Res("L8", dyn=True)
        cmul6("dve", L2[0][:], L2[1][:], TS["lr"][:], TS["li"][:], TS["lr"][:], TS["li"][:], t8[0][:, 0], t8[1][:, 0], [RTS, RL8], [RL8])
        cmul6("dve", L4[0][:], L4[1][:], L2[0][:], L2[1][:], L2[0][:], L2[1][:], t8[0][:, 0], t8[1][:, 0], [RL8], [RL8])
        cmul6("dve", L8[0][:], L8[1][:], L4[0][:], L4[1][:], L4[0][:], L4[1][:], t8[0][:, 0], t8[1][:, 0], [RL8], [RL8])
        LPs = [P.sbuf(f"LPs{i}", [128, 9, 16, 2], F32) for i in range(2)]
        P.op("dve", lambda e: e.memset(LPs[0][:, 0], 1.0), writes=[RL8])
        P.op("dve", lambda e: e.memset(LPs[1][:, 0], 0.0), writes=[RL8])
        P.op("dve", lambda e: e.tensor_copy(LPs[0][:, 1], TS["lr"][:]), reads=[RTS], writes=[RL8])
        P.op("dve", lambda e: e.tensor_copy(LPs[1][:, 1], TS["li"][:]), reads=[RTS], writes=[RL8])
        cmul6("dve", LPs[0][:, 2:4], LPs[1][:, 2:4], LPs[0][:, 0:2], LPs[1][:, 0:2],
              CAP(L2[0][:], 0, [[0, 2], [2, 16], [1, 2]]), CAP(L2[1][:], 0, [[0, 2], [2, 16], [1, 2]]),
              t8[0][:, 0:2], t8[1][:, 0:2], [RL8], [RL8])
        cmul6("dve", LPs[0][:, 4:8], LPs[1][:, 4:8], LPs[0][:, 0:4], LPs[1][:, 0:4],
              CAP(L4[0][:], 0, [[0, 4], [2, 16], [1, 2]]), CAP(L4[1][:], 0, [[0, 4], [2, 16], [1, 2]]),
              t8[0][:], t8[1][:], [RL8], [RL8])
        P.op("dve", lambda e: e.tensor_copy(LPs[0][:, 8], L8[0][:]), reads=[RL8], writes=[RL8])
        P.op("dve", lambda e: e.tensor_copy(LPs[1][:, 8], L8[1][:]), reads=[RL8], writes=[RL8])
        A1 = P.sbuf("A1", [128, 2, 2, 16], F32)
        A2 = P.sbuf("A2", [128, 2, 2, 16], F32)
        L8v = [CAP(L8[i][:], 0, [[1, 2], [2, 16]]) for i in range(2)]
        for ri in range(2):
            P.op("dve", lambda e, ri=ri: e.tensor_copy(A1[:, ri], L8v[0]), reads=[RL8], writes=[RL8])
            P.op("dve", lambda e, ri=ri: e.tensor_scalar_mul(A2[:, ri], L8v[1], -1.0 if ri == 0 else 1.0), reads=[RL8], writes=[RL8])
        Lh = P.sbuf("Lh", [128, 2, 2, 16], F32)
        h0v = [CAP(h0t[:, l], ri, [[2, 2], [4, 16]]) for ri in range(2)]
        lt = [P.sbuf(f"lht{i}", [128, 2, 16], F32) for i in range(2)]
        cmul("dve", Lh[:, 0], Lh[:, 1], L8v[0], L8v[1], h0v[0], h0v[1], lt[0][:], lt[1][:], [RL8, RC], [RL8])
        h0b = P.sbuf("h0b", [128, 2, 2, 16], BF16)
        for ri in range(2):
            P.op("dve", lambda e, ri=ri: e.tensor_copy(h0b[:, ri], h0v[ri]), reads=[RC], writes=[RL8])

        ck("m_tab", l)
        P.scope_begin()
        Ec = [P.sbuf(f"Ec{i}", [128, 8, 2, 64], F32) for i in range(2)]
        tmpf = [P.sbuf(f"tmpf{i}", [128, 576], F32) for i in range(2)]
        et = [tmpf[i][:, 0:512].rearrange("p (a b c) -> p a b c", a=4, b=2, c=64) for i in range(2)]
        gt = [tmpf[i][:, 0:576].rearrange("p (a b c) -> p a b c", a=9, b=4, c=16) for i in range(2)]
        RPw = Res("Ec", dyn=True)
        EFe = P.sbuf("EFe", [128, 4096], BF16)
        Ebuf = EFe[:].rearrange("p (i d r g q) -> p i d r g q", i=8, d=2, r=2, g=2, q=64)
        RE = Res("EFe", dyn=True)
        RG = RPw
        Gbb = [P.sbuf(f"Gb{i}", [128, 2, 4, 2, 9, 16], BF16) for i in range(2)]
        RGbb = [Res(f"Gb{i}", dyn=True) for i in range(2)]
        EFf = P.sbuf("EFf", [128, 4096], BF16)
        EF = EFf
        Fbuf = EFf[:].rearrange("p (a j d r g h) -> p a j d r g h", a=4, j=8, d=2, r=2, g=2, h=16)
        RF = Res("EFf", dyn=True)
        Bpad = P.sbuf("Bpad", [128, 4, 2, 2, 8, 16], BF16)
        RBp = Res("Bpad", dyn=True)
        Kblk = P.sbuf("Kblk", [128, 2, 8, 128], BF16)
        RK = Res("Kblk", dyn=True)
        XS = P.sbuf("XS", [128, 2, 2, 4, 9, 32], F32)
        RXd = [Res("XS0", dyn=True), Res("XS1", dyn=True)]
        st1 = P.sbuf("st1", [128, 2, 2, 4, 9], F32)
        st2 = P.sbuf("st2", [128, 2, 2, 4, 9], F32)
        fx1 = P.sbuf("fx1", [128, 2, 4, 32], F32)
        fx2 = P.sbuf("fx2", [128, 2, 4, 32], F32)
        cs = P.sbuf("cs", [128, 2, 4], F32)
        Sin = P.sbuf("Sin", [128, 2, 2, 4, 256], BF16)
        RSin = Res("Sin", dyn=True)
        XSZ = 2 * 4 * 9 * 32
        SEA = "pool"

        def seg_A1(ct):
            Gb, RGb = Gbb[ct % 2], RGbb[ct % 2]
            for ri in range(2):
                P.op(SEA, lambda e, ri=ri, ct=ct: e.tensor_copy(Ec[ri][:, 0], Bc[ri][:, ct]), reads=[RBb], writes=[RPw])
            for k in range(1, 4):
                cmul6(SEA, Ec[0][:, k], Ec[1][:, k], Ec[0][:, k - 1], Ec[1][:, k - 1], TC["lr"][:, ct], TC["li"][:, ct],
                      et[0][:, 0], et[1][:, 0], [RPw, RTC], [RPw])
            cmul6(SEA, Ec[0][:, 4:8], Ec[1][:, 4:8], Ec[0][:, 0:4], Ec[1][:, 0:4],
                  CAP(L4c[0][:, ct], 0, [[0, 4], [64, 2], [1, 64]]), CAP(L4c[1][:, ct], 0, [[0, 4], [64, 2], [1, 64]]),
                  et[0][:], et[1][:], [RPw, RTC], [RPw])
            for dr in range(2):
                for ri in range(2):
                    o_ap = CAP(EFe[:, 0:1], (7 * 512 if dr == 0 else 0) + dr * 256 + ri * 128,
                               [[-512 if dr == 0 else 512, 8], [64, 2], [1, 64]])
                    P.op(SEA, lambda e, o_ap=o_ap, dr=dr, ri=ri: e.tensor_tensor(
                        o_ap, CAP(Ec[ri][:, 0, dr, 0:1], 0, [[128, 8], [0, 2], [1, 64]]),
                        CAP(maskE[:], 0, [[0, 8], [1, 2], [0, 64]]), ALU.mult), reads=[RPw, RC], writes=[RE])

        def seg_A2(ct):
            Gb, RGb = Gbb[ct % 2], RGbb[ct % 2]
            for dr in range(2):
                Cr = CAP(csl[:, 0, ct * 4, dr, 0:1], 0, [[0, 9], [32, 4], [1, 16]])
                Ci = CAP(csl[:, 1, ct * 4, dr, 0:1], 0, [[0, 9], [32, 4], [1, 16]])
                Pr = CAP(LPs[0][:, 0, ct * 4, dr:dr + 1], 0, [[32, 9], [2, 4], [0, 16]])
                Pi = CAP(LPs[1][:, 0, ct * 4, dr:dr + 1], 0, [[32, 9], [2, 4], [0, 16]])
                o_r = CAP(Gb[:, 0, 0, dr, 0, 0:1], 0, [[16, 9], [288, 4], [1, 16]])
                o_i = CAP(Gb[:, 1, 0, dr, 0, 0:1], 0, [[16, 9], [288, 4], [1, 16]])
                rr, ww = [RG, RL8, Rpar], [RG]
                P.op(SE, lambda e, Cr=Cr, Pr=Pr: e.tensor_tensor(gt[0][:], Cr, Pr, ALU.mult), reads=rr, writes=ww)
                P.op(SE, lambda e, Ci=Ci, Pi=Pi: e.tensor_tensor(gt[1][:], Ci, Pi, ALU.mult), reads=rr, writes=ww)
                P.op(SE, lambda e, o_r=o_r: e.tensor_tensor(o_r, gt[0][:], gt[1][:], ALU.subtract), reads=[RG], writes=[RG, RGb])
                P.op(SE, lambda e, Cr=Cr, Pi=Pi: e.tensor_tensor(gt[0][:], Cr, Pi, ALU.mult), reads=rr, writes=ww)
                P.op(SE, lambda e, Ci=Ci, Pr=Pr: e.tensor_tensor(gt[1][:], Ci, Pr, ALU.mult), reads=rr, writes=ww)
                P.op(SE, lambda e: e.tensor_tensor(gt[0][:], gt[0][:], gt[1][:], ALU.add), reads=[RG], writes=[RG])
                P.op(SE, lambda e, o_i=o_i: e.tensor_scalar_mul(o_i, gt[0][:], -1.0), reads=[RG], writes=[RG, RGb])
            for gpl in range(4):
                for dr in range(2):
                    bk, rb = bank()
                    for ri in range(2):
                        for i in range(8):
                            P.op("pe", lambda e, bk=bk, gpl=gpl, dr=dr, ri=ri, i=i, ct=ct: e.matmul(
                                bk[:, ri * 256:(ri + 1) * 256],
                                lhsT=Ebuf[32 * gpl:32 * gpl + 32, i, dr, ri].rearrange("p a b -> p (a b)"),
                                rhs=CAP(U[32 * gpl:32 * gpl + 32, ct, 0:1], i, [[8, 256]]),
                                start=(i == 0), stop=(i == 7), tile_position=(32 * gpl, 0)),
                                reads=[RE] + RU[ct], writes=[rb])
                    P.op("act", lambda e, bk=bk, gpl=gpl, dr=dr: e.activation(
                        CAP(XS[:, 0, dr, gpl, 0, 0:1], 0, [[XSZ, 2], [1, 256]]),
                        bk[:, :].rearrange("p (r c) -> p r c", r=2), AF.Copy), reads=[rb], writes=[RXd[dr]])
            P.op(SE, lambda e: e.memset(XS[:, :, :, :, 8, :], 0.0), writes=RXd)
            for ri in range(2):
                for dr in range(2):
                    kk = 0 if dr == 0 else 31
                    P.op(SE, lambda e, ri=ri, dr=dr, kk=kk, ct=ct: e.tensor_copy(
                        XS[:, ri, dr, :, 8, kk], CAP(L8[ri][:, ct * 4, dr:dr + 1], 0, [[2, 4]])), reads=[RL8], writes=[RXd[dr]])
            for dr in range(2):
                sgm, kk = (4, 0) if dr == 0 else (7, 31)
                P.op("dve", lambda e, dr=dr, sgm=sgm, kk=kk, ct=ct: e.tensor_tensor(
                    XS[:, :, dr, :, sgm, kk], XS[:, :, dr, :, sgm, kk], Lh[:, :, dr, ct * 4:ct * 4 + 4], ALU.add),
                    reads=[RXd[dr], RL8], writes=[RXd[dr]])

        def seg_S(ct):
            dstride = 4 * 9 * 32
            for k in range(1, 32):
                cur_ap = CAP(XS[:, 0, 0, 0, 0, 0:1], k, [[XSZ, 2], [dstride + 31 - 2 * k, 2], [288, 4], [32, 9]])
                prv_ap = CAP(XS[:, 0, 0, 0, 0, 0:1], k - 1, [[XSZ, 2], [dstride + 33 - 2 * k, 2], [288, 4], [32, 9]])
                prv_sw = CAP(XS[:, 0, 0, 0, 0, 0:1], XSZ + k - 1, [[-XSZ, 2], [dstride + 33 - 2 * k, 2], [288, 4], [32, 9]])
                a1 = CAP(A1[:, 0, 0, ct * 4:ct * 4 + 1], 0, [[32, 2], [16, 2], [1, 4], [0, 9]])
                a2 = CAP(A2[:, 0, 0, ct * 4:ct * 4 + 1], 0, [[32, 2], [16, 2], [1, 4], [0, 9]])
                P.op("dve", lambda e, prv_ap=prv_ap, a1=a1: e.tensor_tensor(st1[:], prv_ap, a1, ALU.mult), reads=RXd + [RL8], writes=RXd)
                P.op("dve", lambda e, prv_sw=prv_sw, a2=a2: e.tensor_tensor(st2[:], prv_sw, a2, ALU.mult), reads=RXd + [RL8], writes=RXd)
                P.op("dve", lambda e, cur_ap=cur_ap: e.tensor_tensor(cur_ap, cur_ap, st1[:], ALU.add), reads=RXd, writes=RXd)
                P.op("dve", lambda e, cur_ap=cur_ap: e.tensor_tensor(cur_ap, cur_ap, st2[:], ALU.add), reads=RXd, writes=RXd)
            for dr in range(2):
                order = (5, 6, 7) if dr == 0 else (6, 5, 4)
                for sgm in order:
                    src_seg, src_k = (sgm - 1, 31) if dr == 0 else (sgm + 1, 0)
                    P.op("dve", lambda e, dr=dr, src_seg=src_seg, src_k=src_k: e.tensor_scalar_mul(
                        cs[:, 0], XS[:, 1, dr, :, src_seg, src_k], -1.0), reads=[RXd[dr]], writes=[RXd[dr]])
                    P.op("dve", lambda e, dr=dr, src_seg=src_seg, src_k=src_k: e.tensor_copy(
                        cs[:, 1], XS[:, 1, dr, :, src_seg, src_k]), reads=[RXd[dr]], writes=[RXd[dr]])
                    pw_ap = CAP(XS[:, 0, dr, 0, 8, 0:1], 0, [[XSZ, 2], [288, 4], [1, 32]])
                    pw_sw = CAP(XS[:, 0, dr, 0, 8, 0:1], XSZ, [[-XSZ, 2], [288, 4], [1, 32]])
                    tgt = CAP(XS[:, 0, dr, 0, sgm, 0:1], 0, [[XSZ, 2], [288, 4], [1, 32]])
                    c_r = CAP(XS[:, 0, dr, 0, src_seg, src_k:src_k + 1], 0, [[0, 2], [288, 4], [0, 32]])
                    c_s = CAP(cs[:, 0, 0:1], 0, [[4, 2], [1, 4], [0, 32]])
                    P.op("dve", lambda e, pw_ap=pw_ap, c_r=c_r: e.tensor_tensor(fx1[:], pw_ap, c_r, ALU.mult), reads=[RXd[dr]], writes=[RXd[dr]])
                    P.op("dve", lambda e, pw_sw=pw_sw, c_s=c_s: e.tensor_tensor(fx2[:], pw_sw, c_s, ALU.mult), reads=[RXd[dr]], writes=[RXd[dr]])
                    P.op("dve", lambda e, tgt=tgt: e.tensor_tensor(tgt, tgt, fx1[:], ALU.add), reads=[RXd[dr]], writes=[RXd[dr]])
                    P.op("dve", lambda e, tgt=tgt: e.tensor_tensor(tgt, tgt, fx2[:], ALU.add), reads=[RXd[dr]], writes=[RXd[dr]])
            for dr in range(2):
                kk = 31 if dr == 0 else 0
                P.op("dve", lambda e, dr=dr, kk=kk, ct=ct: e.tensor_copy(
                    CAP(nsbuf[:, l, 0, ct * 4, dr, 0:1], 0, [[1, 2], [4, 4], [64, 4]]),
                    CAP(XS[:, 0, dr, 0, 0, kk:kk + 1], 0, [[XSZ, 2], [288, 4], [32, 4]])), reads=[RXd[dr]], writes=[Rns])

        def seg_Z1(ct):
            P.op(SE, lambda e: e.memset(Sin[:], 0.0), writes=[RSin])
            P.op("dve", lambda e: e.tensor_copy(
                CAP(Sin[:, 0, 0, 0, 0:1], 1, [[2048, 2], [256, 4], [32, 4], [1, 31]]),
                CAP(XS[:, 0, 0, 0, 0, 0:1], 0, [[XSZ, 2], [288, 4], [32, 4], [1, 31]])), reads=[RXd[0]], writes=[RSin])
            P.op("dve", lambda e: e.tensor_copy(
                CAP(Sin[:, 0, 0, 0, 0:1], 129, [[2048, 2], [256, 4], [1, 127]]),
                CAP(XS[:, 0, 0, 0, 4, 0:1], 0, [[XSZ, 2], [288, 4], [1, 127]])), reads=[RXd[0]], writes=[RSin])
            P.op("dve", lambda e: e.tensor_copy(
                CAP(Sin[:, 0, 1, 0, 0:1], 0, [[2048, 2], [256, 4], [32, 4], [1, 31]]),
                CAP(XS[:, 0, 1, 0, 0, 0:1], 1, [[XSZ, 2], [288, 4], [32, 4], [1, 31]])), reads=[RXd[1]], writes=[RSin])
            P.op("dve", lambda e: e.tensor_copy(
                CAP(Sin[:, 0, 1, 0, 0:1], 128, [[2048, 2], [256, 4], [1, 127]]),
                CAP(XS[:, 0, 1, 0, 4, 0:1], 1, [[XSZ, 2], [288, 4], [1, 127]])), reads=[RXd[1]], writes=[RSin])
            P.op("dve", lambda e, ct=ct: e.tensor_copy(Sin[:, :, 0, :, 128], h0b[:, :, 0, ct * 4:ct * 4 + 4]), reads=[RL8], writes=[RSin])
            P.op("dve", lambda e, ct=ct: e.tensor_copy(Sin[:, :, 1, :, 255], h0b[:, :, 1, ct * 4:ct * 4 + 4]), reads=[RL8], writes=[RSin])

        def seg_Z2(ct):
            Gb, RGb = Gbb[ct % 2], RGbb[ct % 2]
            for dr in range(2):
                for ri in range(2):
                    P.op("pool", lambda e, dr=dr, ri=ri, ct=ct: e.tensor_tensor(
                        Bpad[:, :, dr, ri],
                        CAP(Bs[ri][:, ct * 4, dr, :], 0, [[32, 4], [0, 8], [1, 16]]),
                        CAP(mask3[:], 0, [[8, 4], [1, 8], [0, 16]]), ALU.mult), reads=[RBb, RC], writes=[RBp])
            for dr in range(2):
                for jh in range(2):
                    bks = [bank(), bank()]
                    for gl in range(8):
                        gpl, g2 = gl // 2, gl % 2
                        bk, rb = bks[g2]
                        for ri in range(2):
                            P.op("pe", lambda e, bk=bk, gl=gl, gpl=gpl, g2=g2, ri=ri, dr=dr, jh=jh: e.matmul(
                                CAP(bk[:, 0:1], gl * 16, [[128, 4], [1, 16]]),
                                lhsT=Bpad[64 * g2:64 * g2 + 64, gpl, dr, ri].rearrange("p a b -> p (a b)"),
                                rhs=Gb[64 * g2:64 * g2 + 64, ri, gpl, dr, jh * 4:jh * 4 + 4, :],
                                start=(ri == 0), stop=(ri == 1), tile_position=(64 * g2, 0)),
                                reads=[RBp, RGb], writes=[rb])
                    for g2 in range(2):
                        bk, rb = bks[g2]
                        P.op("act", lambda e, bk=bk, dr=dr, jh=jh, g2=g2: e.activation(
                            CAP(Kblk[:, dr, jh * 4, 0:1], g2 * 16, [[128, 4], [32, 4], [1, 16]]),
                            CAP(bk[:, 0:1], g2 * 16, [[128, 4], [32, 4], [1, 16]]), AF.Copy), reads=[rb], writes=[RK])
            P.op("dve", lambda e, ct=ct: e.scalar_tensor_tensor(
                out=Kblk[:, 0, 0, :], in0=ident[:], scalar=s5d[:, l, ct:ct + 1], in1=Kblk[:, 0, 0, :],
                op0=ALU.mult, op1=ALU.add), reads=[RK, RC], writes=[RK])
            for dr in range(2):
                for ri in range(2):
                    for gpl in range(4):
                        P.op("pool", lambda e, dr=dr, ri=ri, gpl=gpl: e.tensor_tensor(
                            CAP(EFf[:, 0:1], gpl * 1024 + dr * 64 + ri * 32, [[128, 8], [16, 2], [1, 16]]),
                            CAP(Gb[:, ri, gpl, dr, 1, 0:1], 0, [[16, 8], [0, 2], [1, 16]]),
                            CAP(mask2[:], 0, [[0, 8], [1, 2], [0, 16]]), ALU.mult), reads=[RGb, RC], writes=[RF])
            for tb in range(NTB):
                ts = slice(tb * TB, (tb + 1) * TB)
                bk, rb = bank()
                bkv = bk[:, :].rearrange("p (c i) -> p c i", i=8)
                uv = U[:, ct, ts].rearrange("p (c i) -> p c i", i=8)
                for dr in range(2):
                    for j in range(8):
                        if dr == 0:
                            o_ap, r_ap = bkv[:, :, j:8], uv[:, :, 0:8 - j]
                        else:
                            o_ap, r_ap = bkv[:, :, 0:8 - j], uv[:, :, j:8]
                        P.op("pe", lambda e, o_ap=o_ap, r_ap=r_ap, dr=dr, j=j: e.matmul(
                            o_ap, lhsT=Kblk[:, dr, j, :], rhs=r_ap, start=(dr == 0 and j == 0), stop=False),
                            reads=[RK, RU[ct][tb]], writes=[rb])
                last = (3, 7, 1, 1)
                for gpl in range(4):
                    for j in range(8):
                        for dr in range(2):
                            for ri in range(2):
                                pos = j if dr == 0 else 7 - j
                                P.op("pe", lambda e, bk=bk, gpl=gpl, j=j, dr=dr, ri=ri, pos=pos, tb=tb: e.matmul(
                                    CAP(bk[32 * gpl:32 * gpl + 32, 0:1], pos, [[8, 64]]),
                                    lhsT=Fbuf[:, gpl, j, dr, ri].rearrange("p a b -> p (a b)"),
                                    rhs=Sin[:, ri, dr, gpl, tb * 64:(tb + 1) * 64],
                                    start=False, stop=((gpl, j, dr, ri) == last), tile_position=(0, 32 * gpl)),
                                    reads=[RF, RSin], writes=[rb])
                P.op("act", lambda e, bk=bk, ct=ct, ts=ts: e.activation(yg[:, ct, ts], bk[:, :], GELU), reads=[rb], writes=[RYG[ct][tb]])

        seg_A1(0)
        seg_A2(0)
        for ct in range(4):
            if ct < 3:
                seg_A1(ct + 1)
            seg_S(ct)
            seg_Z1(ct)
            if ct < 3:
                seg_A2(ct + 1)
            seg_Z2(ct)
        P.scope_end()
        if "yg" in dbg:
            P.scope_begin()
            for ct in range(4):
                tmpd = P.sbuf(f"dbgyg{ct}", [128, NT], F32)
                Rd = Res(f"dbgyg{ct}", dyn=True)
                P.op("dve", lambda e, ct=ct, tmpd=tmpd: e.tensor_copy(tmpd[:], yg[:, ct, :]), reads=RYG[ct], writes=[Rd])
                dump(f"yg{ct}", tmpd[:], [128, NT], [Rd])
            P.scope_end()
        wglu = P.sbuf("wglu", [128, 4, 512], BF16)
        Rwglu = Res("wglu", dyn=True)
        P.dma("pool", wglu[:], I["s5_w_glu"][l].rearrange("(k p) c -> p k c", p=128), wslot("wglu"), writes=[Rwglu])
        sgl = [P.sbuf(f"sgl{i}", [128, TB], F32) for i in range(2)]
        Rsgl = [Res(f"sgl{i}", dyn=True) for i in range(2)]
        for tb in range(NTB):
            ts = slice(tb * TB, (tb + 1) * TB)
            for co in range(4):
                bk, rb = bank()
                for kt in range(4):
                    P.op("pe", lambda e, bk=bk, co=co, kt=kt, ts=ts: e.matmul(
                        bk[:, :], lhsT=wglu[:, kt, co * 128:(co + 1) * 128], rhs=yg[:, kt, ts], start=(kt == 0), stop=(kt == 3)),
                        reads=[Rwglu, RYG[kt][tb]], writes=[rb])
                s = co % 2
                P.op("act", lambda e, bk=bk, s=s: e.activation(sgl[s][:], bk[:, :], AF.Sigmoid), reads=[rb], writes=[Rsgl[s]])
                P.op("dve", lambda e, s=s, co=co, ts=ts: e.tensor_tensor(y_a[:, co, ts], yg[:, co, ts], sgl[s][:], ALU.mult),
                     reads=[Rsgl[s], RYG[co][tb]], writes=[RYA[co][tb]])
        P.scope_end()
        ck("m_s5", l)

        y_b = P.sbuf("y_b", [128, 2, NT], BF16)
        RYB = [[Res(f"yb{c}_{t}", dyn=True) for t in range(NTB)] for c in range(2)]
        y_c = P.sbuf("y_c", [128, 2, NT], BF16)
        RYC = [[Res(f"yc{c}_{t}", dyn=True) for t in range(NTB)] for c in range(2)]
        hT, RH = alloc_hT()
        norm_stage(l, 1, hT, RH, own_scope=True)
        P.scope_begin()
        gpad = P.sbuf("gpad", [128, 2, GP_LEN], BF16)
        Rgp = [[Res(f"gp{j}_{t}", dyn=True) for t in range(NTB)] for j in range(2)]
        u_c = P.sbuf("u_c", [128, 2, NT], BF16)
        RUC = [[Res(f"uc{j}_{t}", dyn=True) for t in range(NTB)] for j in range(2)]
        vnb = P.sbuf("vnb", [128, 16, 256], BF16)
        RV = [Res(f"vn{t}", dyn=True) for t in range(16)]
        P.op("pool", lambda e: e.memset(gpad[:], 0.0), writes=[Rgp[j][t] for j in range(2) for t in range(NTB)])
        if l == 0:
            MODG["gen"] = mod_gen(1, *mod_bufs())
        P.scope_begin()
        winb = P.sbuf("winb", [128, 8, 1024], BF16)
        Rwinb = Res("winb", dyn=True)
        P.dma("pool", winb[:, :, 0:512], I["w_in"][l, :, 512:1024].rearrange("(k p) c -> p k c", p=128), wslot("winb"), writes=[Rwinb])
        P.dma("pool", winb[:, :, 512:1024], I["w_in"][l, :, 1024:1536].rearrange("(k p) c -> p k c", p=128), wslot("winb"), writes=[Rwinb])
        sgln = P.sbuf("sgln", [128, 2, 256], F32)
        Rsgl_ = Res("sgln", dyn=True)
        P.dma("sp", sgln[:].rearrange("p a b -> p (a b)"), I["sg_ln"][l].rearrange("a b -> (a b)").partition_broadcast(128), wslot(f"sgln{l}"), writes=[Rsgl_])
        sb = [P.sbuf(f"sb{i}", [128, TB], F32) for i in range(2)]
        Rsb = [Res(f"sb{i}", dyn=True) for i in range(2)]

        def gp_ap(j, tb, k):
            if tb < 2:
                return CAP(gpad[:, j, 0:1], GP_OFF[2 * tb] + k, [[286, 2], [1, 256]])
            return CAP(gpad[:, j, 0:1], GP_OFF[4] + (tb - 2) * 512 + k, [[1, 512]])

        for j in range(2):
            for tb in range(NTB):
                ts = slice(tb * TB, (tb + 1) * TB)
                pa, ra = bank()
                pb2, rb2 = bank()
                for which, (pp, rr) in enumerate(((pa, ra), (pb2, rb2))):
                    c0 = which * 256 + j * 128
                    for kt in range(8):
                        P.op("pe", lambda e, pp=pp, c0=c0, kt=kt, ts=ts: e.matmul(
                            pp[:, :], lhsT=winb[:, kt, c0:c0 + 128], rhs=hT[:, kt, ts], start=(kt == 0), stop=(kt == 7)),
                            reads=[Rwinb, RH[kt][tb]], writes=[rr])
                mod_step(2)
                s = tb % 2
                P.op("act", lambda e, pb2=pb2, s=s: e.activation(sb[s][:], pb2[:, :], AF.Sigmoid), reads=[rb2], writes=[Rsb[s]])
                o_ap = gp_ap(j, tb, 15)
                i_ap = pa[:, :].rearrange("p (s t) -> p s t", s=2) if tb < 2 else pa[:, :]
                s_ap = sb[s][:].rearrange("p (s t) -> p s t", s=2) if tb < 2 else sb[s][:]
                P.op("dve", lambda e, o_ap=o_ap, i_ap=i_ap, s_ap=s_ap: e.tensor_tensor(o_ap, i_ap, s_ap, ALU.mult),
                     reads=[ra, Rsb[s]], writes=[Rgp[j][tb]])
        for j in range(2):
            for tb in range(NTB):
                ts = slice(tb * TB, (tb + 1) * TB)
                bk, rb = bank()
                c0 = 512 + j * 128
                for kt in range(8):
                    P.op("pe", lambda e, bk=bk, c0=c0, kt=kt, ts=ts: e.matmul(
                        bk[:, :], lhsT=winb[:, kt, c0:c0 + 128], rhs=hT[:, kt, ts], start=(kt == 0), stop=(kt == 7)),
                        reads=[Rwinb, RH[kt][tb]], writes=[rb])
                mod_step(2)
                P.op("act", lambda e, bk=bk, j=j, ts=ts: e.activation(u_c[:, j, ts], bk[:, :], GELU), reads=[rb], writes=[RUC[j][tb]])
        vg = [P.sbuf(f"vg{i}", [128, 256], F32) for i in range(2)]
        Rvg = [Res(f"vg{i}", dyn=True) for i in range(2)]
        bst = [P.sbuf(f"bst{i}", [128, 6], F32) for i in range(2)]
        bag = [P.sbuf(f"bag{i}", [128, 2], F32) for i in range(2)]
        for tt in range(16):
            tb = tt // 4
            tks = slice(tt * 128, (tt + 1) * 128)
            bk, rb = bank()
            for kt in range(8):
                P.op("pe", lambda e, bk=bk, kt=kt, tks=tks: e.matmul(
                    bk[:, 0:256], lhsT=hT[:, kt, tks], rhs=winb[:, kt, 768:1024], start=(kt == 0), stop=(kt == 7)),
                    reads=[Rwinb, RH[kt][tb]], writes=[rb])
            mod_step(2)
            s = tt % 2
            P.op("act", lambda e, bk=bk, s=s: e.activation(vg[s][:], bk[:, 0:256], GELU), reads=[rb], writes=[Rvg[s]])
            P.op("dve", lambda e, s=s: e.bn_stats(bst[s][:], vg[s][:]), reads=[Rvg[s]], writes=[Rvg[s]])
            P.op("dve", lambda e, s=s: e.bn_aggr(bag[s][:], bst[s][:]), reads=[Rvg[s]], writes=[Rvg[s]])
            P.op("act", lambda e, s=s: e.activation(bag[s][:, 1:2], bag[s][:, 1:2], AF.Sqrt, bias=eps_t[:, 0:1], scale=1.0),
                 reads=[Rvg[s], RC], writes=[Rvg[s]])
            P.op("dve", lambda e, s=s: e.reciprocal(bag[s][:, 1:2], bag[s][:, 1:2]), reads=[Rvg[s]], writes=[Rvg[s]])
            P.op("dve", lambda e, s=s: e.tensor_scalar(vg[s][:], vg[s][:], bag[s][:, 0:1], bag[s][:, 1:2], ALU.subtract, ALU.mult),
                 reads=[Rvg[s]], writes=[Rvg[s]])
            P.op("dve", lambda e, s=s: e.tensor_tensor(vg[s][:], vg[s][:], sgln[:, 0, :], ALU.mult), reads=[Rvg[s], Rsgl_], writes=[Rvg[s]])
            P.op("dve", lambda e, s=s, tt=tt: e.tensor_tensor(vnb[:, tt, :], vg[s][:], sgln[:, 1, :], ALU.add),
                 reads=[Rvg[s], Rsgl_], writes=[RV[tt]])
        P.scope_end()
        ck("m_p1", l)
        if "gpad" in dbg:
            P.scope_begin()
            for j in range(2):
                tmpd = P.sbuf(f"dbggp{j}", [128, GP_LEN], F32)
                Rd = Res(f"dbggp{j}", dyn=True)
                P.op("dve", lambda e, j=j, tmpd=tmpd: e.tensor_copy(tmpd[:], gpad[:, j, :]), reads=Rgp[j], writes=[Rd])
                dump(f"gp{j}", tmpd[:], [128, GP_LEN], [Rd])
            P.scope_end()
        P.scope_begin()
        dg = P.sbuf("dg", [128, 2, 31, 128], BF16)
        Rdg = Res("dg", dyn=True)
        for j in range(2):
            for k in range(31):
                P.op("pool", lambda e, j=j, k=k: e.tensor_scalar_mul(dg[:, j, k, :], ident[:], convw[:, l, j, k:k + 1]),
                     reads=[RC], writes=[Rdg])
        ycf = P.sbuf("ycf", [128, 2, TB], F32)
        ysq = P.sbuf("ysq", [128, 2, TB], F32)
        Rycf = [Res(f"ycf{j}", dyn=True) for j in range(2)]
        Rysq = [Res(f"ysq{j}", dyn=True) for j in range(2)]
        mean_s = P.sbuf("mean_s", [128, TB], F32)
        var_s = P.sbuf("var_s", [128, TB], F32)
        Rms = Res("mean_s", dyn=True)
        Rvs = Res("var_s", dyn=True)
        dtm = [P.sbuf(f"dtm{i}", [128, TB], F32) for i in range(2)]
        Rdtm = [Res(f"dtm{i}", dyn=True) for i in range(2)]
        for tb in range(NTB):
            ts = slice(tb * TB, (tb + 1) * TB)
            for j in range(2):
                bk, rb = bank()
                o_ap = bk[:, :].rearrange("p (s t) -> p s t", s=2) if tb < 2 else bk[:, :]
                for k in range(31):
                    P.op("pe", lambda e, o_ap=o_ap, j=j, k=k, tb=tb: e.matmul(
                        o_ap, lhsT=dg[:, j, k, :], rhs=gp_ap(j, tb, k), start=(k == 0), stop=(k == 30)),
                        reads=[Rdg, Rgp[j][tb]] + ([Rgp[j][tb - 1]] if tb == 3 else []) + ([Rgp[j][tb + 1]] if tb == 2 else []),
                        writes=[rb])
                mod_step()
                P.op("act", lambda e, bk=bk, j=j: e.activation(ycf[:, j, :], bk[:, :], AF.Identity, bias=convv[:, l, 0, j:j + 1], scale=1.0),
                     reads=[rb, RC], writes=[Rycf[j]])
                P.op("act", lambda e, bk=bk, j=j: e.activation(ysq[:, j, :], bk[:, :], AF.Square, bias=convv[:, l, 0, j:j + 1], scale=1.0),
                     reads=[rb, RC], writes=[Rysq[j]])
            bm, rbm = bank()
            bq, rbq = bank()
            for j in range(2):
                P.op("pe", lambda e, bm=bm, j=j: e.matmul(bm[:, :], lhsT=ones256[:], rhs=ycf[:, j, :], start=(j == 0), stop=(j == 1)),
                     reads=[Rycf[j], RC], writes=[rbm])
            for j in range(2):
                P.op("pe", lambda e, bq=bq, j=j: e.matmul(bq[:, :], lhsT=ones256[:], rhs=ysq[:, j, :], start=(j == 0), stop=(j == 1)),
                     reads=[Rysq[j], RC], writes=[rbq])
            P.op("act", lambda e, bm=bm: e.activation(mean_s[:], bm[:, :], AF.Copy), reads=[rbm], writes=[Rms])
            P.op("dve", lambda e: e.tensor_tensor(var_s[:], mean_s[:], mean_s[:], ALU.mult), reads=[Rms], writes=[Rvs])
            P.op("dve", lambda e, bq=bq: e.tensor_tensor(var_s[:], bq[:, :], var_s[:], ALU.subtract), reads=[rbq, Rvs], writes=[Rvs])
            P.op("act", lambda e: e.activation(var_s[:], var_s[:], AF.Sqrt, bias=eps_t[:, 0:1], scale=1.0), reads=[Rvs, RC], writes=[Rvs])
            P.op("dve", lambda e: e.reciprocal(var_s[:], var_s[:]), reads=[Rvs], writes=[Rvs])
            for j in range(2):
                P.op("dve", lambda e, j=j: e.tensor_tensor(dtm[j][:], ycf[:, j, :], mean_s[:], ALU.subtract), reads=[Rycf[j], Rms], writes=[Rdtm[j]])
                P.op("dve", lambda e, j=j: e.tensor_tensor(dtm[j][:], dtm[j][:], var_s[:], ALU.mult), reads=[Rdtm[j], Rvs], writes=[Rdtm[j]])
                P.op("act", lambda e, j=j, ts=ts: e.activation(y_b[:, j, ts], dtm[j][:], AF.Silu, bias=convv[:, l, 2, j:j + 1],
                                                           scale=convv[:, l, 1, j:j + 1]), reads=[Rdtm[j], RC], writes=[RYB[j][tb]])
        P.scope_end()
        ck("m_p2", l)
        P.scope_begin()
        sgw = P.sbuf("sgw", [128, 4, 128], BF16)
        sgb = P.sbuf("sgb", [128, 2, 128], F32)
        Rsgp = Res("sgpar", dyn=True)
        P.dma("pool", sgw[:], I["sg_wT"][:, l], wslot("sgw"), writes=[Rsgp])
        P.dma("sp", sgb[:], I["sg_bT"][:, l], wslot(f"sgb{l}"), writes=[Rsgp])
        stm = [P.sbuf(f"stm{i}", [128, TB], F32) for i in range(2)]
        Rstm = [Res(f"stm{i}", dyn=True) for i in range(2)]
        for hp in range(2):
            for tb in range(NTB):
                ts = slice(tb * TB, (tb + 1) * TB)
                bk, rb = bank()
                for n4 in range(4):
                    tt = tb * 4 + n4
                    for h2 in range(2):
                        h = hp * 2 + h2
                        P.op("pe", lambda e, bk=bk, n4=n4, tt=tt, h2=h2, h=h: e.matmul(
                            bk[64 * h2:64 * h2 + 64, n4 * 128:(n4 + 1) * 128], lhsT=vnb[:, tt, h * 64:(h + 1) * 64], rhs=sgw[:, h, :],
                            start=True, stop=True, tile_position=(0, 64 * h2)), reads=[RV[tt], Rsgp], writes=[rb])
                s = tb % 2
                P.op("dve", lambda e, bk=bk, s=s, hp=hp: e.tensor_tensor(
                    stm[s][:].rearrange("p (n q) -> p n q", n=4), bk[:, :].rearrange("p (n q) -> p n q", n=4),
                    CAP(sgb[:, hp, :], 0, [[0, 4], [1, 128]]), ALU.add), reads=[rb, Rsgp], writes=[Rstm[s]])
                P.op("dve", lambda e, s=s, hp=hp, ts=ts: e.tensor_tensor(y_c[:, hp, ts], stm[s][:], u_c[:, hp, ts], ALU.mult),
                     reads=[Rstm[s], RUC[hp][tb]], writes=[RYC[hp][tb]])
        P.scope_end()
        while MODG["gen"] is not None:
            mod_step()
        P.scope_end()
        ck("m_p3", l)

        if "ybc" in dbg:
            P.scope_begin()
            for nm, buf, RR, n in (("ya", y_a, RYA, 4), ("yb", y_b, RYB, 2), ("yc", y_c, RYC, 2)):
                for ct in range(n):
                    tmpd = P.sbuf(f"dbg{nm}{ct}", [128, NT], F32)
                    Rd = Res(f"dbg{nm}{ct}", dyn=True)
                    P.op("dve", lambda e, ct=ct, tmpd=tmpd, buf=buf: e.tensor_copy(tmpd[:], buf[:, ct, :]), reads=RR[ct], writes=[Rd])
                    dump(f"{nm}{ct}", tmpd[:], [128, NT], [Rd])
            P.scope_end()

        P.scope_begin()
        mg = P.sbuf("mg", [128, 8, NT], BF16)
        RM = [[Res(f"mg{c}_{t}", dyn=True) for t in range(NTB)] for c in range(8)]
        wg = [P.sbuf(f"wg{i}", [128, 3, 8, 128], BF16) for i in range(2)]
        Rwg = [Res(f"wg{i}", dyn=True) for i in range(2)]
        wbr = [P.sbuf(f"wbr{i}", [128, 8, 128], BF16) for i in range(2)]
        Rwbr = [Res(f"wbr{i}", dyn=True) for i in range(2)]
        wo = [P.sbuf(f"wo{i}", [128, 8, 128], BF16) for i in range(2)]
        Rwo = [Res(f"wo{i}", dyn=True) for i in range(2)]
        gs = [P.sbuf(f"gs{i}", [128, TB], F32) for i in range(3)]
        Rgs = [Res(f"gs{i}", dyn=True) for i in range(3)]
        mt = [P.sbuf(f"mt{i}", [128, TB], F32) for i in range(3)]
        Rmt = [Res(f"mt{i}", dyn=True) for i in range(3)]
        ybufs = [(y_a, RYA, 4, 0), (y_b, RYB, 2, 4), (y_c, RYC, 2, 6)]
        for d in range(8):
            b = d % 2
            P.dma("pool", wg[b][:], I["wgate_t"][l, d], wslot(f"wg{b}"), writes=[Rwg[b]])
            P.dma("pool", wbr[b][:], I["wbr_t"][l, d], wslot(f"wbr{b}"), writes=[Rwbr[b]])
            for tb in range(NTB):
                ts = slice(tb * TB, (tb + 1) * TB)
                for br in range(3):
                    bg_, rg_ = bank()
                    for kt in range(8):
                        P.op("pe", lambda e, bg_=bg_, b=b, br=br, kt=kt, ts=ts: e.matmul(
                            bg_[:, :], lhsT=wg[b][:, br, kt, :], rhs=hT[:, kt, ts], start=(kt == 0), stop=(kt == 7)),
                            reads=[Rwg[b], RH[kt][tb]], writes=[rg_])
                    P.op("act", lambda e, bg_=bg_, br=br, d=d: e.activation(
                        gs[br][:], bg_[:, :], AF.Sigmoid, bias=bgate[:, l, br * 8 + d:br * 8 + d + 1], scale=1.0),
                        reads=[rg_, RC], writes=[Rgs[br]])
                    ybuf, RY, nk, k0 = ybufs[br]
                    bp_, rp_ = bank()
                    for kt in range(nk):
                        P.op("pe", lambda e, bp_=bp_, b=b, kt=kt, k0=k0, ybuf=ybuf, nk=nk, ts=ts: e.matmul(
                            bp_[:, :], lhsT=wbr[b][:, k0 + kt, :], rhs=ybuf[:, kt, ts], start=(kt == 0), stop=(kt == nk - 1)),
                            reads=[Rwbr[b], RY[kt][tb]], writes=[rp_])
                    P.op("dve", lambda e, bp_=bp_, br=br: e.tensor_tensor(mt[br][:], bp_[:, :], gs[br][:], ALU.mult),
                         reads=[rp_, Rgs[br]], writes=[Rmt[br]])
                P.op("pool", lambda e: e.tensor_tensor(mt[0][:], mt[0][:], mt[1][:], ALU.add), reads=[Rmt[1]], writes=[Rmt[0]])
                P.op("pool", lambda e, d=d, ts=ts: e.tensor_tensor(mg[:, d, ts], mt[0][:], mt[2][:], ALU.add),
                     reads=[Rmt[0], Rmt[2]], writes=[RM[d][tb]])
        for d in range(8):
            b = d % 2
            P.dma("pool", wo[b][:], I["wout_t"][l, d], wslot(f"wo{b}"), writes=[Rwo[b]])
            for tb in range(NTB):
                c = 0 if tb < 2 else 1
                ts = slice(tb * TB, (tb + 1) * TB)
                po, ro = bank()
                for kt in range(8):
                    P.op("pe", lambda e, po=po, b=b, kt=kt, ts=ts: e.matmul(
                        po[:, :], lhsT=wo[b][:, kt, :], rhs=mg[:, kt, ts], start=(kt == 0), stop=(kt == 7)),
                        reads=[Rwo[b], RM[kt][tb]], writes=[ro])
                P.op("dve", lambda e, po=po, d=d, ts=ts, c=c: e.scalar_tensor_tensor(
                    out=xT[:, d, ts], in0=po[:, :], scalar=gtv[:, l, 1, d, c:c + 1], in1=xT[:, d, ts], op0=ALU.mult, op1=ALU.add),
                    reads=[ro, Rmodl[l]], writes=[RX[d][tb]])
        P.scope_end()
        P.scope_end()

    def final_stage():
        P.scope_begin()
        sq, Rsq, rs, Rrs, tmp, Rtmp = norm_scratch()
        ot = [P.sbuf(f"ot{i}", [128, TB], F32) for i in range(4)]
        Rot = [Res(f"ot{i}", dyn=True) for i in range(4)]
        for tb in range(NTB):
            ts = slice(tb * TB, (tb + 1) * TB)
            for ct in range(8):
                P.op("act", lambda e, ct=ct, ts=ts: e.activation(sq[:, ct, :], xT[:, ct, ts], AF.Square), reads=[RX[ct][tb]], writes=[Rsq[ct]])
            bk, rb = bank()
            for ct in range(8):
                P.op("pe", lambda e, bk=bk, ct=ct: e.matmul(bk[:, :], lhsT=ones_bf[:], rhs=sq[:, ct, :], start=(ct == 0), stop=(ct == 7)),
                     reads=[Rsq[ct], RC], writes=[rb])
            r = tb % 2
            P.op("act", lambda e, bk=bk, r=r: e.activation(rs[r][:], bk[:, :], AF.Sqrt, bias=eps_t[:, 0:1], scale=1.0), reads=[rb, RC], writes=[Rrs[r]])
            P.op("dve", lambda e, r=r: e.reciprocal(rs[r][:], rs[r][:]), reads=[Rrs[r]], writes=[Rrs[r]])
            for ct in range(8):
                t = ct % 4
                P.op("dve", lambda e, ct=ct, ts=ts, t=t, r=r: e.scalar_tensor_tensor(
                    out=ot[t][:], in0=xT[:, ct, ts], scalar=finalg[:, ct:ct + 1], in1=rs[r][:], op0=ALU.mult, op1=ALU.mult),
                    reads=[RX[ct][tb], RC, Rrs[r]], writes=[Rot[t]])
                P.dma("sp", yT[ct * 128:(ct + 1) * 128, ts], ot[t][:], s_out, reads=[Rot[t]])
        P.dma("sp", nsd, nsbuf[:].rearrange("p a b c d e -> p (a b c d e)"), s_out, reads=[Rns])
        P.scope_end()

    def dump_x(tag):
        for ct in range(8):
            dump(f"{tag}_{ct}", xT[:, ct, :], [128, NT], RX[ct])

    done = False
    mod_stage(0)
    for l in range(2):
        if stop == "mod":
            break
        ffn_stage(l, 0, 0)
        if f"x1_{l}" in dbg:
            dump_x(f"x1_{l}")
        if stop == f"x1_{l}":
            done = True
            break
        if mixer_stage(l):
            done = True
            break
        if f"x2_{l}" in dbg:
            dump_x(f"x2_{l}")
        if stop == f"x2_{l}":
            done = True
            break
        ffn_stage(l, 1, 2)
        if f"x3_{l}" in dbg:
            dump_x(f"x3_{l}")
        if stop == f"x3_{l}":
            done = True
            break
    final_stage()
    P.emit(final_waits=[s_out])
    P.close()
    return nc, list(dbg_out)


def _grid_pos_embed_T():
    rows = 1024 // 64
    rr, cc = np.meshgrid(np.arange(rows, dtype=np.float32), np.arange(64, dtype=np.float32), indexing="ij")
    quarter = 256
    omega = (1.0 / (10000.0 ** (np.arange(quarter, dtype=np.float32) / np.float32(quarter)))).astype(np.float32)

    def emb(p):
        ang = p.reshape(-1)[:, None].astype(np.float32) * omega[None, :]
        return np.concatenate([np.sin(ang), np.cos(ang)], axis=-1)

    pe = np.concatenate([emb(rr), emb(cc)], axis=-1).astype(np.float32)
    return np.ascontiguousarray(pe.T)


def _prep_shared(inp):
    f = lambda a: np.ascontiguousarray(np.asarray(a, dtype=np.float32))
    S = {}
    S["pos"] = _grid_pos_embed_T()
    S["w_mod"] = f(inp["w_mod"])
    S["b_modT"] = f(inp["b_mod"].reshape(2, 72, 128).transpose(2, 0, 1))
    S["norm_gT"] = f(inp["norm_g"].reshape(2, 3, 8, 128).transpose(3, 0, 1, 2))
    S["final_gT"] = f(inp["final_g"].reshape(8, 128).T)
    w1 = inp["ffn_w1"].reshape(2, 2, 8, 128, 2, 22, 128)
    S["w1t"] = f(w1.transpose(0, 1, 5, 4, 3, 2, 6))
    S["ffn_w2"] = f(inp["ffn_w2"])
    S["w_in"] = f(inp["w_in"])
    wg = inp["w_gate"].reshape(2, 8, 128, 3, 8, 128)
    S["wgate_t"] = f(wg.transpose(0, 4, 2, 3, 1, 5))
    S["b_gateT"] = f(inp["b_gate"].reshape(2, 24, 128).transpose(2, 0, 1))
    wbr = np.concatenate([inp["w_br_a"], inp["w_br_b"], inp["w_br_c"]], axis=1)
    S["wbr_t"] = f(wbr.reshape(2, 8, 128, 8, 128).transpose(0, 3, 2, 1, 4))
    S["wout_t"] = f(inp["w_out"].reshape(2, 8, 128, 8, 128).transpose(0, 3, 2, 1, 4))
    a = np.stack([inp["s5_a_re"], inp["s5_a_im"]], axis=0)
    a6 = a.reshape(2, 2, 2, 16, 2, 64)
    S["a_sl"] = f(a6.transpose(4, 5, 1, 0, 3, 2).reshape(128, 2, 2, 16, 2))
    ld = inp["s5_log_dt"].reshape(2, 2, 16, 2)
    S["ldt_sl"] = f(np.broadcast_to(ld.transpose(3, 0, 2, 1)[:, None], (2, 64, 2, 16, 2)).reshape(128, 2, 16, 2))
    a7 = a.reshape(2, 2, 2, 4, 8, 64)
    S["a_cl"] = f(np.broadcast_to(a7.transpose(4, 1, 0, 3, 2, 5)[:, None], (8, 16, 2, 2, 4, 2, 64)).reshape(128, 2, 2, 4, 2, 64))
    ld2 = inp["s5_log_dt"].reshape(2, 2, 4, 8)
    S["ldt_cl"] = f(np.broadcast_to(ld2.transpose(3, 0, 2, 1)[:, None, :, :, :, None], (8, 16, 2, 4, 2, 64)).reshape(128, 2, 4, 2, 64))
    b = np.stack([inp["s5_b_re"], inp["s5_b_im"]], axis=0)
    b7 = b.reshape(2, 2, 2, 4, 8, 64, 16)
    S["b_cl"] = f(b7.transpose(4, 6, 1, 0, 3, 2, 5).reshape(128, 2, 2, 4, 2, 64))
    b8 = b.reshape(2, 2, 2, 16, 2, 64, 16)
    S["b_sl"] = f(b8.transpose(4, 5, 1, 0, 3, 2, 6).reshape(128, 2, 2, 16, 2, 16))
    c = np.stack([inp["s5_c_re"], inp["s5_c_im"]], axis=0)
    c8 = c.reshape(2, 2, 2, 16, 2, 16, 64)
    S["c_sl"] = f(c8.transpose(4, 6, 1, 0, 3, 2, 5).reshape(128, 2, 2, 16, 2, 16))
    S["s5_dT"] = f(inp["s5_d"].reshape(2, 4, 128).transpose(2, 0, 1))
    S["s5_w_glu"] = f(inp["s5_w_glu"])
    S["conv_wT"] = f(inp["conv_w"].reshape(2, 31, 2, 128).transpose(3, 0, 2, 1))
    cv = np.stack([inp["conv_b"], inp["conv_ln_g"], inp["conv_ln_b"]], axis=1)
    S["conv_vT"] = f(cv.reshape(2, 3, 2, 128).transpose(3, 0, 1, 2))
    S["sg_ln"] = f(np.stack([inp["sg_ln_g"], inp["sg_ln_b"]], axis=1))
    S["sg_wT"] = f(inp["sg_w"].transpose(3, 0, 1, 2))
    sgb = inp["sg_b"].reshape(2, 2, 2, 128)
    S["sg_bT"] = f(np.broadcast_to(sgb.transpose(2, 0, 1, 3)[:, None], (2, 64, 2, 2, 128)).reshape(128, 2, 2, 128))
    S["ident"] = np.eye(128, dtype=np.float32)
    p = np.arange(128)
    S["maskE"] = f(((p[:, None] // 16) % 2 == np.arange(2)[None, :]))
    S["mask2"] = f(((p[:, None] // 64) == np.arange(2)[None, :]))
    S["mask3"] = f((np.arange(8)[None, None, :] == (2 * np.arange(4)[None, :, None] + (p // 64)[:, None, None])))
    return S


def _prep_core(inp, core):
    f = lambda a: np.ascontiguousarray(np.asarray(a, dtype=np.float32))
    C = {}
    xp = inp["x_prompt"][4 * core:4 * core + 4].reshape(1024, 1024)
    xs = inp["x_sample"][core]
    C["xin"] = f(np.concatenate([xp, xs], axis=0).T)
    C["cond"] = f(np.stack([inp["c_ctx"], inp["c"][core]], axis=1))
    h0 = inp["state_ssm"][core].reshape(2, 2, 16, 2, 64, 2)
    C["h0"] = f(h0.transpose(3, 4, 0, 2, 1, 5).reshape(128, 2, 16, 2, 2))
    return C


_NC_CACHE = {}


def kernel(**inputs):
    inp = {k: np.asarray(v) for k, v in inputs.items()}
    if "nc" not in _NC_CACHE:
        _NC_CACHE["nc"] = build_program()[0]
    nc = _NC_CACHE["nc"]
    S = _prep_shared(inp)
    in_maps = []
    for core in range(8):
        m = dict(S)
        m.update(_prep_core(inp, core))
        in_maps.append(m)
    res = run_bass_kernel_spmd(nc, in_maps, core_ids=list(range(8)))
    y_prompt = np.empty((32, 256, 1024), np.float32)
    y_sample = np.empty((8, 1024, 1024), np.float32)
    new_state = np.empty((32, 2, 2, 32, 64, 2), np.float32)
    for core in range(8):
        r = res.results[core]
        y = np.asarray(r["yT"]).T
        y_prompt[4 * core:4 * core + 4] = y[:1024].reshape(4, 256, 1024)
        y_sample[core] = y[1024:]
        ns = np.asarray(r["ns"]).reshape(2, 64, 2, 4, 16, 2, 2)
        new_state[4 * core:4 * core + 4] = ns.transpose(3, 2, 5, 4, 0, 1, 6).reshape(4, 2, 2, 32, 64, 2)
    return (y_prompt, y_sample, new_state)
```
